# Optimizing a Trainium2 kernel written in Bass

```python
import math
import jax, jax.numpy as jnp
from jax import lax
import numpy as np

D_MODEL = 1024
BATCH = 8
SEQ = 8192
DEPTH = 4
DEC_BATCH = 16
DEC_SEQ = 64
PAST_LEN = 2048

CHUNK = 64
N_MIXERS = 2
N_HGRN_LAYERS = (DEPTH + N_MIXERS - 1) // N_MIXERS
N_GLA_LAYERS = DEPTH // N_MIXERS
HGRN_HEAD_DIM = 128
HGRN_HEADS = D_MODEL // HGRN_HEAD_DIM
HGRN_WIDTH = HGRN_HEADS * HGRN_HEAD_DIM
GLA_HEADS = 4
GLA_KEY_DIM = D_MODEL // (2 * GLA_HEADS)
GLA_VAL_DIM = D_MODEL // GLA_HEADS
GLA_GATE_RANK = 16
GLA_TAU = 16.0
MEM_TOKENS = 256
MEM_HEADS = 4
MEM_HEAD_DIM = 128
MEM_WIDTH = MEM_HEADS * MEM_HEAD_DIM
D_FF = 2816
ALPHA = (2.0 * DEPTH) ** 0.25
BETA = (8.0 * DEPTH) ** -0.25
LN_EPS = 1e-5
RMS_EPS = 1e-6
GATE_CLAMP = 1.0 - 1e-6
HGRN_SPLITS = (HGRN_WIDTH, 2 * HGRN_WIDTH, 3 * HGRN_WIDTH, 4 * HGRN_WIDTH)
HGRN_IN = 4 * HGRN_WIDTH + MEM_WIDTH
HGRN_MIX_WIDTH = HGRN_WIDTH + MEM_WIDTH
GLA_QK = GLA_HEADS * GLA_KEY_DIM
GLA_V = GLA_HEADS * GLA_VAL_DIM
GLA_SPLITS = (GLA_QK, 2 * GLA_QK, 2 * GLA_QK + GLA_V, 2 * GLA_QK + 2 * GLA_V, 2 * GLA_QK + 2 * GLA_V + GLA_GATE_RANK)
GLA_IN = 2 * GLA_QK + 2 * GLA_V + GLA_GATE_RANK + MEM_WIDTH
GLA_MIX_WIDTH = GLA_V + MEM_WIDTH

kernel_name = 'hybrid_hgrn2_gla_streaming_step'


def layer_norm(x, gain, bias):
    xf = x.astype(jnp.float32)
    mu = jnp.mean(xf, axis=-1, keepdims=True)
    var = jnp.mean(jnp.square(xf - mu), axis=-1, keepdims=True)
    return ((xf - mu) * lax.rsqrt(var + LN_EPS) * gain + bias).astype(x.dtype)


def swiglu(x, w_gate, w_up, w_down):
    return (jax.nn.silu(x @ w_gate) * (x @ w_up)) @ w_down


def gated_linear_attention(q, k, v, log_a, s0):
    B, L, H, K = q.shape
    V = v.shape[-1]
    C = min(CHUNK, L)
    N = L // C
    f32 = jnp.float32

    def chunks(t):
        return t.astype(f32).reshape(B, N, C, H, t.shape[-1]).swapaxes(0, 1)

    causal = jnp.tril(jnp.ones((C, C), dtype=bool))[None, :, :, None, None]

    def step(s, blk):
        qc, kc, vc, gc = blk
        b = jnp.cumsum(gc, axis=1)
        diff = b[:, :, None] - b[:, None]
        decay = jnp.where(causal, jnp.exp(jnp.minimum(diff, 0.0)), 0.0)
        scores = jnp.einsum('bthk,btshk,bshk->bhts', qc, decay, kc)
        o = (jnp.einsum('bhts,bshv->bthv', scores, vc)
             + jnp.einsum('bthk,bhkv->bthv', qc * jnp.exp(b), s))
        b_last = b[:, -1]
        s = (jnp.exp(b_last)[..., None] * s
             + jnp.einsum('bshk,bshv->bhkv', kc * jnp.exp(b_last[:, None] - b), vc))
        return s, o

    s_final, o = lax.scan(step, s0.astype(f32), (chunks(q), chunks(k), chunks(v), chunks(log_a)))
    return o.swapaxes(0, 1).reshape(B, L, H, V), s_final.astype(s0.dtype)


def head_rmsnorm_gate(o, gain, gate, dtype):
    B, L, H, V = o.shape
    o = o * lax.rsqrt(jnp.mean(o * o, axis=-1, keepdims=True) + RMS_EPS)
    return (o.reshape(B, L, H * V) * gain * jax.nn.silu(gate.astype(jnp.float32))).astype(dtype)


def hgrn2_mixer(x, w_in, lower_bound, norm_gain, s0):
    B, L, _ = x.shape
    q, f, i, g, xq = jnp.split(x @ w_in, HGRN_SPLITS, axis=-1)
    heads = lambda t: t.reshape(B, L, HGRN_HEADS, HGRN_HEAD_DIM)
    f32 = f.astype(jnp.float32)
    k = (1.0 - lower_bound) * jax.nn.sigmoid(-f32)
    log_f = jnp.log1p(-jnp.minimum(k, GATE_CLAMP))
    q = jax.nn.silu(q) * HGRN_HEAD_DIM ** -0.5
    o, s = gated_linear_attention(heads(q), heads(k), heads(i), heads(log_f), s0)
    return head_rmsnorm_gate(o, norm_gain, g, x.dtype), xq, s


def gla_mixer(x, w_in, w_gate2, b_gate, norm_gain, s0):
    B, L, _ = x.shape
    q, k, v, r, ga, xq = jnp.split(x @ w_in, GLA_SPLITS, axis=-1)
    heads_k = lambda t: t.reshape(B, L, GLA_HEADS, GLA_KEY_DIM)
    log_a = jax.nn.log_sigmoid((ga @ w_gate2 + b_gate).astype(jnp.float32)) / GLA_TAU
    o, s = gated_linear_attention(heads_k(q) * GLA_KEY_DIM ** -0.5, heads_k(k),
                                  v.reshape(B, L, GLA_HEADS, GLA_VAL_DIM), heads_k(log_a), s0)
    return head_rmsnorm_gate(o, norm_gain, r, x.dtype), xq, s


def memory_attention(xq, mem_k, mem_v):
    B, L, _ = xq.shape
    q = xq.reshape(B, L, MEM_HEADS, MEM_HEAD_DIM)
    s = jnp.einsum('blhd,bmhd->bhlm', q, mem_k).astype(jnp.float32) * MEM_HEAD_DIM ** -0.5
    p = jax.nn.softmax(s, axis=-1).astype(mem_v.dtype)
    return jnp.einsum('bhlm,bmhd->blhd', p, mem_v).reshape(B, L, MEM_WIDTH)


def run_trunk(x, mem_k, mem_v, s_hgrn, s_gla, ffn_w_gate, ffn_w_up, ffn_w_down, ln_gain, ln_bias,
              hgrn_w_in, hgrn_lb_logits, hgrn_norm, hgrn_w_out,
              gla_w_in, gla_w_gate2, gla_b_gate, gla_norm, gla_w_out):
    sm = jax.nn.softmax(hgrn_lb_logits.astype(jnp.float32), axis=0)
    lower_bounds = jnp.cumsum(sm, axis=0) - sm[0]
    new_h, new_g = [], []
    for layer in range(DEPTH):
        x = layer_norm(ALPHA * x + 0.5 * swiglu(x, ffn_w_gate[layer, 0], ffn_w_up[layer, 0], ffn_w_down[layer, 0]),
                       ln_gain[layer, 0], ln_bias[layer, 0])
        j = layer // N_MIXERS
        if layer % N_MIXERS == 0:
            mix, xq, s = hgrn2_mixer(x, hgrn_w_in[j], lower_bounds[j], hgrn_norm[j], s_hgrn[j])
            w_out = hgrn_w_out[j]
            new_h.append(s)
        else:
            mix, xq, s = gla_mixer(x, gla_w_in[j], gla_w_gate2[j], gla_b_gate[j], gla_norm[j], s_gla[j])
            w_out = gla_w_out[j]
            new_g.append(s)
        xo = memory_attention(xq, mem_k[layer], mem_v[layer])
        x = layer_norm(ALPHA * x + jnp.concatenate([mix, xo], axis=-1) @ w_out,
                       ln_gain[layer, 1], ln_bias[layer, 1])
        x = layer_norm(ALPHA * x + 0.5 * swiglu(x, ffn_w_gate[layer, 1], ffn_w_up[layer, 1], ffn_w_down[layer, 1]),
                       ln_gain[layer, 2], ln_bias[layer, 2])
    return x, jnp.stack(new_h), jnp.stack(new_g)


def setup_inputs(seed: int = 0) -> dict:
    key = jax.random.key(seed)
    ks = jax.random.split(key, 26)
    nrm = lambda k, shape, scale: jax.random.normal(k, shape, jnp.float32) * scale
    d = D_MODEL
    hgrn_col_scale = jnp.concatenate([jnp.ones((2 * HGRN_WIDTH,)), BETA * jnp.ones((HGRN_WIDTH,)),
                                      jnp.ones((HGRN_WIDTH + MEM_WIDTH,))])
    gla_col_scale = jnp.concatenate([jnp.ones((2 * GLA_QK,)), BETA * jnp.ones((GLA_V,)),
                                     jnp.ones((GLA_V + GLA_GATE_RANK + MEM_WIDTH,))])
    return {
        'x_prompt': nrm(ks[0], (BATCH, SEQ, d), 1.0),
        'x_sample': nrm(ks[1], (DEC_BATCH, DEC_SEQ, d), 1.0),
        'mem_prompt': nrm(ks[2], (BATCH, MEM_TOKENS, d), 1.0),
        'cache_mem_k': nrm(ks[3], (DEPTH, DEC_BATCH, MEM_TOKENS, MEM_HEADS, MEM_HEAD_DIM), 1.0),
        'cache_mem_v': nrm(ks[4], (DEPTH, DEC_BATCH, MEM_TOKENS, MEM_HEADS, MEM_HEAD_DIM), BETA),
        'state_hgrn': nrm(ks[5], (N_HGRN_LAYERS, DEC_BATCH, HGRN_HEADS, HGRN_HEAD_DIM, HGRN_HEAD_DIM), 0.5),
        'state_gla': nrm(ks[6], (N_GLA_LAYERS, DEC_BATCH, GLA_HEADS, GLA_KEY_DIM, GLA_VAL_DIM), 1.0),
        'ffn_w_gate': nrm(ks[7], (DEPTH, 2, d, D_FF), d ** -0.5),
        'ffn_w_up': nrm(ks[8], (DEPTH, 2, d, D_FF), d ** -0.5),
        'ffn_w_down': nrm(ks[9], (DEPTH, 2, D_FF, d), BETA * D_FF ** -0.5),
        'ln_gain': 1.0 + nrm(ks[10], (DEPTH, 3, d), 0.02),
        'ln_bias': nrm(ks[11], (DEPTH, 3, d), 0.02),
        'hgrn_w_in': nrm(ks[12], (N_HGRN_LAYERS, d, HGRN_IN), d ** -0.5) * hgrn_col_scale,
        'hgrn_lb_logits': nrm(ks[13], (N_HGRN_LAYERS, HGRN_WIDTH), 0.5),
        'hgrn_norm': 1.0 + nrm(ks[14], (N_HGRN_LAYERS, HGRN_WIDTH), 0.02),
        'hgrn_w_out': nrm(ks[15], (N_HGRN_LAYERS, HGRN_MIX_WIDTH, d), BETA * HGRN_MIX_WIDTH ** -0.5),
        'gla_w_in': nrm(ks[16], (N_GLA_LAYERS, d, GLA_IN), d ** -0.5) * gla_col_scale,
        'gla_w_gate2': nrm(ks[17], (N_GLA_LAYERS, GLA_GATE_RANK, GLA_QK), GLA_GATE_RANK ** -0.5),
        'gla_b_gate': nrm(ks[18], (N_GLA_LAYERS, GLA_QK), 0.1),
        'gla_norm': 1.0 + nrm(ks[19], (N_GLA_LAYERS, GLA_V), 0.02),
        'gla_w_out': nrm(ks[20], (N_GLA_LAYERS, GLA_MIX_WIDTH, d), BETA * GLA_MIX_WIDTH ** -0.5),
        'mem_w_k': nrm(ks[21], (DEPTH, d, MEM_WIDTH), d ** -0.5),
        'mem_w_v': nrm(ks[22], (DEPTH, d, MEM_WIDTH), BETA * d ** -0.5),
    }


def reference(x_prompt, x_sample, mem_prompt, cache_mem_k, cache_mem_v, state_hgrn, state_gla,
              ffn_w_gate, ffn_w_up, ffn_w_down, ln_gain, ln_bias,
              hgrn_w_in, hgrn_lb_logits, hgrn_norm, hgrn_w_out,
              gla_w_in, gla_w_gate2, gla_b_gate, gla_norm, gla_w_out, mem_w_k, mem_w_v):
    weights = (ffn_w_gate, ffn_w_up, ffn_w_down, ln_gain, ln_bias,
               hgrn_w_in, hgrn_lb_logits, hgrn_norm, hgrn_w_out,
               gla_w_in, gla_w_gate2, gla_b_gate, gla_norm, gla_w_out)
    b = x_prompt.shape[0]
    mem_k_prompt = jnp.einsum('bmd,ldc->lbmc', mem_prompt, mem_w_k).reshape(DEPTH, b, MEM_TOKENS, MEM_HEADS, MEM_HEAD_DIM)
    mem_v_prompt = jnp.einsum('bmd,ldc->lbmc', mem_prompt, mem_w_v).reshape(DEPTH, b, MEM_TOKENS, MEM_HEADS, MEM_HEAD_DIM)
    zeros_h = jnp.zeros((N_HGRN_LAYERS, b, HGRN_HEADS, HGRN_HEAD_DIM, HGRN_HEAD_DIM), x_prompt.dtype)
    zeros_g = jnp.zeros((N_GLA_LAYERS, b, GLA_HEADS, GLA_KEY_DIM, GLA_VAL_DIM), x_prompt.dtype)
    y_prompt, state_hgrn_prompt, state_gla_prompt = run_trunk(
        x_prompt, mem_k_prompt, mem_v_prompt, zeros_h, zeros_g, *weights)
    y_sample, state_hgrn_sample, state_gla_sample = run_trunk(
        x_sample, cache_mem_k, cache_mem_v, state_hgrn, state_gla, *weights)
    return (y_prompt, y_sample, state_hgrn_prompt, state_gla_prompt, mem_k_prompt, mem_v_prompt,
            state_hgrn_sample, state_gla_sample)
```

```python
import os
import numpy as np
from contextlib import ExitStack
import concourse.bass as bass
import concourse.mybir as mybir
from concourse.bass_utils import run_bass_kernel_spmd

F32 = mybir.dt.float32
BF16 = mybir.dt.bfloat16
AF = mybir.ActivationFunctionType
ALU = mybir.AluOpType

D = 1024
DFF = 2816
NFF = 22
CH = 64
MEMT = 256
ALPHA = (2.0 * 4) ** 0.25
LN_EPS = 1e-5
RMS_EPS = 1e-6
GATE_CLAMP = 1.0 - 1e-6
HG_IN = 4608
GL_IN = 3600
NSLOT = 6
SLOTC = 2048
NF = 24
NH_ = 64
SEM_LIM = 16000
STAGE = int(os.environ.get("KSTAGE", "9"))
SUB = int(os.environ.get("KSUB", "9"))


def acopy(e, out, in_):
    return e.activation(out=out, in_=in_, func=AF.Identity)


class Buf:
    __slots__ = ("w", "r", "x")

    def __init__(self, x=False):
        self.w = {}
        self.r = {}
        self.x = x


class Sched:
    def __init__(self, nc, es):
        self.nc = nc
        self.es = es
        self.names = ["pe", "act", "dve", "pool", "sp"]
        self.q = {k: [] for k in self.names}
        self.cnt = {k: 0 for k in self.names}
        self.semlist = {k: [] for k in self.names}
        self.waited = {k: {} for k in self.names}
        self.dsems = []

    def _sem(self, name):
        return self.es.enter_context(self.nc.semaphore(name))

    def new_dsem(self):
        self.dsems.append([self._sem("d%d" % len(self.dsems)), 0])
        return len(self.dsems) - 1

    def resolve(self, key, val):
        if isinstance(key, tuple):
            return self.dsems[key[1]][0], val
        k = (val - 1) // SEM_LIM
        return self.semlist[key][k], (val - 1) % SEM_LIM + 1

    def op(self, eng, fn, reads=(), writes=(), dsem=None, ndma=1):
        deps = {}
        for b in reads:
            for key, v in b.w.items():
                if deps.get(key, 0) < v:
                    deps[key] = v
            if b.x:
                for key, v in b.r.items():
                    if key != eng and deps.get(key, 0) < v:
                        deps[key] = v
        for b in writes:
            for key, v in b.w.items():
                if deps.get(key, 0) < v:
                    deps[key] = v
            for key, v in b.r.items():
                if deps.get(key, 0) < v:
                    deps[key] = v
        wd = self.waited[eng]
        waits = []
        for key, v in deps.items():
            if key == "pe" and eng == "pe":
                continue
            if wd.get(key, 0) < v:
                wd[key] = v
                waits.append((key, v))
        if dsem is None:
            self.cnt[eng] += 1
            val = self.cnt[eng]
            key = eng
            k = (val - 1) // SEM_LIM
            while len(self.semlist[eng]) <= k:
                self.semlist[eng].append(self._sem("%s%d" % (eng, len(self.semlist[eng]))))
            inc = 1
        else:
            d = self.dsems[dsem]
            d[1] += 16 * ndma
            val = d[1]
            key = ("d", dsem)
            inc = 16
        self.q[eng].append((fn, waits, key, val, inc))
        for b in reads:
            if b.r.get(key, 0) < val:
                b.r[key] = val
        for b in writes:
            b.w = {key: val}
            b.r = {}
        return (key, val)

    def emit(self, block):
        bn = {"pe": "tensor", "act": "scalar", "dve": "vector", "pool": "gpsimd", "sp": "sync"}
        for name in self.names:
            ops = self.q[name]

            def body(e, ops=ops):
                for fn, waits, key, val, inc in ops:
                    for k, v in waits:
                        sem, sv = self.resolve(k, v)
                        e.wait_ge(sem, sv)
                    ins = fn(e)
                    sem, _ = self.resolve(key, val)
                    if isinstance(ins, list):
                        for i_ in ins:
                            i_.then_inc(sem, inc)
                    else:
                        ins.then_inc(sem, inc)

            getattr(block, bn[name])(body)


def build(depth, seq):
    NHL = (depth + 1) // 2
    NGL = depth // 2
    assert NHL <= 2
    nc = bass.Bass("TRN2", target_bir_lowering=False)

    def din(name, shape):
        return nc.dram_tensor(name, list(shape), F32, kind="ExternalInput").ap()

    def dout(name, shape):
        return nc.dram_tensor(name, list(shape), F32, kind="ExternalOutput").ap()

    x_p = din("x_p", [seq, D])
    x_s = din("x_s", [128, D])
    mem_p = din("mem_p", [MEMT, D])
    cmk = din("cmk", [depth, 2, MEMT, 512])
    cmv = din("cmv", [depth, 2, MEMT, 512])
    st_h = din("st_h", [NHL, 2, 8, 128, 128])
    st_g = din("st_g", [max(NGL, 1), 2, 4, 128, 256])
    w_gate = din("ffn_w_gate", [depth, 2, D, DFF])
    w_up = din("ffn_w_up", [depth, 2, D, DFF])
    w_down = din("ffn_w_down", [depth, 2, DFF, D])
    ln_gain = din("ln_gain", [depth, 3, D])
    ln_bias = din("ln_bias", [depth, 3, D])
    hg_w_in = din("hgrn_w_in", [NHL, D, HG_IN])
    hg_lb = din("hgrn_lb_logits", [NHL, D])
    hg_norm = din("hgrn_norm", [NHL, D])
    hg_w_out = din("hgrn_w_out", [NHL, 1536, D])
    gl_w_in = din("gla_w_in", [max(NGL, 1), D, GL_IN])
    gl_wg2 = din("gla_w_gate2", [max(NGL, 1), 16, 512])
    gl_bg = din("gla_b_gate", [max(NGL, 1), 512])
    gl_norm = din("gla_norm", [max(NGL, 1), D])
    gl_w_out = din("gla_w_out", [max(NGL, 1), 1536, D])
    mem_wk = din("mem_w_k", [depth, D, 512])
    mem_wv = din("mem_w_v", [depth, D, 512])

    y_p = dout("y_p", [seq, D])
    y_s = dout("y_s", [128, D])
    sh_p = dout("sh_p", [NHL, 8, 128, 128])
    sg_p = dout("sg_p", [max(NGL, 1), 4, 128, 256])
    mk_p = dout("mk_p", [depth, MEMT, 512])
    mv_p = dout("mv_p", [depth, MEMT, 512])
    sh_s = dout("sh_s", [NHL, 2, 8, 128, 128])
    sg_s = dout("sg_s", [max(NGL, 1), 2, 4, 128, 256])

    pieces = []
    pidx = {}

    def reg(name, ncols, subs):
        pidx[name] = len(pieces)
        pieces.append((ncols, subs))

    for l in range(depth):
        j = l // 2
        hg = (l % 2 == 0)

        def reg_ffn(i):
            for c in range(NFF):
                reg((l, "gu", i, c), 2048, [(w_gate[l, i], 0, 8, c * 128, 128, 0),
                                            (w_up[l, i], 0, 8, c * 128, 128, 1024)])
            for jo in range(8):
                for hh in range(2):
                    reg((l, "dn", i, jo, hh), 1408, [(w_down[l, i], hh * 11, 11, jo * 128, 128, 0)])

        reg_ffn(0)
        if hg:
            W = hg_w_in[j]
            for pp in range(4):
                reg((l, "q", pp), 2048, [(W, 0, 8, (2 * pp + ci) * 128, 128, ci * 1024) for ci in range(2)])
                reg((l, "f", pp), 2048, [(W, 0, 8, 1024 + (2 * pp + ci) * 128, 128, ci * 1024) for ci in range(2)])
            for vp in range(4):
                reg((l, "v", vp), 2048, [(W, 0, 8, 2048 + vp * 256, 256, 0)])
            for pp in range(4):
                reg((l, "g", pp), 2048, [(W, 0, 8, 3072 + (2 * pp + ci) * 128, 128, ci * 1024) for ci in range(2)])
            for pp in range(2):
                reg((l, "xq", pp), 2048, [(W, 0, 8, 4096 + (2 * pp + ci) * 128, 128, ci * 1024) for ci in range(2)])
            WO = hg_w_out[j]
        else:
            W = gl_w_in[j]
            reg((l, "ga"), 128, [(W, 0, 8, 3072, 16, 0)])
            for pp in range(2):
                reg((l, "q", pp), 2048, [(W, 0, 8, (2 * pp + ci) * 128, 128, ci * 1024) for ci in range(2)])
                reg((l, "f", pp), 2048, [(W, 0, 8, 512 + (2 * pp + ci) * 128, 128, ci * 1024) for ci in range(2)])
            for vp in range(4):
                reg((l, "v", vp), 2048, [(W, 0, 8, 1024 + vp * 256, 256, 0)])
            for pp in range(4):
                reg((l, "g", pp), 2048, [(W, 0, 8, 2048 + (2 * pp + ci) * 128, 128, ci * 1024) for ci in range(2)])
            for pp in range(2):
                reg((l, "xq", pp), 2048, [(W, 0, 8, 3088 + (2 * pp + ci) * 128, 128, ci * 1024) for ci in range(2)])
            WO = gl_w_out[j]
        for jo in range(8):
            reg((l, "out", jo), 1536, [(WO, 0, 12, jo * 128, 128, 0)])
        reg_ffn(1)
    NP = len(pieces)
    wscr = nc.dram_tensor("wscr", [NP, 128, SLOTC], BF16, kind="Internal").ap()

    es = ExitStack()
    with es:
        S = Sched(nc, es)

        def sb(name, shape, dt):
            return es.enter_context(nc.sbuf_tensor(name, list(shape), dt))

        RG = sb("ring", [128, NSLOT, SLOTC], BF16)
        X = sb("X", [128, 8, 512], F32)
        XB = sb("XB", [128, 8, 512], BF16)
        FA = sb("FA", [128, NF, 512], F32)
        HA = sb("HA", [128, NH_, 512], BF16)
        ST = sb("ST", [128, depth, 1024], F32)
        KTP = sb("KTP", [128, depth, 4, 256], BF16)
        VP = sb("VP", [128, depth, 2, 512], BF16)
        IDF = sb("identf", [128, 128], F32)
        IDB = sb("identb", [128, 128], BF16)
        ONES = sb("onesb", [128, 128], BF16)
        MASK = sb("mask", [64, 512], F32)
        RMASK = sb("rmask", [128, 512], F32)
        CST = sb("cst", [128, 4], F32)
        CA = sb("constA", [128, 128], F32)
        CB = sb("constB", [128, 128], F32)
        PST = sb("pstage", [128, 128], F32)
        HGP = sb("hgp", [128, 2, 2, 8], F32)
        NBG = sb("nbg", [128, 8], F32)
        WG2 = sb("wg2", [16, 2, 512], BF16)
        WG2F = sb("wg2f", [16, 2, 512], F32)
        GAT = sb("gaT", [16, 512], BF16)
        EBL = sb("ebl", [128, 8, 8], F32)
        TINY = sb("tiny", [128, 16], F32)
        PS = es.enter_context(nc.psum_tensor("ps", [128, 8, 512], F32))

        RGb = [Buf() for _ in range(NSLOT)]
        Xb = [Buf() for _ in range(8)]
        XBb = [Buf() for _ in range(8)]
        Fb = [Buf() for _ in range(NF)]
        Hb = [Buf() for _ in range(NH_)]
        STb = [Buf() for _ in range(depth)]
        KTPb = [Buf() for _ in range(depth)]
        VPb = [Buf() for _ in range(depth)]
        PSb = [Buf(True) for _ in range(8)]
        WSb = [Buf() for _ in range(NP)]
        cB = Buf()
        gatB = Buf()
        eblB = Buf()
        tinyB = Buf()
        outB = []

        ring_ds = [S.new_dsem() for _ in range(NSLOT)]

        def H(i, T=512):
            return HA[:, i, :T]

        def Fv(i, T=512):
            return FA[:, i, :T]

        def Fbf(i, rows=128):
            return FA[0:rows, i, :].bitcast(BF16)

        def Fwide(i, n):
            return FA[:, i:i + n, :].rearrange("p a c -> p (a c)")

        def Fwide_bf(i, n):
            return FA[:, i:i + n, :].bitcast(BF16).rearrange("p a c -> p (a c)")

        rot = {"b1": 0, "b2": 0}

        def bank1():
            b = rot["b1"] % 4
            rot["b1"] += 1
            return b

        def bank2():
            b = 4 + 2 * (rot["b2"] % 2)
            rot["b2"] += 1
            return b

        def PSv(b, T=512, rows=128):
            return PS[0:rows, b, :T]

        def PS2(b, rows=128):
            return PS[0:rows, b:b + 2, :].rearrange("p a c -> p (a c)")

        def PSbf(b, rows=128):
            return PS[0:rows, b, :].bitcast(BF16)

        def c_init(e):
            e.memset(CST[:, 0:1], LN_EPS)
            e.memset(CST[:, 1:2], 1.0)
            e.memset(CST[:, 2:3], RMS_EPS)
            e.memset(CST[:, 3:4], 0.0)
            e.memset(ONES[:], 1.0)
            e.memset(IDF[:], 1.0)
            e.memset(MASK[:], 1.0)
            e.memset(HGP[:, 0, :, :], 0.5)
            e.memset(HGP[:, 1, :, :], -0.5)
            return e.memset(RMASK[:], 1.0)

        S.op("pool", c_init, writes=[cB])
        S.op("pool", lambda e: e.affine_select(out=IDF[:], in_=IDF[:], pattern=[[1, 128]], compare_op=ALU.is_equal,
                                               fill=0.0, base=0, channel_multiplier=-1), reads=[cB], writes=[cB])
        S.op("pool", lambda e: e.affine_select(out=MASK[:], in_=MASK[:], pattern=[[0, 8], [1, 64]],
                                               compare_op=ALU.is_ge, fill=0.0, base=0, channel_multiplier=-1),
             reads=[cB], writes=[cB])
        S.op("pool", lambda e: e.memset(RMASK[:].rearrange("p (c t) -> p c t", t=64)[:, :, 0:1], 0.0),
             reads=[cB], writes=[cB])
        S.op("dve", lambda e: e.tensor_copy(out=IDB[:], in_=IDF[:]), reads=[cB], writes=[cB])

        nln = depth * 3 * 8
        oA_lb = nln
        oA_hn = nln + NHL * 8
        oB_bg = nln
        oB_gn = nln + NGL * 4
        io_ds = [S.new_dsem() for _ in range(4)]

        def load_rows(dst, r0, src2d, nrows, ds):
            S.op("pool", lambda e: e.dma_start(out=dst[r0:r0 + nrows, :], in_=src2d), writes=[tinyB], dsem=ds)

        S.op("pool", lambda e: e.memset(PST[:], 0.0), writes=[tinyB])
        load_rows(PST, 0, ln_gain.rearrange("l i (c p) -> (l i c) p", p=128), nln, io_ds[0])
        load_rows(PST, oA_lb, hg_lb.rearrange("j (c p) -> (j c) p", p=128), NHL * 8, io_ds[1])
        load_rows(PST, oA_hn, hg_norm.rearrange("j (c p) -> (j c) p", p=128), NHL * 8, io_ds[2])
        S.op("pe", lambda e: e.transpose(out=PS[:, 0, 0:128], in_=PST[:], identity=IDF[:]),
             reads=[tinyB, cB], writes=[PSb[0]])
        S.op("dve", lambda e: e.tensor_copy(out=CA[:], in_=PS[:, 0, 0:128]), reads=[PSb[0]], writes=[cB])
        S.op("pool", lambda e: e.memset(PST[:], 0.0), writes=[tinyB])
        load_rows(PST, 0, ln_bias.rearrange("l i (c p) -> (l i c) p", p=128), nln, io_ds[0])
        if NGL:
            load_rows(PST, oB_bg, gl_bg.rearrange("j (c p) -> (j c) p", p=128), NGL * 4, io_ds[1])
            load_rows(PST, oB_gn, gl_norm.rearrange("j (c p) -> (j c) p", p=128), NGL * 8, io_ds[2])
        S.op("pe", lambda e: e.transpose(out=PS[:, 1, 0:128], in_=PST[:], identity=IDF[:]),
             reads=[tinyB, cB], writes=[PSb[1]])
        S.op("dve", lambda e: e.tensor_copy(out=CB[:], in_=PS[:, 1, 0:128]), reads=[PSb[1]], writes=[cB])
        if NHL == 2:
            S.op("dve", lambda e: e.tensor_tensor(out=TINY[:, 0:8], in0=CA[:, oA_lb + 8:oA_lb + 16],
                                                  in1=CA[:, oA_lb:oA_lb + 8], op=ALU.subtract),
                 reads=[cB], writes=[tinyB])
            S.op("act", lambda e: e.activation(out=TINY[:, 0:8], in_=TINY[:, 0:8], func=AF.Exp),
                 reads=[tinyB], writes=[tinyB])
            S.op("dve", lambda e: e.tensor_scalar(out=TINY[:, 0:8], in0=TINY[:, 0:8], scalar1=1.0, scalar2=None,
                                                  op0=ALU.add), reads=[tinyB], writes=[tinyB])
            S.op("dve", lambda e: e.reciprocal(out=TINY[:, 8:16], in_=TINY[:, 0:8]), reads=[tinyB], writes=[tinyB])
            S.op("dve", lambda e: e.tensor_scalar(out=HGP[:, 0, 1, :], in0=TINY[:, 8:16], scalar1=0.5, scalar2=None,
                                                  op0=ALU.mult), reads=[tinyB, cB], writes=[cB])
            S.op("dve", lambda e: e.tensor_scalar(out=HGP[:, 1, 1, :], in0=TINY[:, 8:16], scalar1=-0.5, scalar2=None,
                                                  op0=ALU.mult), reads=[tinyB, cB], writes=[cB])
        if NGL:
            S.op("dve", lambda e: e.tensor_scalar(out=NBG[:, 0:NGL * 4], in0=CB[:, oB_bg:oB_bg + NGL * 4],
                                                  scalar1=-1.0, scalar2=None, op0=ALU.mult), reads=[cB], writes=[cB])
            for j in range(NGL):
                S.op("pool", lambda e, j=j: e.dma_start(out=WG2F[:, j, :], in_=gl_wg2[j]), writes=[tinyB],
                     dsem=io_ds[3])
            S.op("dve", lambda e: e.tensor_copy(out=WG2[:, 0:NGL, :], in_=WG2F[:, 0:NGL, :]), reads=[tinyB],
                 writes=[cB])

        st_ds = [S.new_dsem() for _ in range(3)]
        sf_ds = [[S.new_dsem() for _ in range(2)] for _ in range(3)]
        cast_eng = ["dve", "act", "pool"]

        def conv_load(p):
            k = p % 3
            ncols, subs = pieces[p]
            sf = Fwide(4 * k, 4)

            def fn(e):
                out = []
                for (src, kc0, nkc, col0, w, off) in subs:
                    out.append(e.dma_start(
                        out=sf[:, off:off + nkc * w].rearrange("p (k c) -> p k c", c=w),
                        in_=src[kc0 * 128:(kc0 + nkc) * 128, col0:col0 + w].rearrange("(k p) c -> p k c", p=128)))
                return out
            S.op("sp", fn, writes=Fb[4 * k:4 * k + 4], dsem=sf_ds[k][0], ndma=len(subs))

        def conv_cast_store(p):
            k = p % 3
            ncols, subs = pieces[p]
            sf = Fwide(4 * k, 4)
            sbf = Fwide_bf(12 + 2 * k, 2)
            eng = cast_eng[p % 3]

            def fc(e):
                if eng == "act":
                    return acopy(e, out=sbf[:, :ncols], in_=sf[:, :ncols])
                return e.tensor_copy(out=sbf[:, :ncols], in_=sf[:, :ncols])
            S.op(eng, fc, reads=Fb[4 * k:4 * k + 4], writes=Fb[12 + 2 * k:14 + 2 * k])
            S.op("sp", lambda e: e.dma_start(out=wscr[p, :, :ncols], in_=sbf[:, :ncols]),
                 reads=Fb[12 + 2 * k:14 + 2 * k], writes=[WSb[p]], dsem=st_ds[k])

        for p in range((NP + 2) if STAGE >= 1 else 0):
            if p < NP:
                conv_load(p)
            if p >= 2:
                conv_cast_store(p - 2)

        rstate = {"n": 0}

        def fetch(name):
            p = pidx[name]
            ncols = pieces[p][0]
            s = rstate["n"] % NSLOT
            rstate["n"] += 1
            S.op("sp", lambda e: e.dma_start(out=RG[:, s, :ncols], in_=wscr[p, :, :ncols]),
                 reads=[WSb[p]], writes=[RGb[s]], dsem=ring_ds[s])
            return s

        mem_ds = [S.new_dsem() for _ in range(4)]
        memT = Fwide_bf(4, 2).rearrange("p (k m) -> p k m", m=256)
        for g in range(2 if STAGE >= 2 else 0):
            S.op("pool", lambda e, g=g: e.dma_start(out=Fwide(2 * g, 2), in_=mem_p[g * 128:(g + 1) * 128, :]),
                 writes=Fb[2 * g:2 * g + 2], dsem=mem_ds[g])
            for half in range(2):
                b = bank1()

                def ft(e, g=g, half=half, b=b):
                    for k in range(4):
                        c = half * 4 + k
                        ins = e.transpose(out=PS[:, b, k * 128:(k + 1) * 128],
                                          in_=Fwide(2 * g, 2)[:, c * 128:(c + 1) * 128], identity=IDF[:])
                    return ins
                S.op("pe", ft, reads=Fb[2 * g:2 * g + 2] + [cB], writes=[PSb[b]])
                S.op("dve", lambda e, g=g, half=half, b=b: e.tensor_copy(
                    out=memT[:, half * 4:half * 4 + 4, g * 128:(g + 1) * 128],
                    in_=PS[:, b, :].rearrange("p (k m) -> p k m", m=128)),
                    reads=[PSb[b]], writes=Fb[4:6])
        wst = Fwide(6, 8).rearrange("p (k c) -> p k c", c=512)
        wbf = Fwide_bf(14, 4).rearrange("p (k c) -> p k c", c=512)
        for l in range(depth if (STAGE >= 2 and SUB >= 2) else 0):
            for kv in range(2):
                wsrc = (mem_wk if kv == 0 else mem_wv)[l]
                odst = (mk_p if kv == 0 else mv_p)[l]
                S.op("sp", lambda e, wsrc=wsrc: e.dma_start(out=wst, in_=wsrc.rearrange("(k p) c -> p k c", p=128)),
                     writes=Fb[6:14], dsem=mem_ds[2])
                S.op("act", lambda e: acopy(e, out=wbf, in_=wst), reads=Fb[6:14], writes=Fb[14:18])
                for mg in range(2 if SUB >= 3 else 0):
                    b = bank1()

                    def fm(e, mg=mg, b=b):
                        for kc in range(8):
                            ins = e.matmul(PS[:, b, :], lhsT=memT[:, kc, mg * 128:(mg + 1) * 128], rhs=wbf[:, kc, :],
                                           start=(kc == 0), stop=(kc == 7))
                        return ins
                    S.op("pe", fm, reads=Fb[4:6] + Fb[14:18], writes=[PSb[b]])
                    S.op("dve", lambda e, b=b: e.tensor_copy(out=Fv(18), in_=PS[:, b, :]), reads=[PSb[b]],
                         writes=[Fb[18]])
                    if kv == 1:
                        S.op("act", lambda e, b=b, l=l, mg=mg: acopy(e, out=VP[:, l, mg, :], in_=PS[:, b, :]),
                             reads=[PSb[b]], writes=[VPb[l]])
                    if SUB < 4:
                        continue
                    ob = Buf()
                    outB.append(ob)
                    S.op("pool", lambda e, odst=odst, mg=mg: e.dma_start(out=odst[mg * 128:(mg + 1) * 128, :],
                                                                        in_=Fv(18)),
                         reads=[Fb[18]], writes=[ob], dsem=mem_ds[3])
                if kv == 0 and SUB >= 5:
                    for h in range(4):
                        b = bank1()

                        def fk(e, h=h, b=b):
                            for kc in range(8):
                                ins = e.matmul(PS[:, b, 0:256], lhsT=wbf[:, kc, h * 128:(h + 1) * 128],
                                               rhs=memT[:, kc, :], start=(kc == 0), stop=(kc == 7))
                            return ins
                        S.op("pe", fk, reads=Fb[4:6] + Fb[14:18], writes=[PSb[b]])
                        S.op("act", lambda e, b=b, l=l, h=h: acopy(e, out=KTP[:, l, h, :], in_=PS[:, b, 0:256]),
                             reads=[PSb[b]], writes=[KTPb[l]])

        xs_ds = [S.new_dsem() for _ in range(2)]
        ys_ds = [S.new_dsem() for _ in range(2)]
        st_io = [S.new_dsem() for _ in range(depth)]
        smem_ds = [S.new_dsem() for _ in range(4)]

        def load_x(src, tok0, T):
            for g in range(T // 128):
                f0 = 8 + 2 * (g % 2)
                S.op("pool", lambda e, g=g, f0=f0: e.dma_start(out=Fwide(f0, 2),
                                                              in_=src[tok0 + g * 128:tok0 + (g + 1) * 128, :]),
                     writes=Fb[f0:f0 + 2], dsem=xs_ds[g % 2])
                for half in range(2):
                    b = bank1()

                    def ft(e, f0=f0, half=half, b=b):
                        for k in range(4):
                            c = half * 4 + k
                            ins = e.transpose(out=PS[:, b, k * 128:(k + 1) * 128],
                                              in_=Fwide(f0, 2)[:, c * 128:(c + 1) * 128], identity=IDF[:])
                        return ins
                    S.op("pe", ft, reads=Fb[f0:f0 + 2] + [cB], writes=[PSb[b]])
                    pv = PS[:, b, :].rearrange("p (k m) -> p k m", m=128)
                    S.op("act", lambda e, g=g, half=half, pv=pv: acopy(e,
                        out=X[:, half * 4:half * 4 + 4, g * 128:(g + 1) * 128], in_=pv),
                        reads=[PSb[b]], writes=Xb[half * 4:half * 4 + 4])
                    S.op("dve", lambda e, g=g, half=half, pv=pv: e.tensor_copy(
                        out=XB[:, half * 4:half * 4 + 4, g * 128:(g + 1) * 128], in_=pv),
                        reads=[PSb[b]], writes=XBb[half * 4:half * 4 + 4])

        def store_y(dst, tok0, T):
            for g in range(T // 128):
                f0 = 12 + 2 * (g % 2)
                for half in range(2):
                    b = bank1()

                    def ft(e, g=g, half=half, b=b):
                        for k in range(4):
                            c = half * 4 + k
                            ins = e.transpose(out=PS[:, b, k * 128:(k + 1) * 128],
                                              in_=X[:, c, g * 128:(g + 1) * 128], identity=IDF[:])
                        return ins
                    S.op("pe", ft, reads=Xb[half * 4:half * 4 + 4] + [cB], writes=[PSb[b]])
                    S.op("act" if half == 0 else "dve",
                         (lambda e, f0=f0, half=half, b=b: acopy(e, out=Fv(f0 + half), in_=PS[:, b, :])) if half == 0 else
                         (lambda e, f0=f0, half=half, b=b: e.tensor_copy(out=Fv(f0 + half), in_=PS[:, b, :])),
                         reads=[PSb[b]], writes=[Fb[f0 + half]])
                ob = Buf()
                outB.append(ob)
                S.op("pool", lambda e, g=g, f0=f0: e.dma_start(out=dst[tok0 + g * 128:tok0 + (g + 1) * 128, :],
                                                              in_=Fwide(f0, 2)),
                     reads=Fb[f0:f0 + 2], writes=[ob], dsem=ys_ds[g % 2])

        def layer_norm(l, i, T):
            col0 = l * 24 + i * 8
            for c in range(8):
                S.op("act", lambda e, c=c: acopy(e, out=XB[:, c, :T], in_=X[:, c, :T]), reads=[Xb[c]], writes=[XBb[c]])
                S.op("act", lambda e, c=c: e.activation(out=H(25 + c, T), in_=X[:, c, :T], func=AF.Square),
                     reads=[Xb[c]], writes=[Hb[25 + c]])
            bs = 4
            bq = 5

            def fs(e):
                for c in range(8):
                    ins = e.matmul(PSv(bs, T), lhsT=ONES[:], rhs=XB[:, c, :T], start=(c == 0), stop=(c == 7))
                return ins
            S.op("pe", fs, reads=XBb + [cB], writes=[PSb[bs]])

            def fq(e):
                for c in range(8):
                    ins = e.matmul(PSv(bq, T), lhsT=ONES[:], rhs=H(25 + c, T), start=(c == 0), stop=(c == 7))
                return ins
            S.op("pe", fq, reads=Hb[25:33] + [cB], writes=[PSb[bq]])
            S.op("act", lambda e: e.activation(out=Fv(16, T), in_=PSv(bs, T), func=AF.Square, scale=1.0 / D),
                 reads=[PSb[bs]], writes=[Fb[16]])
            S.op("dve", lambda e: e.scalar_tensor_tensor(out=Fv(17, T), in0=PSv(bq, T), scalar=1.0 / D, in1=Fv(16, T),
                                                         op0=ALU.mult, op1=ALU.subtract),
                 reads=[PSb[bq], Fb[16]], writes=[Fb[17]])
            S.op("act", lambda e: e.activation(out=Fv(17, T), in_=Fv(17, T), func=AF.Ln, bias=CST[:, 0:1]),
                 reads=[Fb[17], cB], writes=[Fb[17]])
            br = 6
            S.op("act", lambda e: e.activation(out=PSv(br, T), in_=Fv(17, T), func=AF.Exp, scale=-0.5),
                 reads=[Fb[17]], writes=[PSb[br]])
            for c in range(8):
                tf = 8 + c
                S.op("dve", lambda e, c=c, tf=tf: e.scalar_tensor_tensor(
                    out=Fv(tf, T), in0=PSv(bs, T), scalar=-1.0 / D, in1=X[:, c, :T], op0=ALU.mult, op1=ALU.add),
                    reads=[PSb[bs], Xb[c]], writes=[Fb[tf]])
                S.op("dve", lambda e, tf=tf: e.tensor_tensor(out=Fv(tf, T), in0=Fv(tf, T), in1=PSv(br, T),
                                                             op=ALU.mult),
                     reads=[Fb[tf], PSb[br]], writes=[Fb[tf]])
                S.op("act", lambda e, c=c, tf=tf: e.activation(
                    out=XB[:, c, :T], in_=Fv(tf, T), func=AF.Identity,
                    scale=CA[:, col0 + c:col0 + c + 1], bias=CB[:, col0 + c:col0 + c + 1]),
                    reads=[Fb[tf], cB], writes=[XBb[c]])
            for c in range(8):
                tf = 8 + c
                S.op("act", lambda e, c=c, tf=tf: e.activation(
                    out=X[:, c, :T], in_=Fv(tf, T), func=AF.Identity,
                    scale=CA[:, col0 + c:col0 + c + 1], bias=CB[:, col0 + c:col0 + c + 1]),
                    reads=[Fb[tf], cB], writes=[Xb[c]])

        def ffn(l, i, T):
            def evac(c, bg, bu):
                hs = 22 + (c % 3)
                S.op("act", lambda e: e.activation(out=H(hs, T), in_=PSv(bg, T), func=AF.Silu),
                     reads=[PSb[bg]], writes=[Hb[hs]])
                S.op("dve", lambda e: e.scalar_tensor_tensor(
                    out=H(c, T), in0=PSv(bu, T), scalar=0.5, in1=H(hs, T), op0=ALU.mult, op1=ALU.mult),
                    reads=[PSb[bu], Hb[hs]], writes=[Hb[c]])

            s01 = [fetch((l, "gu", i, 0)), fetch((l, "gu", i, 1))]
            b01 = [(bank1(), bank1()), (bank1(), bank1())]
            for kc in range(8):
                def fk(e, kc=kc):
                    for c in range(2):
                        for gu in range(2):
                            ins = e.matmul(PSv(b01[c][gu], T),
                                           lhsT=RG[:, s01[c], gu * 1024 + kc * 128:gu * 1024 + (kc + 1) * 128],
                                           rhs=XB[:, kc, :T], start=(kc == 0), stop=(kc == 7))
                    return ins
                S.op("pe", fk, reads=[RGb[s01[0]], RGb[s01[1]], XBb[kc]],
                     writes=[PSb[b01[0][0]], PSb[b01[0][1]], PSb[b01[1][0]], PSb[b01[1][1]]])
            evac(0, b01[0][0], b01[0][1])
            evac(1, b01[1][0], b01[1][1])
            for c in range(2, NFF):
                s = fetch((l, "gu", i, c))
                bg = bank1()
                bu = bank1()

                def fm(e, s=s, bg=bg, bu=bu):
                    for kc in range(8):
                        e.matmul(PSv(bg, T), lhsT=RG[:, s, kc * 128:(kc + 1) * 128], rhs=XB[:, kc, :T],
                                 start=(kc == 0), stop=(kc == 7))
                    for kc in range(8):
                        ins = e.matmul(PSv(bu, T), lhsT=RG[:, s, 1024 + kc * 128:1024 + (kc + 1) * 128],
                                       rhs=XB[:, kc, :T], start=(kc == 0), stop=(kc == 7))
                    return ins
                S.op("pe", fm, reads=[RGb[s]] + XBb, writes=[PSb[bg], PSb[bu]])
                evac(c, bg, bu)
            for jo in range(8):
                s0 = fetch((l, "dn", i, jo, 0))
                s1 = fetch((l, "dn", i, jo, 1))
                b = bank1()

                def fd(e, s0=s0, s1=s1, b=b):
                    for kk in range(22):
                        s = s0 if kk < 11 else s1
                        k2 = kk % 11
                        ins = e.matmul(PSv(b, T), lhsT=RG[:, s, k2 * 128:(k2 + 1) * 128], rhs=H(kk, T),
                                       start=(kk == 0), stop=(kk == 21))
                    return ins
                S.op("pe", fd, reads=[RGb[s0], RGb[s1]] + Hb[0:22], writes=[PSb[b]])
                S.op("dve", lambda e, jo=jo, b=b: e.scalar_tensor_tensor(
                    out=X[:, jo, :T], in0=X[:, jo, :T], scalar=ALPHA, in1=PSv(b, T), op0=ALU.mult, op1=ALU.add),
                    reads=[Xb[jo], PSb[b]], writes=[Xb[jo]])

        def fm_chunk(s, ci, T, M=128, kcw=128, base=0):
            b = bank1()

            def fn(e):
                for kc in range(8):
                    o = base + ci * 1024 + kc * kcw
                    ins = e.matmul(PS[0:M, b, :T], lhsT=RG[:, s, o:o + M], rhs=XB[:, kc, :T],
                                   start=(kc == 0), stop=(kc == 7))
                return ins
            S.op("pe", fn, reads=[RGb[s]] + XBb, writes=[PSb[b]])
            return b

        def fm_multi(specs, T):
            banks = [bank1() for _ in specs]
            slots_ = sorted(set(sp[0] for sp in specs))
            for kc in range(8):
                def fn(e, kc=kc):
                    for (s_, off, M, kcw), b in zip(specs, banks):
                        o = off + kc * kcw
                        ins = e.matmul(PS[0:M, b, :T], lhsT=RG[:, s_, o:o + M], rhs=XB[:, kc, :T],
                                       start=(kc == 0), stop=(kc == 7))
                    return ins
                S.op("pe", fn, reads=[RGb[s_] for s_ in slots_] + [XBb[kc]], writes=[PSb[b] for b in banks])
            return banks

        def mixer(l, tile):
            T = tile["T"]
            nch = T // CH
            j = l // 2
            hg = (l % 2 == 0)
            nkh = 8 if hg else 4
            sg = 1.0 if hg else -1.0 / 16.0
            qscale = 128.0 ** -0.5

            def kh_of(u):
                return u if hg else u // 2

            if not hg:
                s = fetch((l, "ga"))
                b = fm_multi([(s, 0, 16, 16)], T)[0]
                S.op("dve", lambda e, b=b: e.tensor_copy(out=GAT[:, :T], in_=PS[0:16, b, :T]), reads=[PSb[b]],
                     writes=[gatB])
            def v_piece(vp):
                sv_ = fetch((l, "v", vp))
                for c0 in range(0, nch, 2):
                    b = bank1()

                    def fv(e, c0=c0, b=b):
                        for cc in range(2):
                            ch = c0 + cc
                            for kc in range(8):
                                ins = e.matmul(PS[0:64, b, cc * 256:(cc + 1) * 256],
                                               lhsT=XB[:, kc, ch * 64:(ch + 1) * 64],
                                               rhs=RG[:, sv_, kc * 256:(kc + 1) * 256],
                                               start=(kc == 0), stop=(kc == 7))
                        return ins
                    S.op("pe", fv, reads=[RGb[sv_]] + XBb, writes=[PSb[b]])
                    S.op("act", lambda e, c0=c0, b=b: acopy(
                        e, out=FA[0:64, 14 + c0:16 + c0, :].bitcast(BF16)[:, :, vp * 256:(vp + 1) * 256],
                        in_=PS[0:64, b, :].rearrange("p (a c) -> p a c", c=256)),
                        reads=[PSb[b]], writes=Fb[14 + c0:16 + c0])

            vdone = 0
            if hg:
                groups = [([0, 1, 2], 3), ([3], 1)]
            else:
                groups = [([0, 1], 0)]
            for pps, nv_after in groups:
                heads = []
                for pp in pps:
                    sq_ = fetch((l, "q", pp))
                    sf_ = fetch((l, "f", pp))
                    pre = {}
                    if pp == 0:
                        ncis = 2 if hg else 1
                        specs = []
                        for ci in range(ncis):
                            specs += [(sq_, ci * 1024, 128, 128), (sf_, ci * 1024, 128, 128)]
                        bks = fm_multi(specs, T)
                        for ci in range(ncis):
                            pre[ci] = (bks[2 * ci], bks[2 * ci + 1])
                    for ci in range(2):
                        kh = 2 * pp + ci
                        heads.append(kh)
                        if ci in pre:
                            bq_, bf_ = pre[ci]
                        else:
                            bq_ = fm_chunk(sq_, ci, T)
                            bf_ = fm_chunk(sf_, ci, T)
                        hi = kh % 6
                        if hg:
                            S.op("act", lambda e, bq_=bq_, hi=hi: e.activation(out=H(52 + hi, T), in_=PSv(bq_, T),
                                                                              func=AF.Silu),
                                 reads=[PSb[bq_]], writes=[Hb[52 + hi]])
                            S.op("act", lambda e, bf_=bf_, hi=hi: e.activation(out=Fv(hi, T), in_=PSv(bf_, T),
                                                                              func=AF.Tanh, scale=0.5),
                                 reads=[PSb[bf_]], writes=[Fb[hi]])
                            S.op("act", lambda e, hi=hi, kh=kh: e.activation(
                                out=Fv(hi, T), in_=Fv(hi, T), func=AF.Identity,
                                scale=HGP[:, 1, j, kh:kh + 1], bias=HGP[:, 0, j, kh:kh + 1]),
                                reads=[Fb[hi], cB], writes=[Fb[hi]])
                            S.op("dve", lambda e, hi=hi: e.tensor_scalar(
                                out=Fv(hi, T), in0=Fv(hi, T), scalar1=GATE_CLAMP, scalar2=None,
                                op0=ALU.min), reads=[Fb[hi]], writes=[Fb[hi]])
                        else:
                            phase1b(l, tile, kh, bq_, bf_, sg, qscale)
                    if not hg:
                        for _ in range(2):
                            v_piece(vdone)
                            vdone += 1
                for _ in range(nv_after):
                    v_piece(vdone)
                    vdone += 1
                if hg:
                    for kh in heads:
                        phase1b(l, tile, kh, None, None, sg, qscale)
            assert vdone == 4

            def stage_A(ch):
                b = bank1()

                def fa(e):
                    for kh in range(nkh):
                        ins = e.matmul(PS[0:64, b, kh * 64:(kh + 1) * 64], lhsT=HA[:, 8 + kh, ch * 64:(ch + 1) * 64],
                                       rhs=HA[:, kh, ch * 64:(ch + 1) * 64], start=True, stop=True)
                    return ins
                S.op("pe", fa, reads=Hb[0:nkh] + Hb[8:8 + nkh], writes=[PSb[b]])
                pt = 50 + ch % 2
                S.op("dve", lambda e: e.tensor_tensor(out=HA[0:64, pt, :nkh * 64], in0=PS[0:64, b, :nkh * 64],
                                                      in1=MASK[:, :nkh * 64], op=ALU.mult),
                     reads=[PSb[b], cB], writes=[Hb[pt]])
                bt = bank1()

                def ftr(e):
                    for kh in range(nkh):
                        ins = e.transpose(out=PSbf(bt, 64)[:, kh * 128:(kh + 1) * 128],
                                          in_=HA[:, 16 + kh, ch * 64:(ch + 1) * 64], identity=IDB[:])
                    return ins
                S.op("pe", ftr, reads=Hb[16:16 + nkh] + [cB], writes=[PSb[bt]])
                kt = 22 + ch % 2
                S.op("act", lambda e: acopy(e, out=Fbf(kt, 64)[:, :nkh * 128], in_=PSbf(bt, 64)[:, :nkh * 128]),
                     reads=[PSb[bt]], writes=[Fb[kt]])
                bA = bank2()
                bB = bA + 1

                def fd(e):
                    for u in range(8):
                        kh = kh_of(u)
                        ins = e.matmul(PS[:, (bA if u < 4 else bB), (u % 4) * 128:(u % 4) * 128 + 128],
                                       lhsT=Fbf(kt, 64)[:, kh * 128:(kh + 1) * 128],
                                       rhs=Fbf(14 + ch, 64)[:, u * 128:(u + 1) * 128], start=True, stop=True)
                    return ins
                S.op("pe", fd, reads=[Fb[kt], Fb[14 + ch]], writes=[PSb[bA], PSb[bB]])
                return (bA, bB)

            def stage_state(ch, b2):
                if tile["kind"] == "sample":
                    src = (st_h if hg else st_g)[j, ch].rearrange("h k v -> k h v")
                    S.op("pool", lambda e: e.dma_start(
                        out=ST[:, l, :].rearrange("p (h v) -> p h v", h=(8 if hg else 4)), in_=src),
                        writes=[STb[l]], dsem=st_io[l])
                elif tile["first"] and ch == 0:
                    S.op("pool", lambda e: e.memset(ST[:, l, :], 0.0), writes=[STb[l]])
                sbuf = 40 + 2 * (ch % 3)
                S.op("act", lambda e: acopy(e, out=HA[:, sbuf:sbuf + 2, :].rearrange("p a c -> p (a c)"),
                                             in_=ST[:, l, :]),
                     reads=[STb[l]], writes=Hb[sbuf:sbuf + 2])

                def fu(e):
                    for u in range(8):
                        kh = kh_of(u)
                        ins = e.scalar_tensor_tensor(
                            out=ST[:, l, u * 128:(u + 1) * 128], in0=ST[:, l, u * 128:(u + 1) * 128],
                            scalar=EBL[:, kh, ch:ch + 1],
                            in1=PS[:, b2[0 if u < 4 else 1], (u % 4) * 128:(u % 4) * 128 + 128],
                            op0=ALU.mult, op1=ALU.add)
                    return ins
                S.op("dve", fu, reads=[STb[l], eblB, PSb[b2[0]], PSb[b2[1]]], writes=[STb[l]])
                dst = None
                if tile["kind"] == "sample":
                    dst = (sh_s if hg else sg_s)[j, ch]
                elif tile["last"] and ch == nch - 1:
                    dst = (sh_p if hg else sg_p)[j]
                if dst is not None:
                    ob = Buf()
                    outB.append(ob)
                    S.op("pool", lambda e: e.dma_start(
                        out=dst.rearrange("h k v -> k h v"),
                        in_=ST[:, l, :].rearrange("p (h v) -> p h v", h=(8 if hg else 4))),
                        reads=[STb[l]], writes=[ob], dsem=st_io[l])
                return sbuf

            def stage_CB(ch, sbuf):
                b = bank1()
                pt = 50 + ch % 2
                sv = HA[:, sbuf:sbuf + 2, :].rearrange("p a c -> p (a c)")

                def fcb(e):
                    for u in range(8):
                        kh = kh_of(u)
                        e.matmul(PS[:, b, u * 64:(u + 1) * 64], lhsT=sv[:, u * 128:(u + 1) * 128],
                                 rhs=HA[:, kh, ch * 64:(ch + 1) * 64], start=True, stop=False)
                        ins = e.matmul(PS[:, b, u * 64:(u + 1) * 64], lhsT=Fbf(14 + ch, 64)[:, u * 128:(u + 1) * 128],
                                       rhs=HA[0:64, pt, kh * 64:(kh + 1) * 64], start=False, stop=True)
                    return ins
                S.op("pe", fcb, reads=Hb[sbuf:sbuf + 2] + Hb[0:nkh] + [Fb[14 + ch], Hb[pt]], writes=[PSb[b]])
                S.op("act", lambda e: acopy(e, out=FA[:, 0:8, ch * 64:(ch + 1) * 64],
                                             in_=PS[:, b, :].rearrange("p (u t) -> p u t", t=64)),
                     reads=[PSb[b]], writes=Fb[0:8])

            def gate_chunk(s_, ci, u):
                b = fm_chunk(s_, ci, T)
                S.op("act", lambda e: e.activation(out=Fbf(8 + u // 2)[:, (u % 2) * 512:(u % 2) * 512 + T],
                                                   in_=PSv(b, T), func=AF.Silu),
                     reads=[PSb[b]], writes=[Fb[8 + u // 2]])

            def xq_chunk(s_, ci, hh):
                b = fm_chunk(s_, ci, T)
                S.op("dve", lambda e: e.tensor_copy(out=H(36 + hh, T), in_=PSv(b, T)),
                     reads=[PSb[b]], writes=[Hb[36 + hh]])

            extras = []
            for pp in range(4):
                extras.append(("g", pp))
            for pp in range(2):
                extras.append(("xq", pp))

            def run_extra(item):
                kind, pp = item
                s_ = fetch((l, kind, pp))
                for ci in range(2):
                    if kind == "g":
                        gate_chunk(s_, ci, 2 * pp + ci)
                    else:
                        xq_chunk(s_, ci, 2 * pp + ci)

            b2s = {0: stage_A(0)}
            for ch in range(nch):
                if ch + 1 < nch:
                    b2s[ch + 1] = stage_A(ch + 1)
                if extras:
                    run_extra(extras.pop(0))
                sbuf = stage_state(ch, b2s[ch])
                stage_CB(ch, sbuf)
            while extras:
                run_extra(extras.pop(0))

            ncol = (oA_hn + j * 8) if hg else (oB_gn + j * 8)
            NC_ = CA if hg else CB
            Vd = 128 if hg else 256
            nh = 8 if hg else 4
            for hh in range(nh):
                us = [hh] if hg else [2 * hh, 2 * hh + 1]
                for u in us:
                    S.op("act", lambda e, u=u: e.activation(out=H(16 + u, T), in_=Fv(u, T), func=AF.Square),
                         reads=[Fb[u]], writes=[Hb[16 + u]])
                b = bank1()

                def fss(e, us=us, b=b):
                    for n_, u in enumerate(us):
                        ins = e.matmul(PSv(b, T), lhsT=ONES[:], rhs=H(16 + u, T), start=(n_ == 0),
                                       stop=(n_ == len(us) - 1))
                    return ins
                S.op("pe", fss, reads=[Hb[16 + u] for u in us] + [cB], writes=[PSb[b]])
                tf = 12 + hh % 2
                S.op("act", lambda e, b=b, tf=tf: e.activation(out=Fv(tf, T), in_=PSv(b, T), func=AF.Ln,
                                                               scale=1.0 / Vd, bias=CST[:, 2:3]),
                     reads=[PSb[b], cB], writes=[Fb[tf]])
                br = bank1()
                S.op("act", lambda e, br=br, tf=tf: e.activation(out=PSv(br, T), in_=Fv(tf, T), func=AF.Exp,
                                                                 scale=-0.5),
                     reads=[Fb[tf]], writes=[PSb[br]])
                for u in us:
                    S.op("dve", lambda e, u=u, br=br: e.tensor_tensor(out=Fv(u, T), in0=Fv(u, T), in1=PSv(br, T),
                                                                       op=ALU.mult),
                         reads=[Fb[u], PSb[br]], writes=[Fb[u]])
                    S.op("dve", lambda e, u=u: e.scalar_tensor_tensor(
                        out=H(24 + u, T), in0=Fv(u, T), scalar=NC_[:, ncol + u:ncol + u + 1],
                        in1=Fbf(8 + u // 2)[:, (u % 2) * 512:(u % 2) * 512 + T],
                        op0=ALU.mult, op1=ALU.mult),
                        reads=[Fb[u], Fb[8 + u // 2], cB], writes=[Hb[24 + u]])

            if tile["kind"] == "sample":
                ranges = []
                for sq in range(2):
                    fst = 8 + 2 * sq
                    S.op("pool", lambda e, sq=sq, fst=fst: e.dma_start(
                        out=Fwide(fst, 2).rearrange("p (g c) -> p g c", c=512),
                        in_=cmk[l, sq].rearrange("(g p) c -> p g c", p=128)),
                        writes=Fb[fst:fst + 2], dsem=smem_ds[sq])
                    kt0 = 56 + 4 * sq
                    ktv = HA[:, kt0:kt0 + 2, :].rearrange("p a c -> p (a c)").rearrange("p (h m) -> p h m", m=256)
                    for mg in range(2):
                        b = bank1()

                        def ftk(e, fst=fst, mg=mg, b=b):
                            for h in range(4):
                                ins = e.transpose(out=PS[:, b, h * 128:(h + 1) * 128],
                                                  in_=Fwide(fst, 2)[:, mg * 512 + h * 128:mg * 512 + (h + 1) * 128],
                                                  identity=IDF[:])
                            return ins
                        S.op("pe", ftk, reads=Fb[fst:fst + 2] + [cB], writes=[PSb[b]])
                        S.op("dve", lambda e, ktv=ktv, mg=mg, b=b: e.tensor_copy(
                            out=ktv[:, :, mg * 128:(mg + 1) * 128],
                            in_=PS[:, b, :].rearrange("p (h m) -> p h m", m=128)),
                            reads=[PSb[b]], writes=Hb[kt0:kt0 + 2])
                    fsv = 12 + 2 * sq
                    S.op("pool", lambda e, sq=sq, fsv=fsv: e.dma_start(
                        out=Fwide(fsv, 2).rearrange("p (g c) -> p g c", c=512),
                        in_=cmv[l, sq].rearrange("(g p) c -> p g c", p=128)),
                        writes=Fb[fsv:fsv + 2], dsem=smem_ds[2 + sq])
                    vv = HA[:, kt0 + 2:kt0 + 4, :].rearrange("p a c -> p (a c)").rearrange("p (g c) -> p g c", c=512)
                    S.op("dve", lambda e, vv=vv, fsv=fsv: e.tensor_copy(
                        out=vv, in_=Fwide(fsv, 2).rearrange("p (g c) -> p g c", c=512)),
                        reads=Fb[fsv:fsv + 2], writes=Hb[kt0 + 2:kt0 + 4])
                    ranges.append((sq * 64, 64, ktv, Hb[kt0:kt0 + 2], vv, Hb[kt0 + 2:kt0 + 4]))
            else:
                ranges = [(0, T, KTP[:, l, :, :], [KTPb[l]], VP[:, l, :, :], [VPb[l]])]
            it = 0
            for hh in range(4):
                for (r0, n, ktv, ktB, vv, vB) in ranges:
                    pbase = 46 + 2 * (it % 2)
                    it += 1
                    for mc in range(2):
                        b = bank1()
                        S.op("pe", lambda e, b=b, mc=mc, ktv=ktv, hh=hh, r0=r0, n=n: e.matmul(
                            PS[:, b, :n], lhsT=ktv[:, hh, mc * 128:(mc + 1) * 128], rhs=HA[:, 36 + hh, r0:r0 + n],
                            start=True, stop=True), reads=ktB + [Hb[36 + hh]], writes=[PSb[b]])
                        S.op("act", lambda e, b=b, mc=mc, pbase=pbase, n=n: e.activation(
                            out=HA[:, pbase + mc, :n], in_=PS[:, b, :n], func=AF.Exp, scale=128.0 ** -0.5),
                            reads=[PSb[b]], writes=[Hb[pbase + mc]])
                    bd = bank1()

                    def fden(e, bd=bd, pbase=pbase, n=n):
                        e.matmul(PS[:, bd, :n], lhsT=ONES[:], rhs=HA[:, pbase, :n], start=True, stop=False)
                        return e.matmul(PS[:, bd, :n], lhsT=ONES[:], rhs=HA[:, pbase + 1, :n], start=False, stop=True)
                    S.op("pe", fden, reads=[Hb[pbase], Hb[pbase + 1], cB], writes=[PSb[bd]])
                    bp = bank1()

                    def fpv(e, bp=bp, pbase=pbase, n=n, vv=vv, hh=hh):
                        e.matmul(PS[:, bp, :n], lhsT=vv[:, 0, hh * 128:(hh + 1) * 128], rhs=HA[:, pbase, :n],
                                 start=True, stop=False)
                        return e.matmul(PS[:, bp, :n], lhsT=vv[:, 1, hh * 128:(hh + 1) * 128],
                                        rhs=HA[:, pbase + 1, :n], start=False, stop=True)
                    S.op("pe", fpv, reads=[Hb[pbase], Hb[pbase + 1]] + vB, writes=[PSb[bp]])
                    tf = 10 + it % 2
                    S.op("act", lambda e, bd=bd, tf=tf, n=n: e.activation(out=Fv(tf, n), in_=PS[:, bd, :n],
                                                                          func=AF.Ln),
                         reads=[PSb[bd]], writes=[Fb[tf]])
                    S.op("act", lambda e, tf=tf, n=n: e.activation(out=Fv(tf, n), in_=Fv(tf, n), func=AF.Exp,
                                                                   scale=-1.0),
                         reads=[Fb[tf]], writes=[Fb[tf]])
                    S.op("dve", lambda e, bp=bp, tf=tf, n=n, hh=hh, r0=r0: e.tensor_tensor(
                        out=HA[:, 32 + hh, r0:r0 + n], in0=PS[:, bp, :n], in1=Fv(tf, n), op=ALU.mult),
                        reads=[PSb[bp], Fb[tf]], writes=[Hb[32 + hh]])

            for jo in range(8):
                s = fetch((l, "out", jo))
                b = bank1()

                def fo(e, s=s, b=b):
                    for kc in range(12):
                        ins = e.matmul(PSv(b, T), lhsT=RG[:, s, kc * 128:(kc + 1) * 128], rhs=H(24 + kc, T),
                                       start=(kc == 0), stop=(kc == 11))
                    return ins
                S.op("pe", fo, reads=[RGb[s]] + Hb[24:36], writes=[PSb[b]])
                S.op("dve", lambda e, jo=jo, b=b: e.scalar_tensor_tensor(
                    out=X[:, jo, :T], in0=X[:, jo, :T], scalar=ALPHA, in1=PSv(b, T), op0=ALU.mult, op1=ALU.add),
                    reads=[Xb[jo], PSb[b]], writes=[Xb[jo]])

        def phase1b(l, tile, kh, bq_, bk_, sg, qscale):
            T = tile["T"]
            nch = T // CH
            j = l // 2
            hg = (l % 2 == 0)
            hi = kh % 6
            st = 8 + 3 * (kh % 2)
            Bt, E1, E2 = st, st + 1, st + 2
            if hg:
                S.op("act", lambda e: e.activation(out=Fv(Bt, T), in_=Fv(hi, T), func=AF.Ln, scale=-1.0,
                                                   bias=CST[:, 1:2]),
                     reads=[Fb[hi], cB], writes=[Fb[Bt]])
            else:
                bz = bank1()
                S.op("pe", lambda e: e.matmul(PSv(bz, T), lhsT=WG2[:, j, kh * 128:(kh + 1) * 128], rhs=GAT[:, :T],
                                              start=True, stop=True),
                     reads=[cB, gatB], writes=[PSb[bz]])
                S.op("act", lambda e: e.activation(out=Fv(Bt, T), in_=PSv(bz, T), func=AF.Exp, scale=-1.0,
                                                   bias=NBG[:, j * 4 + kh:j * 4 + kh + 1]),
                     reads=[PSb[bz], cB], writes=[Fb[Bt]])
                S.op("act", lambda e: e.activation(out=Fv(Bt, T), in_=Fv(Bt, T), func=AF.Ln, bias=CST[:, 1:2]),
                     reads=[Fb[Bt], cB], writes=[Fb[Bt]])
            S.op("dve", lambda e: e.tensor_tensor_scan(out=Fv(Bt, T), data0=RMASK[:, :T], data1=Fv(Bt, T),
                                                       initial=0.0, op0=ALU.mult, op1=ALU.add),
                 reads=[Fb[Bt], cB], writes=[Fb[Bt]])
            if hg:
                S.op("dve", lambda e: e.tensor_scalar(out=Fv(Bt, T), in0=Fv(Bt, T), scalar1=-80.0, scalar2=None,
                                                      op0=ALU.max),
                     reads=[Fb[Bt]], writes=[Fb[Bt]])
            B3 = Fv(Bt, T).rearrange("p (c t) -> p c t", t=64)
            S.op("act", lambda e: e.activation(out=Fv(E1, T), in_=Fv(Bt, T), func=AF.Exp, scale=sg),
                 reads=[Fb[Bt]], writes=[Fb[E1]])
            S.op("act", lambda e: e.activation(out=Fv(E2, T), in_=Fv(Bt, T), func=AF.Exp, scale=-sg),
                 reads=[Fb[Bt]], writes=[Fb[E2]])
            S.op("act", lambda e: e.activation(out=EBL[:, kh, 0:nch], in_=B3[:, :, 63], func=AF.Exp, scale=sg),
                 reads=[Fb[Bt]], writes=[eblB])
            if hg:
                S.op("dve", lambda e: e.scalar_tensor_tensor(out=H(kh, T), in0=H(52 + hi, T), scalar=qscale,
                                                             in1=Fv(E1, T), op0=ALU.mult, op1=ALU.mult),
                     reads=[Hb[52 + hi], Fb[E1]], writes=[Hb[kh]])
                S.op("dve", lambda e: e.tensor_tensor(out=H(8 + kh, T), in0=Fv(hi, T), in1=Fv(E2, T), op=ALU.mult),
                     reads=[Fb[hi], Fb[E2]], writes=[Hb[8 + kh]])
            else:
                S.op("dve", lambda e: e.scalar_tensor_tensor(out=H(kh, T), in0=PSv(bq_, T), scalar=qscale,
                                                             in1=Fv(E1, T), op0=ALU.mult, op1=ALU.mult),
                     reads=[PSb[bq_], Fb[E1]], writes=[Hb[kh]])
                S.op("dve", lambda e: e.tensor_tensor(out=H(8 + kh, T), in0=PSv(bk_, T), in1=Fv(E2, T), op=ALU.mult),
                     reads=[PSb[bk_], Fb[E2]], writes=[Hb[8 + kh]])

            S.op("dve", lambda e: e.tensor_tensor(
                out=H(16 + kh, T).rearrange("p (c t) -> p c t", t=64),
                in0=H(8 + kh, T).rearrange("p (c t) -> p c t", t=64),
                in1=EBL[:, kh, 0:nch].unsqueeze(2).to_broadcast([128, nch, 64]), op=ALU.mult),
                reads=[Hb[8 + kh], eblB], writes=[Hb[16 + kh]])

        tiles = [dict(kind="sample", T=128, tok0=0, first=False, last=False)]
        npt = seq // 512
        for t in range(npt):
            tiles.append(dict(kind="prompt", T=512, tok0=t * 512, first=(t == 0), last=(t == npt - 1)))
        for tile in (tiles if STAGE >= 3 else []):
            T = tile["T"]
            load_x(x_s if tile["kind"] == "sample" else x_p, tile["tok0"], T)
            for l in range(depth if STAGE >= 4 else 0):
                ffn(l, 0, T)
                if STAGE >= 5:
                    layer_norm(l, 0, T)
                if STAGE >= 6:
                    mixer(l, tile)
                    layer_norm(l, 1, T)
                    ffn(l, 1, T)
                    layer_norm(l, 2, T)
            store_y(y_s if tile["kind"] == "sample" else y_p, tile["tok0"], T)

        S.op("pool", lambda e: e.memset(TINY[:, 0:1], 0.0), reads=outB, writes=[tinyB])

        with nc.Block() as block:
            S.emit(block)
    return nc


def run(inputs, depth, seq, ncores):
    nc = build(depth, seq)
    NHL = (depth + 1) // 2
    NGL = depth // 2
    f = lambda a: np.ascontiguousarray(np.asarray(a, dtype=np.float32))
    shared = {}
    for k in ["ffn_w_gate", "ffn_w_up", "ffn_w_down", "ln_gain", "ln_bias", "hgrn_w_in", "hgrn_lb_logits",
              "hgrn_norm", "hgrn_w_out", "gla_w_in", "gla_w_gate2", "gla_b_gate", "gla_norm", "gla_w_out",
              "mem_w_k", "mem_w_v"]:
        shared[k] = f(inputs[k])
    in_maps = []
    for c in range(ncores):
        m = dict(shared)
        m["x_p"] = f(inputs["x_prompt"][c, :seq])
        m["x_s"] = f(np.asarray(inputs["x_sample"])[2 * c:2 * c + 2].reshape(128, D))
        m["mem_p"] = f(inputs["mem_prompt"][c])
        m["cmk"] = f(np.asarray(inputs["cache_mem_k"])[:, 2 * c:2 * c + 2].reshape(depth, 2, MEMT, 512))
        m["cmv"] = f(np.asarray(inputs["cache_mem_v"])[:, 2 * c:2 * c + 2].reshape(depth, 2, MEMT, 512))
        m["st_h"] = f(np.asarray(inputs["state_hgrn"])[:, 2 * c:2 * c + 2])
        m["st_g"] = f(np.asarray(inputs["state_gla"])[:, 2 * c:2 * c + 2])
        in_maps.append(m)
    res = run_bass_kernel_spmd(nc, in_maps, core_ids=list(range(ncores)))
    R = res.results
    y_prompt = np.stack([R[c]["y_p"] for c in range(ncores)], 0)
    y_sample = np.concatenate([R[c]["y_s"].reshape(2, 64, D) for c in range(ncores)], 0)
    sh_p = np.stack([R[c]["sh_p"] for c in range(ncores)], 1)
    sg_p = np.stack([R[c]["sg_p"] for c in range(ncores)], 1)[:NGL]
    mk = np.stack([R[c]["mk_p"] for c in range(ncores)], 1).reshape(depth, ncores, MEMT, 4, 128)
    mv = np.stack([R[c]["mv_p"] for c in range(ncores)], 1).reshape(depth, ncores, MEMT, 4, 128)
    sh_s = np.concatenate([R[c]["sh_s"] for c in range(ncores)], 1)
    sg_s = np.concatenate([R[c]["sg_s"] for c in range(ncores)], 1)[:NGL]
    return tuple(np.ascontiguousarray(a, dtype=np.float32) for a in
                 (y_prompt, y_sample, sh_p, sg_p, mk, mv, sh_s, sg_s))


def kernel(**inputs):
    return run(inputs, 4, 8192, 8)
```

```python
import os
import numpy as np
from contextlib import ExitStack
import concourse.bass as bass
import concourse.mybir as mybir
from concourse.bass_utils import run_bass_kernel_spmd

F32 = mybir.dt.float32
BF16 = mybir.dt.bfloat16
AF = mybir.ActivationFunctionType
ALU = mybir.AluOpType

D = 1024
DFF = 2816
NFF = 22
CH = 64
MEMT = 256
ALPHA = (2.0 * 4) ** 0.25
LN_EPS = 1e-5
RMS_EPS = 1e-6
GATE_CLAMP = 1.0 - 1e-6
HG_IN = 4608
GL_IN = 3600
NSLOT = 6
SLOTC = 2048
NF = 24
NH_ = 64
SEM_LIM = 16000
STAGE = int(os.environ.get("KSTAGE", "9"))
SUB = int(os.environ.get("KSUB", "9"))


def acopy(e, out, in_):
    return e.activation(out=out, in_=in_, func=AF.Identity)


class Buf:
    __slots__ = ("w", "r", "x")

    def __init__(self, x=False):
        self.w = {}
        self.r = {}
        self.x = x


class Sched:
    def __init__(self, nc, es):
        self.nc = nc
        self.es = es
        self.names = ["pe", "act", "dve", "pool", "sp"]
        self.q = {k: [] for k in self.names}
        self.cnt = {k: 0 for k in self.names}
        self.semlist = {k: [] for k in self.names}
        self.waited = {k: {} for k in self.names}
        self.dsems = []

    def _sem(self, name):
        return self.es.enter_context(self.nc.semaphore(name))

    def new_dsem(self):
        self.dsems.append([self._sem("d%d" % len(self.dsems)), 0])
        return len(self.dsems) - 1

    def resolve(self, key, val):
        if isinstance(key, tuple):
            return self.dsems[key[1]][0], val
        k = (val - 1) // SEM_LIM
        return self.semlist[key][k], (val - 1) % SEM_LIM + 1

    def op(self, eng, fn, reads=(), writes=(), dsem=None, ndma=1):
        deps = {}
        for b in reads:
            for key, v in b.w.items():
                if deps.get(key, 0) < v:
                    deps[key] = v
            if b.x:
                for key, v in b.r.items():
                    if key != eng and deps.get(key, 0) < v:
                        deps[key] = v
        for b in writes:
            for key, v in b.w.items():
                if deps.get(key, 0) < v:
                    deps[key] = v
            for key, v in b.r.items():
                if deps.get(key, 0) < v:
                    deps[key] = v
        wd = self.waited[eng]
        waits = []
        for key, v in deps.items():
            if key == "pe" and eng == "pe":
                continue
            if wd.get(key, 0) < v:
                wd[key] = v
                waits.append((key, v))
        if dsem is None:
            self.cnt[eng] += 1
            val = self.cnt[eng]
            key = eng
            k = (val - 1) // SEM_LIM
            while len(self.semlist[eng]) <= k:
                self.semlist[eng].append(self._sem("%s%d" % (eng, len(self.semlist[eng]))))
            inc = 1
        else:
            d = self.dsems[dsem]
            d[1] += 16 * ndma
            val = d[1]
            key = ("d", dsem)
            inc = 16
        self.q[eng].append((fn, waits, key, val, inc))
        for b in reads:
            if b.r.get(key, 0) < val:
                b.r[key] = val
        for b in writes:
            b.w = {key: val}
            b.r = {}
        return (key, val)

    def emit(self, block):
        bn = {"pe": "tensor", "act": "scalar", "dve": "vector", "pool": "gpsimd", "sp": "sync"}
        for name in self.names:
            ops = self.q[name]

            def body(e, ops=ops):
                for fn, waits, key, val, inc in ops:
                    for k, v in waits:
                        sem, sv = self.resolve(k, v)
                        e.wait_ge(sem, sv)
                    ins = fn(e)
                    sem, _ = self.resolve(key, val)
                    if isinstance(ins, list):
                        for i_ in ins:
                            i_.then_inc(sem, inc)
                    else:
                        ins.then_inc(sem, inc)

            getattr(block, bn[name])(body)


def build(depth, seq):
    NHL = (depth + 1) // 2
    NGL = depth // 2
    assert NHL <= 2
    nc = bass.Bass("TRN2", target_bir_lowering=False)

    def din(name, shape):
        return nc.dram_tensor(name, list(shape), F32, kind="ExternalInput").ap()

    def dout(name, shape):
        return nc.dram_tensor(name, list(shape), F32, kind="ExternalOutput").ap()

    x_p = din("x_p", [seq, D])
    x_s = din("x_s", [128, D])
    mem_p = din("mem_p", [MEMT, D])
    cmk = din("cmk", [depth, 2, MEMT, 512])
    cmv = din("cmv", [depth, 2, MEMT, 512])
    st_h = din("st_h", [NHL, 2, 8, 128, 128])
    st_g = din("st_g", [max(NGL, 1), 2, 4, 128, 256])
    w_gate = din("ffn_w_gate", [depth, 2, D, DFF])
    w_up = din("ffn_w_up", [depth, 2, D, DFF])
    w_down = din("ffn_w_down", [depth, 2, DFF, D])
    ln_gain = din("ln_gain", [depth, 3, D])
    ln_bias = din("ln_bias", [depth, 3, D])
    hg_w_in = din("hgrn_w_in", [NHL, D, HG_IN])
    hg_lb = din("hgrn_lb_logits", [NHL, D])
    hg_norm = din("hgrn_norm", [NHL, D])
    hg_w_out = din("hgrn_w_out", [NHL, 1536, D])
    gl_w_in = din("gla_w_in", [max(NGL, 1), D, GL_IN])
    gl_wg2 = din("gla_w_gate2", [max(NGL, 1), 16, 512])
    gl_bg = din("gla_b_gate", [max(NGL, 1), 512])
    gl_norm = din("gla_norm", [max(NGL, 1), D])
    gl_w_out = din("gla_w_out", [max(NGL, 1), 1536, D])
    mem_wk = din("mem_w_k", [depth, D, 512])
    mem_wv = din("mem_w_v", [depth, D, 512])

    y_p = dout("y_p", [seq, D])
    y_s = dout("y_s", [128, D])
    sh_p = dout("sh_p", [NHL, 8, 128, 128])
    sg_p = dout("sg_p", [max(NGL, 1), 4, 128, 256])
    mk_p = dout("mk_p", [depth, MEMT, 512])
    mv_p = dout("mv_p", [depth, MEMT, 512])
    sh_s = dout("sh_s", [NHL, 2, 8, 128, 128])
    sg_s = dout("sg_s", [max(NGL, 1), 2, 4, 128, 256])

    pieces = []
    pidx = {}

    def reg(name, ncols, subs):
        pidx[name] = len(pieces)
        pieces.append((ncols, subs))

    for l in range(depth):
        j = l // 2
        hg = (l % 2 == 0)

        def reg_ffn(i):
            for c in range(NFF):
                reg((l, "gu", i, c), 2048, [(w_gate[l, i], 0, 8, c * 128, 128, 0),
                                            (w_up[l, i], 0, 8, c * 128, 128, 1024)])
            for jo in range(8):
                for hh in range(2):
                    reg((l, "dn", i, jo, hh), 1408, [(w_down[l, i], hh * 11, 11, jo * 128, 128, 0)])

        reg_ffn(0)
        if hg:
            W = hg_w_in[j]
            for pp in range(4):
                reg((l, "q", pp), 2048, [(W, 0, 8, (2 * pp + ci) * 128, 128, ci * 1024) for ci in range(2)])
                reg((l, "f", pp), 2048, [(W, 0, 8, 1024 + (2 * pp + ci) * 128, 128, ci * 1024) for ci in range(2)])
            for vp in range(4):
                reg((l, "v", vp), 2048, [(W, 0, 8, 2048 + vp * 256, 256, 0)])
            for pp in range(4):
                reg((l, "g", pp), 2048, [(W, 0, 8, 3072 + (2 * pp + ci) * 128, 128, ci * 1024) for ci in range(2)])
            for pp in range(2):
                reg((l, "xq", pp), 2048, [(W, 0, 8, 4096 + (2 * pp + ci) * 128, 128, ci * 1024) for ci in range(2)])
            WO = hg_w_out[j]
        else:
            W = gl_w_in[j]
            reg((l, "ga"), 128, [(W, 0, 8, 3072, 16, 0)])
            for pp in range(2):
                reg((l, "q", pp), 2048, [(W, 0, 8, (2 * pp + ci) * 128, 128, ci * 1024) for ci in range(2)])
                reg((l, "f", pp), 2048, [(W, 0, 8, 512 + (2 * pp + ci) * 128, 128, ci * 1024) for ci in range(2)])
            for vp in range(4):
                reg((l, "v", vp), 2048, [(W, 0, 8, 1024 + vp * 256, 256, 0)])
            for pp in range(4):
                reg((l, "g", pp), 2048, [(W, 0, 8, 2048 + (2 * pp + ci) * 128, 128, ci * 1024) for ci in range(2)])
            for pp in range(2):
                reg((l, "xq", pp), 2048, [(W, 0, 8, 3088 + (2 * pp + ci) * 128, 128, ci * 1024) for ci in range(2)])
            WO = gl_w_out[j]
        for jo in range(8):
            reg((l, "out", jo), 1536, [(WO, 0, 12, jo * 128, 128, 0)])
        reg_ffn(1)
    NP = len(pieces)
    wscr = nc.dram_tensor("wscr", [NP, 128, SLOTC], BF16, kind="Internal").ap()

    es = ExitStack()
    with es:
        S = Sched(nc, es)

        def sb(name, shape, dt):
            return es.enter_context(nc.sbuf_tensor(name, list(shape), dt))

        RG = sb("ring", [128, NSLOT, SLOTC], BF16)
        X = sb("X", [128, 8, 512], F32)
        XB = sb("XB", [128, 8, 512], BF16)
        FA = sb("FA", [128, NF, 512], F32)
        HA = sb("HA", [128, NH_, 512], BF16)
        ST = sb("ST", [128, depth, 1024], F32)
        KTP = sb("KTP", [128, depth, 4, 256], BF16)
        VP = sb("VP", [128, depth, 2, 512], BF16)
        IDF = sb("identf", [128, 128], F32)
        IDB = sb("identb", [128, 128], BF16)
        ONES = sb("onesb", [128, 128], BF16)
        MASK = sb("mask", [64, 512], F32)
        RMASK = sb("rmask", [128, 512], F32)
        CST = sb("cst", [128, 4], F32)
        CA = sb("constA", [128, 128], F32)
        CB = sb("constB", [128, 128], F32)
        PST = sb("pstage", [128, 128], F32)
        HGP = sb("hgp", [128, 2, 2, 8], F32)
        NBG = sb("nbg", [128, 8], F32)
        WG2 = sb("wg2", [16, 2, 512], BF16)
        WG2F = sb("wg2f", [16, 2, 512], F32)
        GAT = sb("gaT", [16, 512], BF16)
        EBL = sb("ebl", [128, 8, 8], F32)
        TINY = sb("tiny", [128, 16], F32)
        PS = es.enter_context(nc.psum_tensor("ps", [128, 8, 512], F32))

        RGb = [Buf() for _ in range(NSLOT)]
        Xb = [Buf() for _ in range(8)]
        XBb = [Buf() for _ in range(8)]
        Fb = [Buf() for _ in range(NF)]
        Hb = [Buf() for _ in range(NH_)]
        STb = [Buf() for _ in range(depth)]
        KTPb = [Buf() for _ in range(depth)]
        VPb = [Buf() for _ in range(depth)]
        PSb = [Buf(True) for _ in range(8)]
        WSb = [Buf() for _ in range(NP)]
        cB = Buf()
        gatB = Buf()
        eblB = Buf()
        tinyB = Buf()
        outB = []

        ring_ds = [S.new_dsem() for _ in range(NSLOT)]

        def H(i, T=512):
            return HA[:, i, :T]

        def Fv(i, T=512):
            return FA[:, i, :T]

        def Fbf(i, rows=128):
            return FA[0:rows, i, :].bitcast(BF16)

        def Fwide(i, n):
            return FA[:, i:i + n, :].rearrange("p a c -> p (a c)")

        def Fwide_bf(i, n):
            return FA[:, i:i + n, :].bitcast(BF16).rearrange("p a c -> p (a c)")

        rot = {"b1": 0, "b2": 0}

        def bank1():
            b = rot["b1"] % 4
            rot["b1"] += 1
            return b

        def bank2():
            b = 4 + 2 * (rot["b2"] % 2)
            rot["b2"] += 1
            return b

        def PSv(b, T=512, rows=128):
            return PS[0:rows, b, :T]

        def PS2(b, rows=128):
            return PS[0:rows, b:b + 2, :].rearrange("p a c -> p (a c)")

        def PSbf(b, rows=128):
            return PS[0:rows, b, :].bitcast(BF16)

        def c_init(e):
            e.memset(CST[:, 0:1], LN_EPS)
            e.memset(CST[:, 1:2], 1.0)
            e.memset(CST[:, 2:3], RMS_EPS)
            e.memset(CST[:, 3:4], 0.0)
            e.memset(ONES[:], 1.0)
            e.memset(IDF[:], 1.0)
            e.memset(MASK[:], 1.0)
            e.memset(HGP[:, 0, :, :], 0.5)
            e.memset(HGP[:, 1, :, :], -0.5)
            return e.memset(RMASK[:], 1.0)

        S.op("pool", c_init, writes=[cB])
        S.op("pool", lambda e: e.affine_select(out=IDF[:], in_=IDF[:], pattern=[[1, 128]], compare_op=ALU.is_equal,
                                               fill=0.0, base=0, channel_multiplier=-1), reads=[cB], writes=[cB])
        S.op("pool", lambda e: e.affine_select(out=MASK[:], in_=MASK[:], pattern=[[0, 8], [1, 64]],
                                               compare_op=ALU.is_ge, fill=0.0, base=0, channel_multiplier=-1),
             reads=[cB], writes=[cB])
        S.op("pool", lambda e: e.memset(RMASK[:].rearrange("p (c t) -> p c t", t=64)[:, :, 0:1], 0.0),
             reads=[cB], writes=[cB])
        S.op("dve", lambda e: e.tensor_copy(out=IDB[:], in_=IDF[:]), reads=[cB], writes=[cB])

        nln = depth * 3 * 8
        oA_lb = nln
        oA_hn = nln + NHL * 8
        oB_bg = nln
        oB_gn = nln + NGL * 4
        io_ds = [S.new_dsem() for _ in range(4)]

        def load_rows(dst, r0, src2d, nrows, ds):
            S.op("pool", lambda e: e.dma_start(out=dst[r0:r0 + nrows, :], in_=src2d), writes=[tinyB], dsem=ds)

        S.op("pool", lambda e: e.memset(PST[:], 0.0), writes=[tinyB])
        load_rows(PST, 0, ln_gain.rearrange("l i (c p) -> (l i c) p", p=128), nln, io_ds[0])
        load_rows(PST, oA_lb, hg_lb.rearrange("j (c p) -> (j c) p", p=128), NHL * 8, io_ds[1])
        load_rows(PST, oA_hn, hg_norm.rearrange("j (c p) -> (j c) p", p=128), NHL * 8, io_ds[2])
        S.op("pe", lambda e: e.transpose(out=PS[:, 0, 0:128], in_=PST[:], identity=IDF[:]),
             reads=[tinyB, cB], writes=[PSb[0]])
        S.op("dve", lambda e: e.tensor_copy(out=CA[:], in_=PS[:, 0, 0:128]), reads=[PSb[0]], writes=[cB])
        S.op("pool", lambda e: e.memset(PST[:], 0.0), writes=[tinyB])
        load_rows(PST, 0, ln_bias.rearrange("l i (c p) -> (l i c) p", p=128), nln, io_ds[0])
        if NGL:
            load_rows(PST, oB_bg, gl_bg.rearrange("j (c p) -> (j c) p", p=128), NGL * 4, io_ds[1])
            load_rows(PST, oB_gn, gl_norm.rearrange("j (c p) -> (j c) p", p=128), NGL * 8, io_ds[2])
        S.op("pe", lambda e: e.transpose(out=PS[:, 1, 0:128], in_=PST[:], identity=IDF[:]),
             reads=[tinyB, cB], writes=[PSb[1]])
        S.op("dve", lambda e: e.tensor_copy(out=CB[:], in_=PS[:, 1, 0:128]), reads=[PSb[1]], writes=[cB])
        if NHL == 2:
            S.op("dve", lambda e: e.tensor_tensor(out=TINY[:, 0:8], in0=CA[:, oA_lb + 8:oA_lb + 16],
                                                  in1=CA[:, oA_lb:oA_lb + 8], op=ALU.subtract),
                 reads=[cB], writes=[tinyB])
            S.op("act", lambda e: e.activation(out=TINY[:, 0:8], in_=TINY[:, 0:8], func=AF.Exp),
                 reads=[tinyB], writes=[tinyB])
            S.op("dve", lambda e: e.tensor_scalar(out=TINY[:, 0:8], in0=TINY[:, 0:8], scalar1=1.0, scalar2=None,
                                                  op0=ALU.add), reads=[tinyB], writes=[tinyB])
            S.op("dve", lambda e: e.reciprocal(out=TINY[:, 8:16], in_=TINY[:, 0:8]), reads=[tinyB], writes=[tinyB])
            S.op("dve", lambda e: e.tensor_scalar(out=HGP[:, 0, 1, :], in0=TINY[:, 8:16], scalar1=0.5, scalar2=None,
                                                  op0=ALU.mult), reads=[tinyB, cB], writes=[cB])
            S.op("dve", lambda e: e.tensor_scalar(out=HGP[:, 1, 1, :], in0=TINY[:, 8:16], scalar1=-0.5, scalar2=None,
                                                  op0=ALU.mult), reads=[tinyB, cB], writes=[cB])
        if NGL:
            S.op("dve", lambda e: e.tensor_scalar(out=NBG[:, 0:NGL * 4], in0=CB[:, oB_bg:oB_bg + NGL * 4],
                                                  scalar1=-1.0, scalar2=None, op0=ALU.mult), reads=[cB], writes=[cB])
            for j in range(NGL):
                S.op("pool", lambda e, j=j: e.dma_start(out=WG2F[:, j, :], in_=gl_wg2[j]), writes=[tinyB],
                     dsem=io_ds[3])
            S.op("dve", lambda e: e.tensor_copy(out=WG2[:, 0:NGL, :], in_=WG2F[:, 0:NGL, :]), reads=[tinyB],
                 writes=[cB])

        st_ds = [S.new_dsem() for _ in range(3)]
        sf_ds = [[S.new_dsem() for _ in range(2)] for _ in range(3)]
        cast_eng = ["dve", "act", "pool"]

        def conv_load(p):
            k = p % 3
            ncols, subs = pieces[p]
            sf = Fwide(4 * k, 4)

            def fn(e):
                out = []
                for (src, kc0, nkc, col0, w, off) in subs:
                    out.append(e.dma_start(
                        out=sf[:, off:off + nkc * w].rearrange("p (k c) -> p k c", c=w),
                        in_=src[kc0 * 128:(kc0 + nkc) * 128, col0:col0 + w].rearrange("(k p) c -> p k c", p=128)))
                return out
            S.op("sp", fn, writes=Fb[4 * k:4 * k + 4], dsem=sf_ds[k][0], ndma=len(subs))

        def conv_cast_store(p):
            k = p % 3
            ncols, subs = pieces[p]
            sf = Fwide(4 * k, 4)
            sbf = Fwide_bf(12 + 2 * k, 2)
            eng = cast_eng[p % 3]

            def fc(e):
                if eng == "act":
                    return acopy(e, out=sbf[:, :ncols], in_=sf[:, :ncols])
                return e.tensor_copy(out=sbf[:, :ncols], in_=sf[:, :ncols])
            S.op(eng, fc, reads=Fb[4 * k:4 * k + 4], writes=Fb[12 + 2 * k:14 + 2 * k])
            S.op("sp", lambda e: e.dma_start(out=wscr[p, :, :ncols], in_=sbf[:, :ncols]),
                 reads=Fb[12 + 2 * k:14 + 2 * k], writes=[WSb[p]], dsem=st_ds[k])

        for p in range((NP + 2) if STAGE >= 1 else 0):
            if p < NP:
                conv_load(p)
            if p >= 2:
                conv_cast_store(p - 2)

        rstate = {"n": 0}

        def fetch(name):
            p = pidx[name]
            ncols = pieces[p][0]
            s = rstate["n"] % NSLOT
            rstate["n"] += 1
            S.op("sp", lambda e: e.dma_start(out=RG[:, s, :ncols], in_=wscr[p, :, :ncols]),
                 reads=[WSb[p]], writes=[RGb[s]], dsem=ring_ds[s])
            return s

        mem_ds = [S.new_dsem() for _ in range(4)]
        memT = Fwide_bf(4, 2).rearrange("p (k m) -> p k m", m=256)
        for g in range(2 if STAGE >= 2 else 0):
            S.op("pool", lambda e, g=g: e.dma_start(out=Fwide(2 * g, 2), in_=mem_p[g * 128:(g + 1) * 128, :]),
                 writes=Fb[2 * g:2 * g + 2], dsem=mem_ds[g])
            for half in range(2):
                b = bank1()

                def ft(e, g=g, half=half, b=b):
                    for k in range(4):
                        c = half * 4 + k
                        ins = e.transpose(out=PS[:, b, k * 128:(k + 1) * 128],
                                          in_=Fwide(2 * g, 2)[:, c * 128:(c + 1) * 128], identity=IDF[:])
                    return ins
                S.op("pe", ft, reads=Fb[2 * g:2 * g + 2] + [cB], writes=[PSb[b]])
                S.op("dve", lambda e, g=g, half=half, b=b: e.tensor_copy(
                    out=memT[:, half * 4:half * 4 + 4, g * 128:(g + 1) * 128],
                    in_=PS[:, b, :].rearrange("p (k m) -> p k m", m=128)),
                    reads=[PSb[b]], writes=Fb[4:6])
        wst = Fwide(6, 8).rearrange("p (k c) -> p k c", c=512)
        wbf = Fwide_bf(14, 4).rearrange("p (k c) -> p k c", c=512)
        for l in range(depth if (STAGE >= 2 and SUB >= 2) else 0):
            for kv in range(2):
                wsrc = (mem_wk if kv == 0 else mem_wv)[l]
                odst = (mk_p if kv == 0 else mv_p)[l]
                S.op("sp", lambda e, wsrc=wsrc: e.dma_start(out=wst, in_=wsrc.rearrange("(k p) c -> p k c", p=128)),
                     writes=Fb[6:14], dsem=mem_ds[2])
                S.op("act", lambda e: acopy(e, out=wbf, in_=wst), reads=Fb[6:14], writes=Fb[14:18])
                for mg in range(2 if SUB >= 3 else 0):
                    b = bank1()

                    def fm(e, mg=mg, b=b):
                        for kc in range(8):
                            ins = e.matmul(PS[:, b, :], lhsT=memT[:, kc, mg * 128:(mg + 1) * 128], rhs=wbf[:, kc, :],
                                           start=(kc == 0), stop=(kc == 7))
                        return ins
                    S.op("pe", fm, reads=Fb[4:6] + Fb[14:18], writes=[PSb[b]])
                    S.op("dve", lambda e, b=b: e.tensor_copy(out=Fv(18), in_=PS[:, b, :]), reads=[PSb[b]],
                         writes=[Fb[18]])
                    if kv == 1:
                        S.op("act", lambda e, b=b, l=l, mg=mg: acopy(e, out=VP[:, l, mg, :], in_=PS[:, b, :]),
                             reads=[PSb[b]], writes=[VPb[l]])
                    if SUB < 4:
                        continue
                    ob = Buf()
                    outB.append(ob)
                    S.op("pool", lambda e, odst=odst, mg=mg: e.dma_start(out=odst[mg * 128:(mg + 1) * 128, :],
                                                                        in_=Fv(18)),
                         reads=[Fb[18]], writes=[ob], dsem=mem_ds[3])
                if kv == 0 and SUB >= 5:
                    for h in range(4):
                        b = bank1()

                        def fk(e, h=h, b=b):
                            for kc in range(8):
                                ins = e.matmul(PS[:, b, 0:256], lhsT=wbf[:, kc, h * 128:(h + 1) * 128],
                                               rhs=memT[:, kc, :], start=(kc == 0), stop=(kc == 7))
                            return ins
                        S.op("pe", fk, reads=Fb[4:6] + Fb[14:18], writes=[PSb[b]])
                        S.op("act", lambda e, b=b, l=l, h=h: acopy(e, out=KTP[:, l, h, :], in_=PS[:, b, 0:256]),
                             reads=[PSb[b]], writes=[KTPb[l]])

        xs_ds = [S.new_dsem() for _ in range(2)]
        ys_ds = [S.new_dsem() for _ in range(2)]
        st_io = [S.new_dsem() for _ in range(depth)]
        smem_ds = [S.new_dsem() for _ in range(4)]

        def load_x(src, tok0, T):
            for g in range(T // 128):
                f0 = 8 + 2 * (g % 2)
                S.op("pool", lambda e, g=g, f0=f0: e.dma_start(out=Fwide(f0, 2),
                                                              in_=src[tok0 + g * 128:tok0 + (g + 1) * 128, :]),
                     writes=Fb[f0:f0 + 2], dsem=xs_ds[g % 2])
                for half in range(2):
                    b = bank1()

                    def ft(e, f0=f0, half=half, b=b):
                        for k in range(4):
                            c = half * 4 + k
                            ins = e.transpose(out=PS[:, b, k * 128:(k + 1) * 128],
                                              in_=Fwide(f0, 2)[:, c * 128:(c + 1) * 128], identity=IDF[:])
                        return ins
                    S.op("pe", ft, reads=Fb[f0:f0 + 2] + [cB], writes=[PSb[b]])
                    pv = PS[:, b, :].rearrange("p (k m) -> p k m", m=128)
                    S.op("act", lambda e, g=g, half=half, pv=pv: acopy(e,
                        out=X[:, half * 4:half * 4 + 4, g * 128:(g + 1) * 128], in_=pv),
                        reads=[PSb[b]], writes=Xb[half * 4:half * 4 + 4])
                    S.op("dve", lambda e, g=g, half=half, pv=pv: e.tensor_copy(
                        out=XB[:, half * 4:half * 4 + 4, g * 128:(g + 1) * 128], in_=pv),
                        reads=[PSb[b]], writes=XBb[half * 4:half * 4 + 4])

        def store_y(dst, tok0, T):
            for g in range(T // 128):
                f0 = 12 + 2 * (g % 2)
                for half in range(2):
                    b = bank1()

                    def ft(e, g=g, half=half, b=b):
                        for k in range(4):
                            c = half * 4 + k
                            ins = e.transpose(out=PS[:, b, k * 128:(k + 1) * 128],
                                              in_=X[:, c, g * 128:(g + 1) * 128], identity=IDF[:])
                        return ins
                    S.op("pe", ft, reads=Xb[half * 4:half * 4 + 4] + [cB], writes=[PSb[b]])
                    S.op("act" if half == 0 else "dve",
                         (lambda e, f0=f0, half=half, b=b: acopy(e, out=Fv(f0 + half), in_=PS[:, b, :])) if half == 0 else
                         (lambda e, f0=f0, half=half, b=b: e.tensor_copy(out=Fv(f0 + half), in_=PS[:, b, :])),
                         reads=[PSb[b]], writes=[Fb[f0 + half]])
                ob = Buf()
                outB.append(ob)
                S.op("pool", lambda e, g=g, f0=f0: e.dma_start(out=dst[tok0 + g * 128:tok0 + (g + 1) * 128, :],
                                                              in_=Fwide(f0, 2)),
                     reads=Fb[f0:f0 + 2], writes=[ob], dsem=ys_ds[g % 2])

        def layer_norm(l, i, T):
            col0 = l * 24 + i * 8
            for c in range(8):
                S.op("act", lambda e, c=c: acopy(e, out=XB[:, c, :T], in_=X[:, c, :T]), reads=[Xb[c]], writes=[XBb[c]])
                S.op("act", lambda e, c=c: e.activation(out=H(25 + c, T), in_=X[:, c, :T], func=AF.Square),
                     reads=[Xb[c]], writes=[Hb[25 + c]])
            bs = 4
            bq = 5

            def fs(e):
                for c in range(8):
                    ins = e.matmul(PSv(bs, T), lhsT=ONES[:], rhs=XB[:, c, :T], start=(c == 0), stop=(c == 7))
                return ins
            S.op("pe", fs, reads=XBb + [cB], writes=[PSb[bs]])

            def fq(e):
                for c in range(8):
                    ins = e.matmul(PSv(bq, T), lhsT=ONES[:], rhs=H(25 + c, T), start=(c == 0), stop=(c == 7))
                return ins
            S.op("pe", fq, reads=Hb[25:33] + [cB], writes=[PSb[bq]])
            S.op("act", lambda e: e.activation(out=Fv(16, T), in_=PSv(bs, T), func=AF.Square, scale=1.0 / D),
                 reads=[PSb[bs]], writes=[Fb[16]])
            S.op("dve", lambda e: e.scalar_tensor_tensor(out=Fv(17, T), in0=PSv(bq, T), scalar=1.0 / D, in1=Fv(16, T),
                                                         op0=ALU.mult, op1=ALU.subtract),
                 reads=[PSb[bq], Fb[16]], writes=[Fb[17]])
            S.op("act", lambda e: e.activation(out=Fv(17, T), in_=Fv(17, T), func=AF.Ln, bias=CST[:, 0:1]),
                 reads=[Fb[17], cB], writes=[Fb[17]])
            br = 6
            S.op("act", lambda e: e.activation(out=PSv(br, T), in_=Fv(17, T), func=AF.Exp, scale=-0.5),
                 reads=[Fb[17]], writes=[PSb[br]])
            for c in range(8):
                tf = 8 + c
                S.op("dve", lambda e, c=c, tf=tf: e.scalar_tensor_tensor(
                    out=Fv(tf, T), in0=PSv(bs, T), scalar=-1.0 / D, in1=X[:, c, :T], op0=ALU.mult, op1=ALU.add),
                    reads=[PSb[bs], Xb[c]], writes=[Fb[tf]])
                S.op("dve", lambda e, tf=tf: e.tensor_tensor(out=Fv(tf, T), in0=Fv(tf, T), in1=PSv(br, T),
                                                             op=ALU.mult),
                     reads=[Fb[tf], PSb[br]], writes=[Fb[tf]])
                S.op("act", lambda e, c=c, tf=tf: e.activation(
                    out=XB[:, c, :T], in_=Fv(tf, T), func=AF.Identity,
                    scale=CA[:, col0 + c:col0 + c + 1], bias=CB[:, col0 + c:col0 + c + 1]),
                    reads=[Fb[tf], cB], writes=[XBb[c]])
            for c in range(8):
                tf = 8 + c
                S.op("act", lambda e, c=c, tf=tf: e.activation(
                    out=X[:, c, :T], in_=Fv(tf, T), func=AF.Identity,
                    scale=CA[:, col0 + c:col0 + c + 1], bias=CB[:, col0 + c:col0 + c + 1]),
                    reads=[Fb[tf], cB], writes=[Xb[c]])

        def ffn(l, i, T):
            def evac(c, bg, bu):
                hs = 22 + (c % 3)
                S.op("act", lambda e: e.activation(out=H(hs, T), in_=PSv(bg, T), func=AF.Silu),
                     reads=[PSb[bg]], writes=[Hb[hs]])
                S.op("dve", lambda e: e.scalar_tensor_tensor(
                    out=H(c, T), in0=PSv(bu, T), scalar=0.5, in1=H(hs, T), op0=ALU.mult, op1=ALU.mult),
                    reads=[PSb[bu], Hb[hs]], writes=[Hb[c]])

            s01 = [fetch((l, "gu", i, 0)), fetch((l, "gu", i, 1))]
            b01 = [(bank1(), bank1()), (bank1(), bank1())]
            for kc in range(8):
                def fk(e, kc=kc):
                    for c in range(2):
                        for gu in range(2):
                            ins = e.matmul(PSv(b01[c][gu], T),
                                           lhsT=RG[:, s01[c], gu * 1024 + kc * 128:gu * 1024 + (kc + 1) * 128],
                                           rhs=XB[:, kc, :T], start=(kc == 0), stop=(kc == 7))
                    return ins
                S.op("pe", fk, reads=[RGb[s01[0]], RGb[s01[1]], XBb[kc]],
                     writes=[PSb[b01[0][0]], PSb[b01[0][1]], PSb[b01[1][0]], PSb[b01[1][1]]])
            evac(0, b01[0][0], b01[0][1])
            evac(1, b01[1][0], b01[1][1])
            for c in range(2, NFF):
                s = fetch((l, "gu", i, c))
                bg = bank1()
                bu = bank1()

                def fm(e, s=s, bg=bg, bu=bu):
                    for kc in range(8):
                        e.matmul(PSv(bg, T), lhsT=RG[:, s, kc * 128:(kc + 1) * 128], rhs=XB[:, kc, :T],
                                 start=(kc == 0), stop=(kc == 7))
                    for kc in range(8):
                        ins = e.matmul(PSv(bu, T), lhsT=RG[:, s, 1024 + kc * 128:1024 + (kc + 1) * 128],
                                       rhs=XB[:, kc, :T], start=(kc == 0), stop=(kc == 7))
                    return ins
                S.op("pe", fm, reads=[RGb[s]] + XBb, writes=[PSb[bg], PSb[bu]])
                evac(c, bg, bu)
            for jo in range(8):
                s0 = fetch((l, "dn", i, jo, 0))
                s1 = fetch((l, "dn", i, jo, 1))
                b = bank1()

                def fd(e, s0=s0, s1=s1, b=b):
                    for kk in range(22):
                        s = s0 if kk < 11 else s1
                        k2 = kk % 11
                        ins = e.matmul(PSv(b, T), lhsT=RG[:, s, k2 * 128:(k2 + 1) * 128], rhs=H(kk, T),
                                       start=(kk == 0), stop=(kk == 21))
                    return ins
                S.op("pe", fd, reads=[RGb[s0], RGb[s1]] + Hb[0:22], writes=[PSb[b]])
                S.op("dve", lambda e, jo=jo, b=b: e.scalar_tensor_tensor(
                    out=X[:, jo, :T], in0=X[:, jo, :T], scalar=ALPHA, in1=PSv(b, T), op0=ALU.mult, op1=ALU.add),
                    reads=[Xb[jo], PSb[b]], writes=[Xb[jo]])

        def fm_chunk(s, ci, T, M=128, kcw=128, base=0):
            b = bank1()

            def fn(e):
                for kc in range(8):
                    o = base + ci * 1024 + kc * kcw
                    ins = e.matmul(PS[0:M, b, :T], lhsT=RG[:, s, o:o + M], rhs=XB[:, kc, :T],
                                   start=(kc == 0), stop=(kc == 7))
                return ins
            S.op("pe", fn, reads=[RGb[s]] + XBb, writes=[PSb[b]])
            return b

        def fm_multi(specs, T):
            banks = [bank1() for _ in specs]
            slots_ = sorted(set(sp[0] for sp in specs))
            for kc in range(8):
                def fn(e, kc=kc):
                    for (s_, off, M, kcw), b in zip(specs, banks):
                        o = off + kc * kcw
                        ins = e.matmul(PS[0:M, b, :T], lhsT=RG[:, s_, o:o + M], rhs=XB[:, kc, :T],
                                       start=(kc == 0), stop=(kc == 7))
                    return ins
                S.op("pe", fn, reads=[RGb[s_] for s_ in slots_] + [XBb[kc]], writes=[PSb[b] for b in banks])
            return banks

        def mixer(l, tile):
            T = tile["T"]
            nch = T // CH
            j = l // 2
            hg = (l % 2 == 0)
            nkh = 8 if hg else 4
            sg = 1.0 if hg else -1.0 / 16.0
            qscale = 128.0 ** -0.5

            def kh_of(u):
                return u if hg else u // 2

            if not hg:
                s = fetch((l, "ga"))
                b = fm_multi([(s, 0, 16, 16)], T)[0]
                S.op("dve", lambda e, b=b: e.tensor_copy(out=GAT[:, :T], in_=PS[0:16, b, :T]), reads=[PSb[b]],
                     writes=[gatB])
            def v_piece(vp):
                sv_ = fetch((l, "v", vp))
                for c0 in range(0, nch, 2):
                    b = bank1()

                    def fv(e, c0=c0, b=b):
                        for cc in range(2):
                            ch = c0 + cc
                            for kc in range(8):
                                ins = e.matmul(PS[0:64, b, cc * 256:(cc + 1) * 256],
                                               lhsT=XB[:, kc, ch * 64:(ch + 1) * 64],
                                               rhs=RG[:, sv_, kc * 256:(kc + 1) * 256],
                                               start=(kc == 0), stop=(kc == 7))
                        return ins
                    S.op("pe", fv, reads=[RGb[sv_]] + XBb, writes=[PSb[b]])
                    S.op("act", lambda e, c0=c0, b=b: acopy(
                        e, out=FA[0:64, 14 + c0:16 + c0, :].bitcast(BF16)[:, :, vp * 256:(vp + 1) * 256],
                        in_=PS[0:64, b, :].rearrange("p (a c) -> p a c", c=256)),
                        reads=[PSb[b]], writes=Fb[14 + c0:16 + c0])

            vdone = 0
            if hg:
                groups = [([0, 1, 2], 3), ([3], 1)]
            else:
                groups = [([0, 1], 0)]
            for pps, nv_after in groups:
                heads = []
                for pp in pps:
                    sq_ = fetch((l, "q", pp))
                    sf_ = fetch((l, "f", pp))
                    pre = {}
                    if pp == 0:
                        ncis = 2 if hg else 1
                        specs = []
                        for ci in range(ncis):
                            specs += [(sq_, ci * 1024, 128, 128), (sf_, ci * 1024, 128, 128)]
                        bks = fm_multi(specs, T)
                        for ci in range(ncis):
                            pre[ci] = (bks[2 * ci], bks[2 * ci + 1])
                    for ci in range(2):
                        kh = 2 * pp + ci
                        heads.append(kh)
                        if ci in pre:
                            bq_, bf_ = pre[ci]
                        else:
                            bq_ = fm_chunk(sq_, ci, T)
                            bf_ = fm_chunk(sf_, ci, T)
                        hi = kh % 6
                        if hg:
                            S.op("act", lambda e, bq_=bq_, hi=hi: e.activation(out=H(52 + hi, T), in_=PSv(bq_, T),
                                                                              func=AF.Silu),
                                 reads=[PSb[bq_]], writes=[Hb[52 + hi]])
                            S.op("act", lambda e, bf_=bf_, hi=hi: e.activation(out=Fv(hi, T), in_=PSv(bf_, T),
                                                                              func=AF.Tanh, scale=0.5),
                                 reads=[PSb[bf_]], writes=[Fb[hi]])
                            S.op("act", lambda e, hi=hi, kh=kh: e.activation(
                                out=Fv(hi, T), in_=Fv(hi, T), func=AF.Identity,
                                scale=HGP[:, 1, j, kh:kh + 1], bias=HGP[:, 0, j, kh:kh + 1]),
                                reads=[Fb[hi], cB], writes=[Fb[hi]])
                            S.op("dve", lambda e, hi=hi: e.tensor_scalar(
                                out=Fv(hi, T), in0=Fv(hi, T), scalar1=GATE_CLAMP, scalar2=None,
                                op0=ALU.min), reads=[Fb[hi]], writes=[Fb[hi]])
                        else:
                            phase1b(l, tile, kh, bq_, bf_, sg, qscale)
                    if not hg:
                        for _ in range(2):
                            v_piece(vdone)
                            vdone += 1
                for _ in range(nv_after):
                    v_piece(vdone)
                    vdone += 1
                if hg:
                    n_ = len(heads)
                    for it_ in range(n_ + 3):
                        for off, stg in ((3, "D"), (2, "C"), (1, "B"), (0, "A")):
                            k_ = it_ - off
                            if 0 <= k_ < n_:
                                phase1b(l, tile, heads[k_], None, None, sg, qscale, stages=stg)
            assert vdone == 4

            def stage_A(ch):
                b = bank1()

                def fa(e):
                    for kh in range(nkh):
                        ins = e.matmul(PS[0:64, b, kh * 64:(kh + 1) * 64], lhsT=HA[:, 8 + kh, ch * 64:(ch + 1) * 64],
                                       rhs=HA[:, kh, ch * 64:(ch + 1) * 64], start=True, stop=True)
                    return ins
                S.op("pe", fa, reads=Hb[0:nkh] + Hb[8:8 + nkh], writes=[PSb[b]])
                pt = 50 + ch % 2
                S.op("dve", lambda e: e.tensor_tensor(out=HA[0:64, pt, :nkh * 64], in0=PS[0:64, b, :nkh * 64],
                                                      in1=MASK[:, :nkh * 64], op=ALU.mult),
                     reads=[PSb[b], cB], writes=[Hb[pt]])
                bt = bank1()

                def ftr(e):
                    for kh in range(nkh):
                        ins = e.transpose(out=PSbf(bt, 64)[:, kh * 128:(kh + 1) * 128],
                                          in_=HA[:, 16 + kh, ch * 64:(ch + 1) * 64], identity=IDB[:])
                    return ins
                S.op("pe", ftr, reads=Hb[16:16 + nkh] + [cB], writes=[PSb[bt]])
                kt = 22 + ch % 2
                S.op("act", lambda e: acopy(e, out=Fbf(kt, 64)[:, :nkh * 128], in_=PSbf(bt, 64)[:, :nkh * 128]),
                     reads=[PSb[bt]], writes=[Fb[kt]])
                bA = bank2()
                bB = bA + 1

                def fd(e):
                    for u in range(8):
                        kh = kh_of(u)
                        ins = e.matmul(PS[:, (bA if u < 4 else bB), (u % 4) * 128:(u % 4) * 128 + 128],
                                       lhsT=Fbf(kt, 64)[:, kh * 128:(kh + 1) * 128],
                                       rhs=Fbf(14 + ch, 64)[:, u * 128:(u + 1) * 128], start=True, stop=True)
                    return ins
                S.op("pe", fd, reads=[Fb[kt], Fb[14 + ch]], writes=[PSb[bA], PSb[bB]])
                return (bA, bB)

            def stage_state(ch, b2):
                if tile["kind"] == "sample":
                    src = (st_h if hg else st_g)[j, ch].rearrange("h k v -> k h v")
                    S.op("pool", lambda e: e.dma_start(
                        out=ST[:, l, :].rearrange("p (h v) -> p h v", h=(8 if hg else 4)), in_=src),
                        writes=[STb[l]], dsem=st_io[l])
                elif tile["first"] and ch == 0:
                    S.op("pool", lambda e: e.memset(ST[:, l, :], 0.0), writes=[STb[l]])
                sbuf = 40 + 2 * (ch % 3)
                S.op("act", lambda e: acopy(e, out=HA[:, sbuf:sbuf + 2, :].rearrange("p a c -> p (a c)"),
                                             in_=ST[:, l, :]),
                     reads=[STb[l]], writes=Hb[sbuf:sbuf + 2])

                def fu(e):
                    for u in range(8):
                        kh = kh_of(u)
                        ins = e.scalar_tensor_tensor(
                            out=ST[:, l, u * 128:(u + 1) * 128], in0=ST[:, l, u * 128:(u + 1) * 128],
                            scalar=EBL[:, kh, ch:ch + 1],
                            in1=PS[:, b2[0 if u < 4 else 1], (u % 4) * 128:(u % 4) * 128 + 128],
                            op0=ALU.mult, op1=ALU.add)
                    return ins
                S.op("dve", fu, reads=[STb[l], eblB, PSb[b2[0]], PSb[b2[1]]], writes=[STb[l]])
                dst = None
                if tile["kind"] == "sample":
                    dst = (sh_s if hg else sg_s)[j, ch]
                elif tile["last"] and ch == nch - 1:
                    dst = (sh_p if hg else sg_p)[j]
                if dst is not None:
                    ob = Buf()
                    outB.append(ob)
                    S.op("pool", lambda e: e.dma_start(
                        out=dst.rearrange("h k v -> k h v"),
                        in_=ST[:, l, :].rearrange("p (h v) -> p h v", h=(8 if hg else 4))),
                        reads=[STb[l]], writes=[ob], dsem=st_io[l])
                return sbuf

            def stage_CB(ch, sbuf):
                b = bank1()
                pt = 50 + ch % 2
                sv = HA[:, sbuf:sbuf + 2, :].rearrange("p a c -> p (a c)")

                def fcb(e):
                    for u in range(8):
                        kh = kh_of(u)
                        e.matmul(PS[:, b, u * 64:(u + 1) * 64], lhsT=sv[:, u * 128:(u + 1) * 128],
                                 rhs=HA[:, kh, ch * 64:(ch + 1) * 64], start=True, stop=False)
                        ins = e.matmul(PS[:, b, u * 64:(u + 1) * 64], lhsT=Fbf(14 + ch, 64)[:, u * 128:(u + 1) * 128],
                                       rhs=HA[0:64, pt, kh * 64:(kh + 1) * 64], start=False, stop=True)
                    return ins
                S.op("pe", fcb, reads=Hb[sbuf:sbuf + 2] + Hb[0:nkh] + [Fb[14 + ch], Hb[pt]], writes=[PSb[b]])
                S.op("act", lambda e: acopy(e, out=FA[:, 0:8, ch * 64:(ch + 1) * 64],
                                             in_=PS[:, b, :].rearrange("p (u t) -> p u t", t=64)),
                     reads=[PSb[b]], writes=Fb[0:8])

            def gate_chunk(s_, ci, u):
                b = fm_chunk(s_, ci, T)
                S.op("act", lambda e: e.activation(out=Fbf(8 + u // 2)[:, (u % 2) * 512:(u % 2) * 512 + T],
                                                   in_=PSv(b, T), func=AF.Silu),
                     reads=[PSb[b]], writes=[Fb[8 + u // 2]])

            def xq_chunk(s_, ci, hh):
                b = fm_chunk(s_, ci, T)
                S.op("dve", lambda e: e.tensor_copy(out=H(36 + hh, T), in_=PSv(b, T)),
                     reads=[PSb[b]], writes=[Hb[36 + hh]])

            extras = []
            for pp in range(4):
                extras.append(("g", pp))
            for pp in range(2):
                extras.append(("xq", pp))

            def run_extra(item):
                kind, pp = item
                s_ = fetch((l, kind, pp))
                for ci in range(2):
                    if kind == "g":
                        gate_chunk(s_, ci, 2 * pp + ci)
                    else:
                        xq_chunk(s_, ci, 2 * pp + ci)

            b2s = {0: stage_A(0)}
            for ch in range(nch):
                if ch + 1 < nch:
                    b2s[ch + 1] = stage_A(ch + 1)
                if extras:
                    run_extra(extras.pop(0))
                sbuf = stage_state(ch, b2s[ch])
                stage_CB(ch, sbuf)
            while extras:
                run_extra(extras.pop(0))

            ncol = (oA_hn + j * 8) if hg else (oB_gn + j * 8)
            NC_ = CA if hg else CB
            Vd = 128 if hg else 256
            nh = 8 if hg else 4
            for hh in range(nh):
                us = [hh] if hg else [2 * hh, 2 * hh + 1]
                for u in us:
                    S.op("act", lambda e, u=u: e.activation(out=H(16 + u, T), in_=Fv(u, T), func=AF.Square),
                         reads=[Fb[u]], writes=[Hb[16 + u]])
                b = bank1()

                def fss(e, us=us, b=b):
                    for n_, u in enumerate(us):
                        ins = e.matmul(PSv(b, T), lhsT=ONES[:], rhs=H(16 + u, T), start=(n_ == 0),
                                       stop=(n_ == len(us) - 1))
                    return ins
                S.op("pe", fss, reads=[Hb[16 + u] for u in us] + [cB], writes=[PSb[b]])
                tf = 12 + hh % 2
                S.op("act", lambda e, b=b, tf=tf: e.activation(out=Fv(tf, T), in_=PSv(b, T), func=AF.Ln,
                                                               scale=1.0 / Vd, bias=CST[:, 2:3]),
                     reads=[PSb[b], cB], writes=[Fb[tf]])
                br = bank1()
                S.op("act", lambda e, br=br, tf=tf: e.activation(out=PSv(br, T), in_=Fv(tf, T), func=AF.Exp,
                                                                 scale=-0.5),
                     reads=[Fb[tf]], writes=[PSb[br]])
                for u in us:
                    S.op("dve", lambda e, u=u, br=br: e.tensor_tensor(out=Fv(u, T), in0=Fv(u, T), in1=PSv(br, T),
                                                                       op=ALU.mult),
                         reads=[Fb[u], PSb[br]], writes=[Fb[u]])
                    S.op("dve", lambda e, u=u: e.scalar_tensor_tensor(
                        out=H(24 + u, T), in0=Fv(u, T), scalar=NC_[:, ncol + u:ncol + u + 1],
                        in1=Fbf(8 + u // 2)[:, (u % 2) * 512:(u % 2) * 512 + T],
                        op0=ALU.mult, op1=ALU.mult),
                        reads=[Fb[u], Fb[8 + u // 2], cB], writes=[Hb[24 + u]])

            if tile["kind"] == "sample":
                ranges = []
                for sq in range(2):
                    fst = 8 + 2 * sq
                    S.op("pool", lambda e, sq=sq, fst=fst: e.dma_start(
                        out=Fwide(fst, 2).rearrange("p (g c) -> p g c", c=512),
                        in_=cmk[l, sq].rearrange("(g p) c -> p g c", p=128)),
                        writes=Fb[fst:fst + 2], dsem=smem_ds[sq])
                    kt0 = 56 + 4 * sq
                    ktv = HA[:, kt0:kt0 + 2, :].rearrange("p a c -> p (a c)").rearrange("p (h m) -> p h m", m=256)
                    for mg in range(2):
                        b = bank1()

                        def ftk(e, fst=fst, mg=mg, b=b):
                            for h in range(4):
                                ins = e.transpose(out=PS[:, b, h * 128:(h + 1) * 128],
                                                  in_=Fwide(fst, 2)[:, mg * 512 + h * 128:mg * 512 + (h + 1) * 128],
                                                  identity=IDF[:])
                            return ins
                        S.op("pe", ftk, reads=Fb[fst:fst + 2] + [cB], writes=[PSb[b]])
                        S.op("dve", lambda e, ktv=ktv, mg=mg, b=b: e.tensor_copy(
                            out=ktv[:, :, mg * 128:(mg + 1) * 128],
                            in_=PS[:, b, :].rearrange("p (h m) -> p h m", m=128)),
                            reads=[PSb[b]], writes=Hb[kt0:kt0 + 2])
                    fsv = 12 + 2 * sq
                    S.op("pool", lambda e, sq=sq, fsv=fsv: e.dma_start(
                        out=Fwide(fsv, 2).rearrange("p (g c) -> p g c", c=512),
                        in_=cmv[l, sq].rearrange("(g p) c -> p g c", p=128)),
                        writes=Fb[fsv:fsv + 2], dsem=smem_ds[2 + sq])
                    vv = HA[:, kt0 + 2:kt0 + 4, :].rearrange("p a c -> p (a c)").rearrange("p (g c) -> p g c", c=512)
                    S.op("dve", lambda e, vv=vv, fsv=fsv: e.tensor_copy(
                        out=vv, in_=Fwide(fsv, 2).rearrange("p (g c) -> p g c", c=512)),
                        reads=Fb[fsv:fsv + 2], writes=Hb[kt0 + 2:kt0 + 4])
                    ranges.append((sq * 64, 64, ktv, Hb[kt0:kt0 + 2], vv, Hb[kt0 + 2:kt0 + 4]))
            else:
                ranges = [(0, T, KTP[:, l, :, :], [KTPb[l]], VP[:, l, :, :], [VPb[l]])]
            it = 0
            for hh in range(4):
                for (r0, n, ktv, ktB, vv, vB) in ranges:
                    pbase = 46 + 2 * (it % 2)
                    it += 1
                    for mc in range(2):
                        b = bank1()
                        S.op("pe", lambda e, b=b, mc=mc, ktv=ktv, hh=hh, r0=r0, n=n: e.matmul(
                            PS[:, b, :n], lhsT=ktv[:, hh, mc * 128:(mc + 1) * 128], rhs=HA[:, 36 + hh, r0:r0 + n],
                            start=True, stop=True), reads=ktB + [Hb[36 + hh]], writes=[PSb[b]])
                        S.op("act", lambda e, b=b, mc=mc, pbase=pbase, n=n: e.activation(
                            out=HA[:, pbase + mc, :n], in_=PS[:, b, :n], func=AF.Exp, scale=128.0 ** -0.5),
                            reads=[PSb[b]], writes=[Hb[pbase + mc]])
                    bd = bank1()

                    def fden(e, bd=bd, pbase=pbase, n=n):
                        e.matmul(PS[:, bd, :n], lhsT=ONES[:], rhs=HA[:, pbase, :n], start=True, stop=False)
                        return e.matmul(PS[:, bd, :n], lhsT=ONES[:], rhs=HA[:, pbase + 1, :n], start=False, stop=True)
                    S.op("pe", fden, reads=[Hb[pbase], Hb[pbase + 1], cB], writes=[PSb[bd]])
                    bp = bank1()

                    def fpv(e, bp=bp, pbase=pbase, n=n, vv=vv, hh=hh):
                        e.matmul(PS[:, bp, :n], lhsT=vv[:, 0, hh * 128:(hh + 1) * 128], rhs=HA[:, pbase, :n],
                                 start=True, stop=False)
                        return e.matmul(PS[:, bp, :n], lhsT=vv[:, 1, hh * 128:(hh + 1) * 128],
                                        rhs=HA[:, pbase + 1, :n], start=False, stop=True)
                    S.op("pe", fpv, reads=[Hb[pbase], Hb[pbase + 1]] + vB, writes=[PSb[bp]])
                    tf = 10 + it % 2
                    S.op("act", lambda e, bd=bd, tf=tf, n=n: e.activation(out=Fv(tf, n), in_=PS[:, bd, :n],
                                                                          func=AF.Ln),
                         reads=[PSb[bd]], writes=[Fb[tf]])
                    S.op("act", lambda e, tf=tf, n=n: e.activation(out=Fv(tf, n), in_=Fv(tf, n), func=AF.Exp,
                                                                   scale=-1.0),
                         reads=[Fb[tf]], writes=[Fb[tf]])
                    S.op("dve", lambda e, bp=bp, tf=tf, n=n, hh=hh, r0=r0: e.tensor_tensor(
                        out=HA[:, 32 + hh, r0:r0 + n], in0=PS[:, bp, :n], in1=Fv(tf, n), op=ALU.mult),
                        reads=[PSb[bp], Fb[tf]], writes=[Hb[32 + hh]])

            for jo in range(8):
                s = fetch((l, "out", jo))
                b = bank1()

                def fo(e, s=s, b=b):
                    for kc in range(12):
                        ins = e.matmul(PSv(b, T), lhsT=RG[:, s, kc * 128:(kc + 1) * 128], rhs=H(24 + kc, T),
                                       start=(kc == 0), stop=(kc == 11))
                    return ins
                S.op("pe", fo, reads=[RGb[s]] + Hb[24:36], writes=[PSb[b]])
                S.op("dve", lambda e, jo=jo, b=b: e.scalar_tensor_tensor(
                    out=X[:, jo, :T], in0=X[:, jo, :T], scalar=ALPHA, in1=PSv(b, T), op0=ALU.mult, op1=ALU.add),
                    reads=[Xb[jo], PSb[b]], writes=[Xb[jo]])

        def phase1b(l, tile, kh, bq_, bk_, sg, qscale, stages="ABCD"):
            T = tile["T"]
            nch = T // CH
            j = l // 2
            hg = (l % 2 == 0)
            hi = kh % 6
            st = 8 + 3 * (kh % 2)
            Bt, E1, E2 = st, st + 1, st + 2
            B3 = Fv(Bt, T).rearrange("p (c t) -> p c t", t=64)
            if "A" in stages:
                if hg:
                    S.op("act", lambda e: e.activation(out=Fv(Bt, T), in_=Fv(hi, T), func=AF.Ln, scale=-1.0,
                                                       bias=CST[:, 1:2]),
                         reads=[Fb[hi], cB], writes=[Fb[Bt]])
                else:
                    bz = bank1()
                    S.op("pe", lambda e: e.matmul(PSv(bz, T), lhsT=WG2[:, j, kh * 128:(kh + 1) * 128],
                                                  rhs=GAT[:, :T], start=True, stop=True),
                         reads=[cB, gatB], writes=[PSb[bz]])
                    S.op("act", lambda e: e.activation(out=Fv(Bt, T), in_=PSv(bz, T), func=AF.Exp, scale=-1.0,
                                                       bias=NBG[:, j * 4 + kh:j * 4 + kh + 1]),
                         reads=[PSb[bz], cB], writes=[Fb[Bt]])
                    S.op("act", lambda e: e.activation(out=Fv(Bt, T), in_=Fv(Bt, T), func=AF.Ln, bias=CST[:, 1:2]),
                         reads=[Fb[Bt], cB], writes=[Fb[Bt]])
            if "B" in stages:
                S.op("dve", lambda e: e.tensor_tensor_scan(out=Fv(Bt, T), data0=RMASK[:, :T], data1=Fv(Bt, T),
                                                           initial=0.0, op0=ALU.mult, op1=ALU.add),
                     reads=[Fb[Bt], cB], writes=[Fb[Bt]])
                if hg:
                    S.op("dve", lambda e: e.tensor_scalar(out=Fv(Bt, T), in0=Fv(Bt, T), scalar1=-80.0,
                                                          scalar2=None, op0=ALU.max),
                         reads=[Fb[Bt]], writes=[Fb[Bt]])
            if "C" in stages:
                S.op("act", lambda e: e.activation(out=Fv(E1, T), in_=Fv(Bt, T), func=AF.Exp, scale=sg),
                     reads=[Fb[Bt]], writes=[Fb[E1]])
                S.op("act", lambda e: e.activation(out=Fv(E2, T), in_=Fv(Bt, T), func=AF.Exp, scale=-sg),
                     reads=[Fb[Bt]], writes=[Fb[E2]])
                S.op("act", lambda e: e.activation(out=EBL[:, kh, 0:nch], in_=B3[:, :, 63], func=AF.Exp, scale=sg),
                     reads=[Fb[Bt]], writes=[eblB])
            if "D" in stages:
                if hg:
                    S.op("dve", lambda e: e.scalar_tensor_tensor(out=H(kh, T), in0=H(52 + hi, T), scalar=qscale,
                                                                 in1=Fv(E1, T), op0=ALU.mult, op1=ALU.mult),
                         reads=[Hb[52 + hi], Fb[E1]], writes=[Hb[kh]])
                    S.op("dve", lambda e: e.tensor_tensor(out=H(8 + kh, T), in0=Fv(hi, T), in1=Fv(E2, T),
                                                          op=ALU.mult),
                         reads=[Fb[hi], Fb[E2]], writes=[Hb[8 + kh]])
                else:
                    S.op("dve", lambda e: e.scalar_tensor_tensor(out=H(kh, T), in0=PSv(bq_, T), scalar=qscale,
                                                                 in1=Fv(E1, T), op0=ALU.mult, op1=ALU.mult),
                         reads=[PSb[bq_], Fb[E1]], writes=[Hb[kh]])
                    S.op("dve", lambda e: e.tensor_tensor(out=H(8 + kh, T), in0=PSv(bk_, T), in1=Fv(E2, T),
                                                          op=ALU.mult),
                         reads=[PSb[bk_], Fb[E2]], writes=[Hb[8 + kh]])
                S.op("dve", lambda e: e.tensor_tensor(
                    out=H(16 + kh, T).rearrange("p (c t) -> p c t", t=64),
                    in0=H(8 + kh, T).rearrange("p (c t) -> p c t", t=64),
                    in1=EBL[:, kh, 0:nch].unsqueeze(2).to_broadcast([128, nch, 64]), op=ALU.mult),
                    reads=[Hb[8 + kh], eblB], writes=[Hb[16 + kh]])

        tiles = [dict(kind="sample", T=128, tok0=0, first=False, last=False)]
        npt = seq // 512
        for t in range(npt):
            tiles.append(dict(kind="prompt", T=512, tok0=t * 512, first=(t == 0), last=(t == npt - 1)))
        for tile in (tiles if STAGE >= 3 else []):
            T = tile["T"]
            load_x(x_s if tile["kind"] == "sample" else x_p, tile["tok0"], T)
            for l in range(depth if STAGE >= 4 else 0):
                ffn(l, 0, T)
                if STAGE >= 5:
                    layer_norm(l, 0, T)
                if STAGE >= 6:
                    mixer(l, tile)
                    layer_norm(l, 1, T)
                    ffn(l, 1, T)
                    layer_norm(l, 2, T)
            store_y(y_s if tile["kind"] == "sample" else y_p, tile["tok0"], T)

        S.op("pool", lambda e: e.memset(TINY[:, 0:1], 0.0), reads=outB, writes=[tinyB])

        with nc.Block() as block:
            S.emit(block)
    return nc


def run(inputs, depth, seq, ncores):
    nc = build(depth, seq)
    NHL = (depth + 1) // 2
    NGL = depth // 2
    f = lambda a: np.ascontiguousarray(np.asarray(a, dtype=np.float32))
    shared = {}
    for k in ["ffn_w_gate", "ffn_w_up", "ffn_w_down", "ln_gain", "ln_bias", "hgrn_w_in", "hgrn_lb_logits",
              "hgrn_norm", "hgrn_w_out", "gla_w_in", "gla_w_gate2", "gla_b_gate", "gla_norm", "gla_w_out",
              "mem_w_k", "mem_w_v"]:
        shared[k] = f(inputs[k])
    in_maps = []
    for c in range(ncores):
        m = dict(shared)
        m["x_p"] = f(inputs["x_prompt"][c, :seq])
        m["x_s"] = f(np.asarray(inputs["x_sample"])[2 * c:2 * c + 2].reshape(128, D))
        m["mem_p"] = f(inputs["mem_prompt"][c])
        m["cmk"] = f(np.asarray(inputs["cache_mem_k"])[:, 2 * c:2 * c + 2].reshape(depth, 2, MEMT, 512))
        m["cmv"] = f(np.asarray(inputs["cache_mem_v"])[:, 2 * c:2 * c + 2].reshape(depth, 2, MEMT, 512))
        m["st_h"] = f(np.asarray(inputs["state_hgrn"])[:, 2 * c:2 * c + 2])
        m["st_g"] = f(np.asarray(inputs["state_gla"])[:, 2 * c:2 * c + 2])
        in_maps.append(m)
    res = run_bass_kernel_spmd(nc, in_maps, core_ids=list(range(ncores)))
    R = res.results
    y_prompt = np.stack([R[c]["y_p"] for c in range(ncores)], 0)
    y_sample = np.concatenate([R[c]["y_s"].reshape(2, 64, D) for c in range(ncores)], 0)
    sh_p = np.stack([R[c]["sh_p"] for c in range(ncores)], 1)
    sg_p = np.stack([R[c]["sg_p"] for c in range(ncores)], 1)[:NGL]
    mk = np.stack([R[c]["mk_p"] for c in range(ncores)], 1).reshape(depth, ncores, MEMT, 4, 128)
    mv = np.stack([R[c]["mv_p"] for c in range(ncores)], 1).reshape(depth, ncores, MEMT, 4, 128)
    sh_s = np.concatenate([R[c]["sh_s"] for c in range(ncores)], 1)
    sg_s = np.concatenate([R[c]["sg_s"] for c in range(ncores)], 1)[:NGL]
    return tuple(np.ascontiguousarray(a, dtype=np.float32) for a in
                 (y_prompt, y_sample, sh_p, sg_p, mk, mv, sh_s, sg_s))


def kernel(**inputs):
    return run(inputs, 4, 8192, 8)
```

```python
import os
import numpy as np
from contextlib import ExitStack
import concourse.bass as bass
import concourse.mybir as mybir
from concourse.bass_utils import run_bass_kernel_spmd

F32 = mybir.dt.float32
BF16 = mybir.dt.bfloat16
AF = mybir.ActivationFunctionType
ALU = mybir.AluOpType

D = 1024
DFF = 2816
NFF = 22
CH = 64
MEMT = 256
ALPHA = (2.0 * 4) ** 0.25
LN_EPS = 1e-5
RMS_EPS = 1e-6
GATE_CLAMP = 1.0 - 1e-6
HG_IN = 4608
GL_IN = 3600
NSLOT = 6
SLOTC = 2048
NF = 24
NH_ = 64
SEM_LIM = 16000
STAGE = int(os.environ.get("KSTAGE", "9"))
SUB = int(os.environ.get("KSUB", "9"))


def acopy(e, out, in_):
    return e.activation(out=out, in_=in_, func=AF.Identity)


class Buf:
    __slots__ = ("w", "r", "x")

    def __init__(self, x=False):
        self.w = {}
        self.r = {}
        self.x = x


class Sched:
    def __init__(self, nc, es):
        self.nc = nc
        self.es = es
        self.names = ["pe", "act", "dve", "pool", "sp"]
        self.q = {k: [] for k in self.names}
        self.cnt = {k: 0 for k in self.names}
        self.semlist = {k: [] for k in self.names}
        self.waited = {k: {} for k in self.names}
        self.dsems = []

    def _sem(self, name):
        return self.es.enter_context(self.nc.semaphore(name))

    def new_dsem(self):
        self.dsems.append([self._sem("d%d" % len(self.dsems)), 0])
        return len(self.dsems) - 1

    def resolve(self, key, val):
        if isinstance(key, tuple):
            return self.dsems[key[1]][0], val
        k = (val - 1) // SEM_LIM
        return self.semlist[key][k], (val - 1) % SEM_LIM + 1

    def op(self, eng, fn, reads=(), writes=(), dsem=None, ndma=1):
        deps = {}
        for b in reads:
            for key, v in b.w.items():
                if deps.get(key, 0) < v:
                    deps[key] = v
            if b.x:
                for key, v in b.r.items():
                    if key != eng and deps.get(key, 0) < v:
                        deps[key] = v
        for b in writes:
            for key, v in b.w.items():
                if deps.get(key, 0) < v:
                    deps[key] = v
            for key, v in b.r.items():
                if deps.get(key, 0) < v:
                    deps[key] = v
        wd = self.waited[eng]
        waits = []
        for key, v in deps.items():
            if key == "pe" and eng == "pe":
                continue
            if wd.get(key, 0) < v:
                wd[key] = v
                waits.append((key, v))
        if dsem is None:
            self.cnt[eng] += 1
            val = self.cnt[eng]
            key = eng
            k = (val - 1) // SEM_LIM
            while len(self.semlist[eng]) <= k:
                self.semlist[eng].append(self._sem("%s%d" % (eng, len(self.semlist[eng]))))
            inc = 1
        else:
            d = self.dsems[dsem]
            d[1] += 16 * ndma
            val = d[1]
            key = ("d", dsem)
            inc = 16
        self.q[eng].append((fn, waits, key, val, inc))
        for b in reads:
            if b.r.get(key, 0) < val:
                b.r[key] = val
        for b in writes:
            b.w = {key: val}
            b.r = {}
        return (key, val)

    def emit(self, block):
        bn = {"pe": "tensor", "act": "scalar", "dve": "vector", "pool": "gpsimd", "sp": "sync"}
        for name in self.names:
            ops = self.q[name]

            def body(e, ops=ops):
                for fn, waits, key, val, inc in ops:
                    for k, v in waits:
                        sem, sv = self.resolve(k, v)
                        e.wait_ge(sem, sv)
                    ins = fn(e)
                    sem, _ = self.resolve(key, val)
                    if isinstance(ins, list):
                        for i_ in ins:
                            i_.then_inc(sem, inc)
                    else:
                        ins.then_inc(sem, inc)

            getattr(block, bn[name])(body)


def build(depth, seq):
    NHL = (depth + 1) // 2
    NGL = depth // 2
    assert NHL <= 2
    nc = bass.Bass("TRN2", target_bir_lowering=False)

    def din(name, shape):
        return nc.dram_tensor(name, list(shape), F32, kind="ExternalInput").ap()

    def dout(name, shape):
        return nc.dram_tensor(name, list(shape), F32, kind="ExternalOutput").ap()

    x_p = din("x_p", [seq, D])
    x_s = din("x_s", [128, D])
    mem_p = din("mem_p", [MEMT, D])
    cmk = din("cmk", [depth, 2, MEMT, 512])
    cmv = din("cmv", [depth, 2, MEMT, 512])
    st_h = din("st_h", [NHL, 2, 8, 128, 128])
    st_g = din("st_g", [max(NGL, 1), 2, 4, 128, 256])
    w_gate = din("ffn_w_gate", [depth, 2, D, DFF])
    w_up = din("ffn_w_up", [depth, 2, D, DFF])
    w_down = din("ffn_w_down", [depth, 2, DFF, D])
    ln_gain = din("ln_gain", [depth, 3, D])
    ln_bias = din("ln_bias", [depth, 3, D])
    hg_w_in = din("hgrn_w_in", [NHL, D, HG_IN])
    hg_lb = din("hgrn_lb_logits", [NHL, D])
    hg_norm = din("hgrn_norm", [NHL, D])
    hg_w_out = din("hgrn_w_out", [NHL, 1536, D])
    gl_w_in = din("gla_w_in", [max(NGL, 1), D, GL_IN])
    gl_wg2 = din("gla_w_gate2", [max(NGL, 1), 16, 512])
    gl_bg = din("gla_b_gate", [max(NGL, 1), 512])
    gl_norm = din("gla_norm", [max(NGL, 1), D])
    gl_w_out = din("gla_w_out", [max(NGL, 1), 1536, D])
    mem_wk = din("mem_w_k", [depth, D, 512])
    mem_wv = din("mem_w_v", [depth, D, 512])

    y_p = dout("y_p", [seq, D])
    y_s = dout("y_s", [128, D])
    sh_p = dout("sh_p", [NHL, 8, 128, 128])
    sg_p = dout("sg_p", [max(NGL, 1), 4, 128, 256])
    mk_p = dout("mk_p", [depth, MEMT, 512])
    mv_p = dout("mv_p", [depth, MEMT, 512])
    sh_s = dout("sh_s", [NHL, 2, 8, 128, 128])
    sg_s = dout("sg_s", [max(NGL, 1), 2, 4, 128, 256])

    pieces = []
    pidx = {}

    def reg(name, ncols, subs):
        pidx[name] = len(pieces)
        pieces.append((ncols, subs))

    for l in range(depth):
        j = l // 2
        hg = (l % 2 == 0)

        def reg_ffn(i):
            for c in range(NFF):
                reg((l, "gu", i, c), 2048, [(w_gate[l, i], 0, 8, c * 128, 128, 0),
                                            (w_up[l, i], 0, 8, c * 128, 128, 1024)])
            for jo in range(8):
                for hh in range(2):
                    reg((l, "dn", i, jo, hh), 1408, [(w_down[l, i], hh * 11, 11, jo * 128, 128, 0)])

        reg_ffn(0)
        if hg:
            W = hg_w_in[j]
            for pp in range(4):
                reg((l, "q", pp), 2048, [(W, 0, 8, (2 * pp + ci) * 128, 128, ci * 1024) for ci in range(2)])
                reg((l, "f", pp), 2048, [(W, 0, 8, 1024 + (2 * pp + ci) * 128, 128, ci * 1024) for ci in range(2)])
            for vp in range(4):
                reg((l, "v", vp), 2048, [(W, 0, 8, 2048 + vp * 256, 256, 0)])
            for pp in range(4):
                reg((l, "g", pp), 2048, [(W, 0, 8, 3072 + (2 * pp + ci) * 128, 128, ci * 1024) for ci in range(2)])
            for pp in range(2):
                reg((l, "xq", pp), 2048, [(W, 0, 8, 4096 + (2 * pp + ci) * 128, 128, ci * 1024) for ci in range(2)])
            WO = hg_w_out[j]
        else:
            W = gl_w_in[j]
            reg((l, "ga"), 128, [(W, 0, 8, 3072, 16, 0)])
            for pp in range(2):
                reg((l, "q", pp), 2048, [(W, 0, 8, (2 * pp + ci) * 128, 128, ci * 1024) for ci in range(2)])
                reg((l, "f", pp), 2048, [(W, 0, 8, 512 + (2 * pp + ci) * 128, 128, ci * 1024) for ci in range(2)])
            for vp in range(4):
                reg((l, "v", vp), 2048, [(W, 0, 8, 1024 + vp * 256, 256, 0)])
            for pp in range(4):
                reg((l, "g", pp), 2048, [(W, 0, 8, 2048 + (2 * pp + ci) * 128, 128, ci * 1024) for ci in range(2)])
            for pp in range(2):
                reg((l, "xq", pp), 2048, [(W, 0, 8, 3088 + (2 * pp + ci) * 128, 128, ci * 1024) for ci in range(2)])
            WO = gl_w_out[j]
        for jo in range(8):
            reg((l, "out", jo), 1536, [(WO, 0, 12, jo * 128, 128, 0)])
        reg_ffn(1)
    NP = len(pieces)
    wscr = nc.dram_tensor("wscr", [NP, 128, SLOTC], BF16, kind="Internal").ap()

    es = ExitStack()
    with es:
        S = Sched(nc, es)

        def sb(name, shape, dt):
            return es.enter_context(nc.sbuf_tensor(name, list(shape), dt))

        RG = sb("ring", [128, NSLOT, SLOTC], BF16)
        X = sb("X", [128, 8, 512], F32)
        XB = sb("XB", [128, 8, 512], BF16)
        FA = sb("FA", [128, NF, 512], F32)
        HA = sb("HA", [128, NH_, 512], BF16)
        ST = sb("ST", [128, depth, 1024], F32)
        KTP = sb("KTP", [128, depth, 4, 256], BF16)
        VP = sb("VP", [128, depth, 2, 512], BF16)
        IDF = sb("identf", [128, 128], F32)
        IDB = sb("identb", [128, 128], BF16)
        ONES = sb("onesb", [128, 128], BF16)
        MASK = sb("mask", [64, 512], F32)
        RMASK = sb("rmask", [128, 512], F32)
        CST = sb("cst", [128, 4], F32)
        CA = sb("constA", [128, 128], F32)
        CB = sb("constB", [128, 128], F32)
        PST = sb("pstage", [128, 128], F32)
        HGP = sb("hgp", [128, 2, 2, 8], F32)
        NBG = sb("nbg", [128, 8], F32)
        WG2 = sb("wg2", [16, 2, 512], BF16)
        WG2F = sb("wg2f", [16, 2, 512], F32)
        GAT = sb("gaT", [16, 512], BF16)
        EBL = sb("ebl", [128, 8, 8], F32)
        TINY = sb("tiny", [128, 16], F32)
        PS = es.enter_context(nc.psum_tensor("ps", [128, 8, 512], F32))

        RGb = [Buf() for _ in range(NSLOT)]
        Xb = [Buf() for _ in range(8)]
        XBb = [Buf() for _ in range(8)]
        Fb = [Buf() for _ in range(NF)]
        Hb = [Buf() for _ in range(NH_)]
        STb = [Buf() for _ in range(depth)]
        KTPb = [Buf() for _ in range(depth)]
        VPb = [Buf() for _ in range(depth)]
        PSb = [Buf(True) for _ in range(8)]
        WSb = [Buf() for _ in range(NP)]
        cB = Buf()
        gatB = Buf()
        eblB = Buf()
        tinyB = Buf()
        outB = []

        ring_ds = [S.new_dsem() for _ in range(NSLOT)]

        def H(i, T=512):
            return HA[:, i, :T]

        def Fv(i, T=512):
            return FA[:, i, :T]

        def Fbf(i, rows=128):
            return FA[0:rows, i, :].bitcast(BF16)

        def Fwide(i, n):
            return FA[:, i:i + n, :].rearrange("p a c -> p (a c)")

        def Fwide_bf(i, n):
            return FA[:, i:i + n, :].bitcast(BF16).rearrange("p a c -> p (a c)")

        rot = {"b1": 0, "b2": 0}

        def bank1():
            b = rot["b1"] % 4
            rot["b1"] += 1
            return b

        def bank2():
            b = 4 + 2 * (rot["b2"] % 2)
            rot["b2"] += 1
            return b

        def PSv(b, T=512, rows=128):
            return PS[0:rows, b, :T]

        def PS2(b, rows=128):
            return PS[0:rows, b:b + 2, :].rearrange("p a c -> p (a c)")

        def PSbf(b, rows=128):
            return PS[0:rows, b, :].bitcast(BF16)

        def c_init(e):
            e.memset(CST[:, 0:1], LN_EPS)
            e.memset(CST[:, 1:2], 1.0)
            e.memset(CST[:, 2:3], RMS_EPS)
            e.memset(CST[:, 3:4], 0.0)
            e.memset(ONES[:], 1.0)
            e.memset(IDF[:], 1.0)
            e.memset(MASK[:], 1.0)
            e.memset(HGP[:, 0, :, :], 0.5)
            e.memset(HGP[:, 1, :, :], -0.5)
            return e.memset(RMASK[:], 1.0)

        S.op("pool", c_init, writes=[cB])
        S.op("pool", lambda e: e.affine_select(out=IDF[:], in_=IDF[:], pattern=[[1, 128]], compare_op=ALU.is_equal,
                                               fill=0.0, base=0, channel_multiplier=-1), reads=[cB], writes=[cB])
        S.op("pool", lambda e: e.affine_select(out=MASK[:], in_=MASK[:], pattern=[[0, 8], [1, 64]],
                                               compare_op=ALU.is_ge, fill=0.0, base=0, channel_multiplier=-1),
             reads=[cB], writes=[cB])
        S.op("pool", lambda e: e.memset(RMASK[:].rearrange("p (c t) -> p c t", t=64)[:, :, 0:1], 0.0),
             reads=[cB], writes=[cB])
        S.op("dve", lambda e: e.tensor_copy(out=IDB[:], in_=IDF[:]), reads=[cB], writes=[cB])

        nln = depth * 3 * 8
        oA_lb = nln
        oA_hn = nln + NHL * 8
        oB_bg = nln
        oB_gn = nln + NGL * 4
        io_ds = [S.new_dsem() for _ in range(4)]

        def load_rows(dst, r0, src2d, nrows, ds):
            S.op("pool", lambda e: e.dma_start(out=dst[r0:r0 + nrows, :], in_=src2d), writes=[tinyB], dsem=ds)

        S.op("pool", lambda e: e.memset(PST[:], 0.0), writes=[tinyB])
        load_rows(PST, 0, ln_gain.rearrange("l i (c p) -> (l i c) p", p=128), nln, io_ds[0])
        load_rows(PST, oA_lb, hg_lb.rearrange("j (c p) -> (j c) p", p=128), NHL * 8, io_ds[1])
        load_rows(PST, oA_hn, hg_norm.rearrange("j (c p) -> (j c) p", p=128), NHL * 8, io_ds[2])
        S.op("pe", lambda e: e.transpose(out=PS[:, 0, 0:128], in_=PST[:], identity=IDF[:]),
             reads=[tinyB, cB], writes=[PSb[0]])
        S.op("dve", lambda e: e.tensor_copy(out=CA[:], in_=PS[:, 0, 0:128]), reads=[PSb[0]], writes=[cB])
        S.op("pool", lambda e: e.memset(PST[:], 0.0), writes=[tinyB])
        load_rows(PST, 0, ln_bias.rearrange("l i (c p) -> (l i c) p", p=128), nln, io_ds[0])
        if NGL:
            load_rows(PST, oB_bg, gl_bg.rearrange("j (c p) -> (j c) p", p=128), NGL * 4, io_ds[1])
            load_rows(PST, oB_gn, gl_norm.rearrange("j (c p) -> (j c) p", p=128), NGL * 8, io_ds[2])
        S.op("pe", lambda e: e.transpose(out=PS[:, 1, 0:128], in_=PST[:], identity=IDF[:]),
             reads=[tinyB, cB], writes=[PSb[1]])
        S.op("dve", lambda e: e.tensor_copy(out=CB[:], in_=PS[:, 1, 0:128]), reads=[PSb[1]], writes=[cB])
        if NHL == 2:
            S.op("dve", lambda e: e.tensor_tensor(out=TINY[:, 0:8], in0=CA[:, oA_lb + 8:oA_lb + 16],
                                                  in1=CA[:, oA_lb:oA_lb + 8], op=ALU.subtract),
                 reads=[cB], writes=[tinyB])
            S.op("act", lambda e: e.activation(out=TINY[:, 0:8], in_=TINY[:, 0:8], func=AF.Exp),
                 reads=[tinyB], writes=[tinyB])
            S.op("dve", lambda e: e.tensor_scalar(out=TINY[:, 0:8], in0=TINY[:, 0:8], scalar1=1.0, scalar2=None,
                                                  op0=ALU.add), reads=[tinyB], writes=[tinyB])
            S.op("dve", lambda e: e.reciprocal(out=TINY[:, 8:16], in_=TINY[:, 0:8]), reads=[tinyB], writes=[tinyB])
            S.op("dve", lambda e: e.tensor_scalar(out=HGP[:, 0, 1, :], in0=TINY[:, 8:16], scalar1=0.5, scalar2=None,
                                                  op0=ALU.mult), reads=[tinyB, cB], writes=[cB])
            S.op("dve", lambda e: e.tensor_scalar(out=HGP[:, 1, 1, :], in0=TINY[:, 8:16], scalar1=-0.5, scalar2=None,
                                                  op0=ALU.mult), reads=[tinyB, cB], writes=[cB])
        if NGL:
            S.op("dve", lambda e: e.tensor_scalar(out=NBG[:, 0:NGL * 4], in0=CB[:, oB_bg:oB_bg + NGL * 4],
                                                  scalar1=-1.0, scalar2=None, op0=ALU.mult), reads=[cB], writes=[cB])
            for j in range(NGL):
                S.op("pool", lambda e, j=j: e.dma_start(out=WG2F[:, j, :], in_=gl_wg2[j]), writes=[tinyB],
                     dsem=io_ds[3])
            S.op("dve", lambda e: e.tensor_copy(out=WG2[:, 0:NGL, :], in_=WG2F[:, 0:NGL, :]), reads=[tinyB],
                 writes=[cB])

        st_ds = [S.new_dsem() for _ in range(3)]
        sf_ds = [[S.new_dsem() for _ in range(2)] for _ in range(3)]
        cast_eng = ["dve", "act", "pool"]

        def conv_load(p):
            k = p % 3
            ncols, subs = pieces[p]
            sf = Fwide(4 * k, 4)

            def fn(e):
                out = []
                for (src, kc0, nkc, col0, w, off) in subs:
                    out.append(e.dma_start(
                        out=sf[:, off:off + nkc * w].rearrange("p (k c) -> p k c", c=w),
                        in_=src[kc0 * 128:(kc0 + nkc) * 128, col0:col0 + w].rearrange("(k p) c -> p k c", p=128)))
                return out
            S.op("sp", fn, writes=Fb[4 * k:4 * k + 4], dsem=sf_ds[k][0], ndma=len(subs))

        def conv_cast_store(p):
            k = p % 3
            ncols, subs = pieces[p]
            sf = Fwide(4 * k, 4)
            sbf = Fwide_bf(12 + 2 * k, 2)
            eng = cast_eng[p % 3]

            def fc(e):
                if eng == "act":
                    return acopy(e, out=sbf[:, :ncols], in_=sf[:, :ncols])
                return e.tensor_copy(out=sbf[:, :ncols], in_=sf[:, :ncols])
            S.op(eng, fc, reads=Fb[4 * k:4 * k + 4], writes=Fb[12 + 2 * k:14 + 2 * k])
            S.op("sp", lambda e: e.dma_start(out=wscr[p, :, :ncols], in_=sbf[:, :ncols]),
                 reads=Fb[12 + 2 * k:14 + 2 * k], writes=[WSb[p]], dsem=st_ds[k])

        for p in range((NP + 2) if STAGE >= 1 else 0):
            if p < NP:
                conv_load(p)
            if p >= 2:
                conv_cast_store(p - 2)

        rstate = {"n": 0}

        def fetch(name):
            p = pidx[name]
            ncols = pieces[p][0]
            s = rstate["n"] % NSLOT
            rstate["n"] += 1
            S.op("sp", lambda e: e.dma_start(out=RG[:, s, :ncols], in_=wscr[p, :, :ncols]),
                 reads=[WSb[p]], writes=[RGb[s]], dsem=ring_ds[s])
            return s

        mem_ds = [S.new_dsem() for _ in range(4)]
        memT = Fwide_bf(4, 2).rearrange("p (k m) -> p k m", m=256)
        for g in range(2 if STAGE >= 2 else 0):
            S.op("pool", lambda e, g=g: e.dma_start(out=Fwide(2 * g, 2), in_=mem_p[g * 128:(g + 1) * 128, :]),
                 writes=Fb[2 * g:2 * g + 2], dsem=mem_ds[g])
            for half in range(2):
                b = bank1()

                def ft(e, g=g, half=half, b=b):
                    for k in range(4):
                        c = half * 4 + k
                        ins = e.transpose(out=PS[:, b, k * 128:(k + 1) * 128],
                                          in_=Fwide(2 * g, 2)[:, c * 128:(c + 1) * 128], identity=IDF[:])
                    return ins
                S.op("pe", ft, reads=Fb[2 * g:2 * g + 2] + [cB], writes=[PSb[b]])
                S.op("dve", lambda e, g=g, half=half, b=b: e.tensor_copy(
                    out=memT[:, half * 4:half * 4 + 4, g * 128:(g + 1) * 128],
                    in_=PS[:, b, :].rearrange("p (k m) -> p k m", m=128)),
                    reads=[PSb[b]], writes=Fb[4:6])
        wst = Fwide(6, 8).rearrange("p (k c) -> p k c", c=512)
        wbf = Fwide_bf(14, 4).rearrange("p (k c) -> p k c", c=512)
        for l in range(depth if (STAGE >= 2 and SUB >= 2) else 0):
            for kv in range(2):
                wsrc = (mem_wk if kv == 0 else mem_wv)[l]
                odst = (mk_p if kv == 0 else mv_p)[l]
                S.op("sp", lambda e, wsrc=wsrc: e.dma_start(out=wst, in_=wsrc.rearrange("(k p) c -> p k c", p=128)),
                     writes=Fb[6:14], dsem=mem_ds[2])
                S.op("act", lambda e: acopy(e, out=wbf, in_=wst), reads=Fb[6:14], writes=Fb[14:18])
                for mg in range(2 if SUB >= 3 else 0):
                    b = bank1()

                    def fm(e, mg=mg, b=b):
                        for kc in range(8):
                            ins = e.matmul(PS[:, b, :], lhsT=memT[:, kc, mg * 128:(mg + 1) * 128], rhs=wbf[:, kc, :],
                                           start=(kc == 0), stop=(kc == 7))
                        return ins
                    S.op("pe", fm, reads=Fb[4:6] + Fb[14:18], writes=[PSb[b]])
                    S.op("dve", lambda e, b=b: e.tensor_copy(out=Fv(18), in_=PS[:, b, :]), reads=[PSb[b]],
                         writes=[Fb[18]])
                    if kv == 1:
                        S.op("act", lambda e, b=b, l=l, mg=mg: acopy(e, out=VP[:, l, mg, :], in_=PS[:, b, :]),
                             reads=[PSb[b]], writes=[VPb[l]])
                    if SUB < 4:
                        continue
                    ob = Buf()
                    outB.append(ob)
                    S.op("pool", lambda e, odst=odst, mg=mg: e.dma_start(out=odst[mg * 128:(mg + 1) * 128, :],
                                                                        in_=Fv(18)),
                         reads=[Fb[18]], writes=[ob], dsem=mem_ds[3])
                if kv == 0 and SUB >= 5:
                    for h in range(4):
                        b = bank1()

                        def fk(e, h=h, b=b):
                            for kc in range(8):
                                ins = e.matmul(PS[:, b, 0:256], lhsT=wbf[:, kc, h * 128:(h + 1) * 128],
                                               rhs=memT[:, kc, :], start=(kc == 0), stop=(kc == 7))
                            return ins
                        S.op("pe", fk, reads=Fb[4:6] + Fb[14:18], writes=[PSb[b]])
                        S.op("act", lambda e, b=b, l=l, h=h: acopy(e, out=KTP[:, l, h, :], in_=PS[:, b, 0:256]),
                             reads=[PSb[b]], writes=[KTPb[l]])

        xs_ds = [S.new_dsem() for _ in range(2)]
        ys_ds = [S.new_dsem() for _ in range(2)]
        st_io = [S.new_dsem() for _ in range(depth)]
        smem_ds = [S.new_dsem() for _ in range(4)]

        def load_x(src, tok0, T):
            for g in range(T // 128):
                f0 = 8 + 2 * (g % 2)
                S.op("pool", lambda e, g=g, f0=f0: e.dma_start(out=Fwide(f0, 2),
                                                              in_=src[tok0 + g * 128:tok0 + (g + 1) * 128, :]),
                     writes=Fb[f0:f0 + 2], dsem=xs_ds[g % 2])
                for half in range(2):
                    b = bank1()

                    def ft(e, f0=f0, half=half, b=b):
                        for k in range(4):
                            c = half * 4 + k
                            ins = e.transpose(out=PS[:, b, k * 128:(k + 1) * 128],
                                              in_=Fwide(f0, 2)[:, c * 128:(c + 1) * 128], identity=IDF[:])
                        return ins
                    S.op("pe", ft, reads=Fb[f0:f0 + 2] + [cB], writes=[PSb[b]])
                    pv = PS[:, b, :].rearrange("p (k m) -> p k m", m=128)
                    S.op("act", lambda e, g=g, half=half, pv=pv: acopy(e,
                        out=X[:, half * 4:half * 4 + 4, g * 128:(g + 1) * 128], in_=pv),
                        reads=[PSb[b]], writes=Xb[half * 4:half * 4 + 4])
                    S.op("dve", lambda e, g=g, half=half, pv=pv: e.tensor_copy(
                        out=XB[:, half * 4:half * 4 + 4, g * 128:(g + 1) * 128], in_=pv),
                        reads=[PSb[b]], writes=XBb[half * 4:half * 4 + 4])

        def store_y(dst, tok0, T):
            for g in range(T // 128):
                f0 = 12 + 2 * (g % 2)
                for half in range(2):
                    b = bank1()

                    def ft(e, g=g, half=half, b=b):
                        for k in range(4):
                            c = half * 4 + k
                            ins = e.transpose(out=PS[:, b, k * 128:(k + 1) * 128],
                                              in_=X[:, c, g * 128:(g + 1) * 128], identity=IDF[:])
                        return ins
                    S.op("pe", ft, reads=Xb[half * 4:half * 4 + 4] + [cB], writes=[PSb[b]])
                    S.op("act" if half == 0 else "dve",
                         (lambda e, f0=f0, half=half, b=b: acopy(e, out=Fv(f0 + half), in_=PS[:, b, :])) if half == 0 else
                         (lambda e, f0=f0, half=half, b=b: e.tensor_copy(out=Fv(f0 + half), in_=PS[:, b, :])),
                         reads=[PSb[b]], writes=[Fb[f0 + half]])
                ob = Buf()
                outB.append(ob)
                S.op("pool", lambda e, g=g, f0=f0: e.dma_start(out=dst[tok0 + g * 128:tok0 + (g + 1) * 128, :],
                                                              in_=Fwide(f0, 2)),
                     reads=Fb[f0:f0 + 2], writes=[ob], dsem=ys_ds[g % 2])

        def layer_norm(l, i, T):
            col0 = l * 24 + i * 8
            for c in range(8):
                S.op("act", lambda e, c=c: acopy(e, out=XB[:, c, :T], in_=X[:, c, :T]), reads=[Xb[c]], writes=[XBb[c]])
                S.op("act", lambda e, c=c: e.activation(out=H(40 + c, T), in_=X[:, c, :T], func=AF.Square),
                     reads=[Xb[c]], writes=[Hb[40 + c]])
            bs = 4
            bq = 5

            def fs(e):
                for c in range(8):
                    ins = e.matmul(PSv(bs, T), lhsT=ONES[:], rhs=XB[:, c, :T], start=(c == 0), stop=(c == 7))
                return ins
            S.op("pe", fs, reads=XBb + [cB], writes=[PSb[bs]])

            def fq(e):
                for c in range(8):
                    ins = e.matmul(PSv(bq, T), lhsT=ONES[:], rhs=H(40 + c, T), start=(c == 0), stop=(c == 7))
                return ins
            S.op("pe", fq, reads=Hb[40:48] + [cB], writes=[PSb[bq]])
            S.op("act", lambda e: e.activation(out=Fv(16, T), in_=PSv(bs, T), func=AF.Square, scale=1.0 / D),
                 reads=[PSb[bs]], writes=[Fb[16]])
            S.op("dve", lambda e: e.scalar_tensor_tensor(out=Fv(17, T), in0=PSv(bq, T), scalar=1.0 / D, in1=Fv(16, T),
                                                         op0=ALU.mult, op1=ALU.subtract),
                 reads=[PSb[bq], Fb[16]], writes=[Fb[17]])
            S.op("act", lambda e: e.activation(out=Fv(17, T), in_=Fv(17, T), func=AF.Ln, bias=CST[:, 0:1]),
                 reads=[Fb[17], cB], writes=[Fb[17]])
            br = 6
            S.op("act", lambda e: e.activation(out=PSv(br, T), in_=Fv(17, T), func=AF.Exp, scale=-0.5),
                 reads=[Fb[17]], writes=[PSb[br]])
            def TF(c):
                return HA[:, 24 + 2 * c:26 + 2 * c, :].rearrange("p a c -> p (a c)").bitcast(F32)[:, :T]

            deferred = []
            for c in range(8):
                tb = Hb[24 + 2 * c:26 + 2 * c]
                S.op("dve", lambda e, c=c: e.scalar_tensor_tensor(
                    out=TF(c), in0=PSv(bs, T), scalar=-1.0 / D, in1=X[:, c, :T], op0=ALU.mult, op1=ALU.add),
                    reads=[PSb[bs], Xb[c]], writes=tb)
                S.op("dve", lambda e, c=c: e.tensor_tensor(out=TF(c), in0=TF(c), in1=PSv(br, T), op=ALU.mult),
                     reads=tb + [PSb[br]], writes=tb)
                S.op("act", lambda e, c=c: e.activation(
                    out=XB[:, c, :T], in_=TF(c), func=AF.Identity,
                    scale=CA[:, col0 + c:col0 + c + 1], bias=CB[:, col0 + c:col0 + c + 1]),
                    reads=tb + [cB], writes=[XBb[c]])

                def dfn(c=c, tb=tb):
                    S.op("act", lambda e: e.activation(
                        out=X[:, c, :T], in_=TF(c), func=AF.Identity,
                        scale=CA[:, col0 + c:col0 + c + 1], bias=CB[:, col0 + c:col0 + c + 1]),
                        reads=tb + [cB], writes=[Xb[c]])
                deferred.append(dfn)
            return deferred

        def ffn(l, i, T, deferred=()):
            deferred = list(deferred)

            def evac(c, bg, bu):
                hs = 48 + (c % 3)
                S.op("act", lambda e: e.activation(out=H(hs, T), in_=PSv(bg, T), func=AF.Silu),
                     reads=[PSb[bg]], writes=[Hb[hs]])
                S.op("dve", lambda e: e.scalar_tensor_tensor(
                    out=H(c, T), in0=PSv(bu, T), scalar=0.5, in1=H(hs, T), op0=ALU.mult, op1=ALU.mult),
                    reads=[PSb[bu], Hb[hs]], writes=[Hb[c]])

            s01 = [fetch((l, "gu", i, 0)), fetch((l, "gu", i, 1))]
            b01 = [(bank1(), bank1()), (bank1(), bank1())]
            for kc in range(8):
                def fk(e, kc=kc):
                    for c in range(2):
                        for gu in range(2):
                            ins = e.matmul(PSv(b01[c][gu], T),
                                           lhsT=RG[:, s01[c], gu * 1024 + kc * 128:gu * 1024 + (kc + 1) * 128],
                                           rhs=XB[:, kc, :T], start=(kc == 0), stop=(kc == 7))
                    return ins
                S.op("pe", fk, reads=[RGb[s01[0]], RGb[s01[1]], XBb[kc]],
                     writes=[PSb[b01[0][0]], PSb[b01[0][1]], PSb[b01[1][0]], PSb[b01[1][1]]])
            evac(0, b01[0][0], b01[0][1])
            evac(1, b01[1][0], b01[1][1])
            for c in range(2, NFF):
                s = fetch((l, "gu", i, c))
                bg = bank1()
                bu = bank1()

                def fm(e, s=s, bg=bg, bu=bu):
                    for kc in range(8):
                        e.matmul(PSv(bg, T), lhsT=RG[:, s, kc * 128:(kc + 1) * 128], rhs=XB[:, kc, :T],
                                 start=(kc == 0), stop=(kc == 7))
                    for kc in range(8):
                        ins = e.matmul(PSv(bu, T), lhsT=RG[:, s, 1024 + kc * 128:1024 + (kc + 1) * 128],
                                       rhs=XB[:, kc, :T], start=(kc == 0), stop=(kc == 7))
                    return ins
                S.op("pe", fm, reads=[RGb[s]] + XBb, writes=[PSb[bg], PSb[bu]])
                evac(c, bg, bu)
                if deferred:
                    deferred.pop(0)()
            for jo in range(8):
                s0 = fetch((l, "dn", i, jo, 0))
                s1 = fetch((l, "dn", i, jo, 1))
                b = bank1()

                def fd(e, s0=s0, s1=s1, b=b):
                    for kk in range(22):
                        s = s0 if kk < 11 else s1
                        k2 = kk % 11
                        ins = e.matmul(PSv(b, T), lhsT=RG[:, s, k2 * 128:(k2 + 1) * 128], rhs=H(kk, T),
                                       start=(kk == 0), stop=(kk == 21))
                    return ins
                S.op("pe", fd, reads=[RGb[s0], RGb[s1]] + Hb[0:22], writes=[PSb[b]])
                S.op("dve", lambda e, jo=jo, b=b: e.scalar_tensor_tensor(
                    out=X[:, jo, :T], in0=X[:, jo, :T], scalar=ALPHA, in1=PSv(b, T), op0=ALU.mult, op1=ALU.add),
                    reads=[Xb[jo], PSb[b]], writes=[Xb[jo]])

        def fm_chunk(s, ci, T, M=128, kcw=128, base=0):
            b = bank1()

            def fn(e):
                for kc in range(8):
                    o = base + ci * 1024 + kc * kcw
                    ins = e.matmul(PS[0:M, b, :T], lhsT=RG[:, s, o:o + M], rhs=XB[:, kc, :T],
                                   start=(kc == 0), stop=(kc == 7))
                return ins
            S.op("pe", fn, reads=[RGb[s]] + XBb, writes=[PSb[b]])
            return b

        def fm_multi(specs, T):
            banks = [bank1() for _ in specs]
            slots_ = sorted(set(sp[0] for sp in specs))
            for kc in range(8):
                def fn(e, kc=kc):
                    for (s_, off, M, kcw), b in zip(specs, banks):
                        o = off + kc * kcw
                        ins = e.matmul(PS[0:M, b, :T], lhsT=RG[:, s_, o:o + M], rhs=XB[:, kc, :T],
                                       start=(kc == 0), stop=(kc == 7))
                    return ins
                S.op("pe", fn, reads=[RGb[s_] for s_ in slots_] + [XBb[kc]], writes=[PSb[b] for b in banks])
            return banks

        def mixer(l, tile, deferred=()):
            deferred = list(deferred)
            T = tile["T"]
            nch = T // CH
            j = l // 2
            hg = (l % 2 == 0)
            nkh = 8 if hg else 4
            sg = 1.0 if hg else -1.0 / 16.0
            qscale = 128.0 ** -0.5

            def kh_of(u):
                return u if hg else u // 2

            if not hg:
                s = fetch((l, "ga"))
                b = fm_multi([(s, 0, 16, 16)], T)[0]
                S.op("dve", lambda e, b=b: e.tensor_copy(out=GAT[:, :T], in_=PS[0:16, b, :T]), reads=[PSb[b]],
                     writes=[gatB])
            def v_piece(vp):
                sv_ = fetch((l, "v", vp))
                for c0 in range(0, nch, 2):
                    b = bank1()

                    def fv(e, c0=c0, b=b):
                        for cc in range(2):
                            ch = c0 + cc
                            for kc in range(8):
                                ins = e.matmul(PS[0:64, b, cc * 256:(cc + 1) * 256],
                                               lhsT=XB[:, kc, ch * 64:(ch + 1) * 64],
                                               rhs=RG[:, sv_, kc * 256:(kc + 1) * 256],
                                               start=(kc == 0), stop=(kc == 7))
                        return ins
                    S.op("pe", fv, reads=[RGb[sv_]] + XBb, writes=[PSb[b]])
                    S.op("act", lambda e, c0=c0, b=b: acopy(
                        e, out=FA[0:64, 14 + c0:16 + c0, :].bitcast(BF16)[:, :, vp * 256:(vp + 1) * 256],
                        in_=PS[0:64, b, :].rearrange("p (a c) -> p a c", c=256)),
                        reads=[PSb[b]], writes=Fb[14 + c0:16 + c0])

            vdone = 0
            if hg:
                groups = [([0, 1, 2], 3), ([3], 1)]
            else:
                groups = [([0, 1], 0)]
            for pps, nv_after in groups:
                heads = []
                for pp in pps:
                    sq_ = fetch((l, "q", pp))
                    sf_ = fetch((l, "f", pp))
                    pre = {}
                    if pp == 0:
                        ncis = 2 if hg else 1
                        specs = []
                        for ci in range(ncis):
                            specs += [(sq_, ci * 1024, 128, 128), (sf_, ci * 1024, 128, 128)]
                        bks = fm_multi(specs, T)
                        for ci in range(ncis):
                            pre[ci] = (bks[2 * ci], bks[2 * ci + 1])
                    for ci in range(2):
                        kh = 2 * pp + ci
                        heads.append(kh)
                        if ci in pre:
                            bq_, bf_ = pre[ci]
                        else:
                            bq_ = fm_chunk(sq_, ci, T)
                            bf_ = fm_chunk(sf_, ci, T)
                        hi = kh % 6
                        if hg:
                            S.op("act", lambda e, bq_=bq_, hi=hi: e.activation(out=H(52 + hi, T), in_=PSv(bq_, T),
                                                                              func=AF.Silu),
                                 reads=[PSb[bq_]], writes=[Hb[52 + hi]])
                            S.op("act", lambda e, bf_=bf_, hi=hi: e.activation(out=Fv(hi, T), in_=PSv(bf_, T),
                                                                              func=AF.Tanh, scale=0.5),
                                 reads=[PSb[bf_]], writes=[Fb[hi]])
                            S.op("act", lambda e, hi=hi, kh=kh: e.activation(
                                out=Fv(hi, T), in_=Fv(hi, T), func=AF.Identity,
                                scale=HGP[:, 1, j, kh:kh + 1], bias=HGP[:, 0, j, kh:kh + 1]),
                                reads=[Fb[hi], cB], writes=[Fb[hi]])
                            S.op("dve", lambda e, hi=hi: e.tensor_scalar(
                                out=Fv(hi, T), in0=Fv(hi, T), scalar1=GATE_CLAMP, scalar2=None,
                                op0=ALU.min), reads=[Fb[hi]], writes=[Fb[hi]])
                        else:
                            phase1b(l, tile, kh, bq_, bf_, sg, qscale)
                        for _ in range(1 if hg else 2):
                            if deferred:
                                deferred.pop(0)()
                    if not hg:
                        for _ in range(2):
                            v_piece(vdone)
                            vdone += 1
                for _ in range(nv_after):
                    v_piece(vdone)
                    vdone += 1
                if hg:
                    n_ = len(heads)
                    for it_ in range(n_ + 3):
                        for off, stg in ((3, "D"), (2, "C"), (1, "B"), (0, "A")):
                            k_ = it_ - off
                            if 0 <= k_ < n_:
                                phase1b(l, tile, heads[k_], None, None, sg, qscale, stages=stg)
            assert vdone == 4
            while deferred:
                deferred.pop(0)()

            def stage_A(ch):
                b = bank1()

                def fa(e):
                    for kh in range(nkh):
                        ins = e.matmul(PS[0:64, b, kh * 64:(kh + 1) * 64], lhsT=HA[:, 8 + kh, ch * 64:(ch + 1) * 64],
                                       rhs=HA[:, kh, ch * 64:(ch + 1) * 64], start=True, stop=True)
                    return ins
                S.op("pe", fa, reads=Hb[0:nkh] + Hb[8:8 + nkh], writes=[PSb[b]])
                pt = 50 + ch % 2
                S.op("dve", lambda e: e.tensor_tensor(out=HA[0:64, pt, :nkh * 64], in0=PS[0:64, b, :nkh * 64],
                                                      in1=MASK[:, :nkh * 64], op=ALU.mult),
                     reads=[PSb[b], cB], writes=[Hb[pt]])
                bt = bank1()

                def ftr(e):
                    for kh in range(nkh):
                        ins = e.transpose(out=PSbf(bt, 64)[:, kh * 128:(kh + 1) * 128],
                                          in_=HA[:, 16 + kh, ch * 64:(ch + 1) * 64], identity=IDB[:])
                    return ins
                S.op("pe", ftr, reads=Hb[16:16 + nkh] + [cB], writes=[PSb[bt]])
                kt = 22 + ch % 2
                S.op("act", lambda e: acopy(e, out=Fbf(kt, 64)[:, :nkh * 128], in_=PSbf(bt, 64)[:, :nkh * 128]),
                     reads=[PSb[bt]], writes=[Fb[kt]])
                bA = bank2()
                bB = bA + 1

                def fd(e):
                    for u in range(8):
                        kh = kh_of(u)
                        ins = e.matmul(PS[:, (bA if u < 4 else bB), (u % 4) * 128:(u % 4) * 128 + 128],
                                       lhsT=Fbf(kt, 64)[:, kh * 128:(kh + 1) * 128],
                                       rhs=Fbf(14 + ch, 64)[:, u * 128:(u + 1) * 128], start=True, stop=True)
                    return ins
                S.op("pe", fd, reads=[Fb[kt], Fb[14 + ch]], writes=[PSb[bA], PSb[bB]])
                return (bA, bB)

            def stage_state(ch, b2):
                if tile["kind"] == "sample":
                    src = (st_h if hg else st_g)[j, ch].rearrange("h k v -> k h v")
                    S.op("pool", lambda e: e.dma_start(
                        out=ST[:, l, :].rearrange("p (h v) -> p h v", h=(8 if hg else 4)), in_=src),
                        writes=[STb[l]], dsem=st_io[l])
                elif tile["first"] and ch == 0:
                    S.op("pool", lambda e: e.memset(ST[:, l, :], 0.0), writes=[STb[l]])
                sbuf = 40 + 2 * (ch % 3)
                S.op("act", lambda e: acopy(e, out=HA[:, sbuf:sbuf + 2, :].rearrange("p a c -> p (a c)"),
                                             in_=ST[:, l, :]),
                     reads=[STb[l]], writes=Hb[sbuf:sbuf + 2])

                def fu(e):
                    for u in range(8):
                        kh = kh_of(u)
                        ins = e.scalar_tensor_tensor(
                            out=ST[:, l, u * 128:(u + 1) * 128], in0=ST[:, l, u * 128:(u + 1) * 128],
                            scalar=EBL[:, kh, ch:ch + 1],
                            in1=PS[:, b2[0 if u < 4 else 1], (u % 4) * 128:(u % 4) * 128 + 128],
                            op0=ALU.mult, op1=ALU.add)
                    return ins
                S.op("dve", fu, reads=[STb[l], eblB, PSb[b2[0]], PSb[b2[1]]], writes=[STb[l]])
                dst = None
                if tile["kind"] == "sample":
                    dst = (sh_s if hg else sg_s)[j, ch]
                elif tile["last"] and ch == nch - 1:
                    dst = (sh_p if hg else sg_p)[j]
                if dst is not None:
                    ob = Buf()
                    outB.append(ob)
                    S.op("pool", lambda e: e.dma_start(
                        out=dst.rearrange("h k v -> k h v"),
                        in_=ST[:, l, :].rearrange("p (h v) -> p h v", h=(8 if hg else 4))),
                        reads=[STb[l]], writes=[ob], dsem=st_io[l])
                return sbuf

            def stage_CB(ch, sbuf):
                b = bank1()
                pt = 50 + ch % 2
                sv = HA[:, sbuf:sbuf + 2, :].rearrange("p a c -> p (a c)")

                def fcb(e):
                    for u in range(8):
                        kh = kh_of(u)
                        e.matmul(PS[:, b, u * 64:(u + 1) * 64], lhsT=sv[:, u * 128:(u + 1) * 128],
                                 rhs=HA[:, kh, ch * 64:(ch + 1) * 64], start=True, stop=False)
                        ins = e.matmul(PS[:, b, u * 64:(u + 1) * 64], lhsT=Fbf(14 + ch, 64)[:, u * 128:(u + 1) * 128],
                                       rhs=HA[0:64, pt, kh * 64:(kh + 1) * 64], start=False, stop=True)
                    return ins
                S.op("pe", fcb, reads=Hb[sbuf:sbuf + 2] + Hb[0:nkh] + [Fb[14 + ch], Hb[pt]], writes=[PSb[b]])
                S.op("act", lambda e: acopy(e, out=FA[:, 0:8, ch * 64:(ch + 1) * 64],
                                             in_=PS[:, b, :].rearrange("p (u t) -> p u t", t=64)),
                     reads=[PSb[b]], writes=Fb[0:8])

            def gate_chunk(s_, ci, u):
                b = fm_chunk(s_, ci, T)
                S.op("act", lambda e: e.activation(out=Fbf(8 + u // 2)[:, (u % 2) * 512:(u % 2) * 512 + T],
                                                   in_=PSv(b, T), func=AF.Silu),
                     reads=[PSb[b]], writes=[Fb[8 + u // 2]])

            def xq_chunk(s_, ci, hh):
                b = fm_chunk(s_, ci, T)
                S.op("dve", lambda e: e.tensor_copy(out=H(36 + hh, T), in_=PSv(b, T)),
                     reads=[PSb[b]], writes=[Hb[36 + hh]])

            extras = []
            for pp in range(4):
                extras.append(("g", pp))
            for pp in range(2):
                extras.append(("xq", pp))

            def run_extra(item):
                kind, pp = item
                s_ = fetch((l, kind, pp))
                for ci in range(2):
                    if kind == "g":
                        gate_chunk(s_, ci, 2 * pp + ci)
                    else:
                        xq_chunk(s_, ci, 2 * pp + ci)

            b2s = {0: stage_A(0)}
            for ch in range(nch):
                if ch + 1 < nch:
                    b2s[ch + 1] = stage_A(ch + 1)
                if extras:
                    run_extra(extras.pop(0))
                sbuf = stage_state(ch, b2s[ch])
                stage_CB(ch, sbuf)
            while extras:
                run_extra(extras.pop(0))

            ncol = (oA_hn + j * 8) if hg else (oB_gn + j * 8)
            NC_ = CA if hg else CB
            Vd = 128 if hg else 256
            nh = 8 if hg else 4
            for hh in range(nh):
                us = [hh] if hg else [2 * hh, 2 * hh + 1]
                for u in us:
                    S.op("act", lambda e, u=u: e.activation(out=H(16 + u, T), in_=Fv(u, T), func=AF.Square),
                         reads=[Fb[u]], writes=[Hb[16 + u]])
                b = bank1()

                def fss(e, us=us, b=b):
                    for n_, u in enumerate(us):
                        ins = e.matmul(PSv(b, T), lhsT=ONES[:], rhs=H(16 + u, T), start=(n_ == 0),
                                       stop=(n_ == len(us) - 1))
                    return ins
                S.op("pe", fss, reads=[Hb[16 + u] for u in us] + [cB], writes=[PSb[b]])
                tf = 12 + hh % 2
                S.op("act", lambda e, b=b, tf=tf: e.activation(out=Fv(tf, T), in_=PSv(b, T), func=AF.Ln,
                                                               scale=1.0 / Vd, bias=CST[:, 2:3]),
                     reads=[PSb[b], cB], writes=[Fb[tf]])
                br = bank1()
                S.op("act", lambda e, br=br, tf=tf: e.activation(out=PSv(br, T), in_=Fv(tf, T), func=AF.Exp,
                                                                 scale=-0.5),
                     reads=[Fb[tf]], writes=[PSb[br]])
                for u in us:
                    S.op("dve", lambda e, u=u, br=br: e.tensor_tensor(out=Fv(u, T), in0=Fv(u, T), in1=PSv(br, T),
                                                                       op=ALU.mult),
                         reads=[Fb[u], PSb[br]], writes=[Fb[u]])
                    S.op("dve", lambda e, u=u: e.scalar_tensor_tensor(
                        out=H(24 + u, T), in0=Fv(u, T), scalar=NC_[:, ncol + u:ncol + u + 1],
                        in1=Fbf(8 + u // 2)[:, (u % 2) * 512:(u % 2) * 512 + T],
                        op0=ALU.mult, op1=ALU.mult),
                        reads=[Fb[u], Fb[8 + u // 2], cB], writes=[Hb[24 + u]])

            if tile["kind"] == "sample":
                ranges = []
                for sq in range(2):
                    fst = 8 + 2 * sq
                    S.op("pool", lambda e, sq=sq, fst=fst: e.dma_start(
                        out=Fwide(fst, 2).rearrange("p (g c) -> p g c", c=512),
                        in_=cmk[l, sq].rearrange("(g p) c -> p g c", p=128)),
                        writes=Fb[fst:fst + 2], dsem=smem_ds[sq])
                    kt0 = 56 + 4 * sq
                    ktv = HA[:, kt0:kt0 + 2, :].rearrange("p a c -> p (a c)").rearrange("p (h m) -> p h m", m=256)
                    for mg in range(2):
                        b = bank1()

                        def ftk(e, fst=fst, mg=mg, b=b):
                            for h in range(4):
                                ins = e.transpose(out=PS[:, b, h * 128:(h + 1) * 128],
                                                  in_=Fwide(fst, 2)[:, mg * 512 + h * 128:mg * 512 + (h + 1) * 128],
                                                  identity=IDF[:])
                            return ins
                        S.op("pe", ftk, reads=Fb[fst:fst + 2] + [cB], writes=[PSb[b]])
                        S.op("dve", lambda e, ktv=ktv, mg=mg, b=b: e.tensor_copy(
                            out=ktv[:, :, mg * 128:(mg + 1) * 128],
                            in_=PS[:, b, :].rearrange("p (h m) -> p h m", m=128)),
                            reads=[PSb[b]], writes=Hb[kt0:kt0 + 2])
                    fsv = 12 + 2 * sq
                    S.op("pool", lambda e, sq=sq, fsv=fsv: e.dma_start(
                        out=Fwide(fsv, 2).rearrange("p (g c) -> p g c", c=512),
                        in_=cmv[l, sq].rearrange("(g p) c -> p g c", p=128)),
                        writes=Fb[fsv:fsv + 2], dsem=smem_ds[2 + sq])
                    vv = HA[:, kt0 + 2:kt0 + 4, :].rearrange("p a c -> p (a c)").rearrange("p (g c) -> p g c", c=512)
                    S.op("dve", lambda e, vv=vv, fsv=fsv: e.tensor_copy(
                        out=vv, in_=Fwide(fsv, 2).rearrange("p (g c) -> p g c", c=512)),
                        reads=Fb[fsv:fsv + 2], writes=Hb[kt0 + 2:kt0 + 4])
                    ranges.append((sq * 64, 64, ktv, Hb[kt0:kt0 + 2], vv, Hb[kt0 + 2:kt0 + 4]))
            else:
                ranges = [(0, T, KTP[:, l, :, :], [KTPb[l]], VP[:, l, :, :], [VPb[l]])]
            it = 0
            for hh in range(4):
                for (r0, n, ktv, ktB, vv, vB) in ranges:
                    pbase = 46 + 2 * (it % 2)
                    it += 1
                    for mc in range(2):
                        b = bank1()
                        S.op("pe", lambda e, b=b, mc=mc, ktv=ktv, hh=hh, r0=r0, n=n: e.matmul(
                            PS[:, b, :n], lhsT=ktv[:, hh, mc * 128:(mc + 1) * 128], rhs=HA[:, 36 + hh, r0:r0 + n],
                            start=True, stop=True), reads=ktB + [Hb[36 + hh]], writes=[PSb[b]])
                        S.op("act", lambda e, b=b, mc=mc, pbase=pbase, n=n: e.activation(
                            out=HA[:, pbase + mc, :n], in_=PS[:, b, :n], func=AF.Exp, scale=128.0 ** -0.5),
                            reads=[PSb[b]], writes=[Hb[pbase + mc]])
                    bd = bank1()

                    def fden(e, bd=bd, pbase=pbase, n=n):
                        e.matmul(PS[:, bd, :n], lhsT=ONES[:], rhs=HA[:, pbase, :n], start=True, stop=False)
                        return e.matmul(PS[:, bd, :n], lhsT=ONES[:], rhs=HA[:, pbase + 1, :n], start=False, stop=True)
                    S.op("pe", fden, reads=[Hb[pbase], Hb[pbase + 1], cB], writes=[PSb[bd]])
                    bp = bank1()

                    def fpv(e, bp=bp, pbase=pbase, n=n, vv=vv, hh=hh):
                        e.matmul(PS[:, bp, :n], lhsT=vv[:, 0, hh * 128:(hh + 1) * 128], rhs=HA[:, pbase, :n],
                                 start=True, stop=False)
                        return e.matmul(PS[:, bp, :n], lhsT=vv[:, 1, hh * 128:(hh + 1) * 128],
                                        rhs=HA[:, pbase + 1, :n], start=False, stop=True)
                    S.op("pe", fpv, reads=[Hb[pbase], Hb[pbase + 1]] + vB, writes=[PSb[bp]])
                    tf = 10 + it % 2
                    S.op("act", lambda e, bd=bd, tf=tf, n=n: e.activation(out=Fv(tf, n), in_=PS[:, bd, :n],
                                                                          func=AF.Ln),
                         reads=[PSb[bd]], writes=[Fb[tf]])
                    S.op("act", lambda e, tf=tf, n=n: e.activation(out=Fv(tf, n), in_=Fv(tf, n), func=AF.Exp,
                                                                   scale=-1.0),
                         reads=[Fb[tf]], writes=[Fb[tf]])
                    S.op("dve", lambda e, bp=bp, tf=tf, n=n, hh=hh, r0=r0: e.tensor_tensor(
                        out=HA[:, 32 + hh, r0:r0 + n], in0=PS[:, bp, :n], in1=Fv(tf, n), op=ALU.mult),
                        reads=[PSb[bp], Fb[tf]], writes=[Hb[32 + hh]])

            for jo in range(8):
                s = fetch((l, "out", jo))
                b = bank1()

                def fo(e, s=s, b=b):
                    for kc in range(12):
                        ins = e.matmul(PSv(b, T), lhsT=RG[:, s, kc * 128:(kc + 1) * 128], rhs=H(24 + kc, T),
                                       start=(kc == 0), stop=(kc == 11))
                    return ins
                S.op("pe", fo, reads=[RGb[s]] + Hb[24:36], writes=[PSb[b]])
                S.op("dve", lambda e, jo=jo, b=b: e.scalar_tensor_tensor(
                    out=X[:, jo, :T], in0=X[:, jo, :T], scalar=ALPHA, in1=PSv(b, T), op0=ALU.mult, op1=ALU.add),
                    reads=[Xb[jo], PSb[b]], writes=[Xb[jo]])

        def phase1b(l, tile, kh, bq_, bk_, sg, qscale, stages="ABCD"):
            T = tile["T"]
            nch = T // CH
            j = l // 2
            hg = (l % 2 == 0)
            hi = kh % 6
            st = 8 + 3 * (kh % 2)
            Bt, E1, E2 = st, st + 1, st + 2
            B3 = Fv(Bt, T).rearrange("p (c t) -> p c t", t=64)
            if "A" in stages:
                if hg:
                    S.op("act", lambda e: e.activation(out=Fv(Bt, T), in_=Fv(hi, T), func=AF.Ln, scale=-1.0,
                                                       bias=CST[:, 1:2]),
                         reads=[Fb[hi], cB], writes=[Fb[Bt]])
                else:
                    bz = bank1()
                    S.op("pe", lambda e: e.matmul(PSv(bz, T), lhsT=WG2[:, j, kh * 128:(kh + 1) * 128],
                                                  rhs=GAT[:, :T], start=True, stop=True),
                         reads=[cB, gatB], writes=[PSb[bz]])
                    S.op("act", lambda e: e.activation(out=Fv(Bt, T), in_=PSv(bz, T), func=AF.Exp, scale=-1.0,
                                                       bias=NBG[:, j * 4 + kh:j * 4 + kh + 1]),
                         reads=[PSb[bz], cB], writes=[Fb[Bt]])
                    S.op("act", lambda e: e.activation(out=Fv(Bt, T), in_=Fv(Bt, T), func=AF.Ln, bias=CST[:, 1:2]),
                         reads=[Fb[Bt], cB], writes=[Fb[Bt]])
            if "B" in stages:
                S.op("dve", lambda e: e.tensor_tensor_scan(out=Fv(Bt, T), data0=RMASK[:, :T], data1=Fv(Bt, T),
                                                           initial=0.0, op0=ALU.mult, op1=ALU.add),
                     reads=[Fb[Bt], cB], writes=[Fb[Bt]])
                if hg:
                    S.op("dve", lambda e: e.tensor_scalar(out=Fv(Bt, T), in0=Fv(Bt, T), scalar1=-80.0,
                                                          scalar2=None, op0=ALU.max),
                         reads=[Fb[Bt]], writes=[Fb[Bt]])
            if "C" in stages:
                S.op("act", lambda e: e.activation(out=Fv(E1, T), in_=Fv(Bt, T), func=AF.Exp, scale=sg),
                     reads=[Fb[Bt]], writes=[Fb[E1]])
                S.op("act", lambda e: e.activation(out=Fv(E2, T), in_=Fv(Bt, T), func=AF.Exp, scale=-sg),
                     reads=[Fb[Bt]], writes=[Fb[E2]])
                S.op("act", lambda e: e.activation(out=EBL[:, kh, 0:nch], in_=B3[:, :, 63], func=AF.Exp, scale=sg),
                     reads=[Fb[Bt]], writes=[eblB])
            if "D" in stages:
                if hg:
                    S.op("dve", lambda e: e.scalar_tensor_tensor(out=H(kh, T), in0=H(52 + hi, T), scalar=qscale,
                                                                 in1=Fv(E1, T), op0=ALU.mult, op1=ALU.mult),
                         reads=[Hb[52 + hi], Fb[E1]], writes=[Hb[kh]])
                    S.op("dve", lambda e: e.tensor_tensor(out=H(8 + kh, T), in0=Fv(hi, T), in1=Fv(E2, T),
                                                          op=ALU.mult),
                         reads=[Fb[hi], Fb[E2]], writes=[Hb[8 + kh]])
                else:
                    S.op("dve", lambda e: e.scalar_tensor_tensor(out=H(kh, T), in0=PSv(bq_, T), scalar=qscale,
                                                                 in1=Fv(E1, T), op0=ALU.mult, op1=ALU.mult),
                         reads=[PSb[bq_], Fb[E1]], writes=[Hb[kh]])
                    S.op("dve", lambda e: e.tensor_tensor(out=H(8 + kh, T), in0=PSv(bk_, T), in1=Fv(E2, T),
                                                          op=ALU.mult),
                         reads=[PSb[bk_], Fb[E2]], writes=[Hb[8 + kh]])
                S.op("dve", lambda e: e.tensor_tensor(
                    out=H(16 + kh, T).rearrange("p (c t) -> p c t", t=64),
                    in0=H(8 + kh, T).rearrange("p (c t) -> p c t", t=64),
                    in1=EBL[:, kh, 0:nch].unsqueeze(2).to_broadcast([128, nch, 64]), op=ALU.mult),
                    reads=[Hb[8 + kh], eblB], writes=[Hb[16 + kh]])

        tiles = [dict(kind="sample", T=128, tok0=0, first=False, last=False)]
        npt = seq // 512
        for t in range(npt):
            tiles.append(dict(kind="prompt", T=512, tok0=t * 512, first=(t == 0), last=(t == npt - 1)))
        for tile in (tiles if STAGE >= 3 else []):
            T = tile["T"]
            load_x(x_s if tile["kind"] == "sample" else x_p, tile["tok0"], T)
            dfr = []
            for l in range(depth):
                ffn(l, 0, T, dfr)
                dfr = layer_norm(l, 0, T)
                mixer(l, tile, dfr)
                dfr = layer_norm(l, 1, T)
                ffn(l, 1, T, dfr)
                dfr = layer_norm(l, 2, T)
            for f_ in dfr:
                f_()
            store_y(y_s if tile["kind"] == "sample" else y_p, tile["tok0"], T)

        S.op("pool", lambda e: e.memset(TINY[:, 0:1], 0.0), reads=outB, writes=[tinyB])

        with nc.Block() as block:
            S.emit(block)
    return nc


def run(inputs, depth, seq, ncores):
    nc = build(depth, seq)
    NHL = (depth + 1) // 2
    NGL = depth // 2
    f = lambda a: np.ascontiguousarray(np.asarray(a, dtype=np.float32))
    shared = {}
    for k in ["ffn_w_gate", "ffn_w_up", "ffn_w_down", "ln_gain", "ln_bias", "hgrn_w_in", "hgrn_lb_logits",
              "hgrn_norm", "hgrn_w_out", "gla_w_in", "gla_w_gate2", "gla_b_gate", "gla_norm", "gla_w_out",
              "mem_w_k", "mem_w_v"]:
        shared[k] = f(inputs[k])
    in_maps = []
    for c in range(ncores):
        m = dict(shared)
        m["x_p"] = f(inputs["x_prompt"][c, :seq])
        m["x_s"] = f(np.asarray(inputs["x_sample"])[2 * c:2 * c + 2].reshape(128, D))
        m["mem_p"] = f(inputs["mem_prompt"][c])
        m["cmk"] = f(np.asarray(inputs["cache_mem_k"])[:, 2 * c:2 * c + 2].reshape(depth, 2, MEMT, 512))
        m["cmv"] = f(np.asarray(inputs["cache_mem_v"])[:, 2 * c:2 * c + 2].reshape(depth, 2, MEMT, 512))
        m["st_h"] = f(np.asarray(inputs["state_hgrn"])[:, 2 * c:2 * c + 2])
        m["st_g"] = f(np.asarray(inputs["state_gla"])[:, 2 * c:2 * c + 2])
        in_maps.append(m)
    res = run_bass_kernel_spmd(nc, in_maps, core_ids=list(range(ncores)))
    R = res.results
    y_prompt = np.stack([R[c]["y_p"] for c in range(ncores)], 0)
    y_sample = np.concatenate([R[c]["y_s"].reshape(2, 64, D) for c in range(ncores)], 0)
    sh_p = np.stack([R[c]["sh_p"] for c in range(ncores)], 1)
    sg_p = np.stack([R[c]["sg_p"] for c in range(ncores)], 1)[:NGL]
    mk = np.stack([R[c]["mk_p"] for c in range(ncores)], 1).reshape(depth, ncores, MEMT, 4, 128)
    mv = np.stack([R[c]["mv_p"] for c in range(ncores)], 1).reshape(depth, ncores, MEMT, 4, 128)
    sh_s = np.concatenate([R[c]["sh_s"] for c in range(ncores)], 1)
    sg_s = np.concatenate([R[c]["sg_s"] for c in range(ncores)], 1)[:NGL]
    return tuple(np.ascontiguousarray(a, dtype=np.float32) for a in
                 (y_prompt, y_sample, sh_p, sg_p, mk, mv, sh_s, sg_s))


def kernel(**inputs):
    return run(inputs, 4, 8192, 8)
```
